# Optimizing a Trainium2 kernel written in Bass

```python
import math
import jax, jax.numpy as jnp
from jax import lax
import numpy as np


D_MODEL = 1024
BATCH = 8
SEQ = 4096
DEPTH = 2

CTX_LEN = 256
GRID_W = 64
N_EVEN = (DEPTH + 1) // 2
N_ODD = DEPTH // 2
EPS = 1e-6

D_FOURIER = D_MODEL // 2
N_FOURIER_GROUPS = 4
FOURIER_GROUP_DIM = D_FOURIER // N_FOURIER_GROUPS

D_HYENA = D_MODEL // 2
HYENA_EMB_DIM = 33
HYENA_BANDS = (HYENA_EMB_DIM - 1) // 2
HYENA_FILTER_WIDTH = 64
HYENA_TARGET = 1e-2
HYENA_FAST_DECAY_PCT = 0.3
HYENA_SLOW_DECAY_PCT = 1.5
HYENA_MIN_DECAY = math.log(HYENA_TARGET) / HYENA_FAST_DECAY_PCT
HYENA_MAX_DECAY = math.log(HYENA_TARGET) / HYENA_SLOW_DECAY_PCT

MLA_HEADS = 16
Q_LORA_RANK = 256
KV_LORA_RANK = 128
QK_NOPE_DIM = 64
QK_ROPE_DIM = 32
V_HEAD_DIM = 64
QK_HEAD_DIM = QK_NOPE_DIM + QK_ROPE_DIM
ROPE_THETA = 10000.0
Q_BLOCK = 128

D_FF = 2816

kernel_name = 'hybrid_fourier_hyena_mla_dit_block'


def _f32(a):
    return a.astype(jnp.float32)


def _rms_norm(x, g):
    x32 = _f32(x)
    y = x32 * lax.rsqrt(jnp.mean(x32 * x32, axis=-1, keepdims=True) + EPS)
    return (y * _f32(g)).astype(x.dtype)


def _dwconv3(x, w, b):
    xp = jnp.pad(x, ((0, 0), (1, 1), (0, 0)))
    return xp[:, :-2] * w[0] + xp[:, 1:-1] * w[1] + xp[:, 2:] * w[2] + b


def _adaln(cond, w_mod, b_mod):
    mod = jax.nn.silu(cond) @ w_mod + b_mod
    return jnp.split(mod[:, None, :], 6, axis=-1)


def _modulate(h, shift, scale):
    return h * (1 + scale) + shift


def _hyena_filter_spectrum(L, w1, b1, w2, b2, w3, b3, w4, freq):
    pos = jnp.arange(L, dtype=jnp.float32)
    t = pos / max(L - 1, 1)
    bands = jnp.linspace(1e-4, HYENA_BANDS - 1, HYENA_BANDS, dtype=jnp.float32)
    ang = (2.0 * math.pi / L) * pos[:, None] * bands[None, :]
    z = jnp.concatenate([t[:, None], jnp.cos(ang), -jnp.sin(ang)], axis=-1)
    f = _f32(freq)
    h = jnp.sin(f * (z @ _f32(w1) + _f32(b1)))
    h = jnp.sin(f * (h @ _f32(w2) + _f32(b2)))
    h = jnp.sin(f * (h @ _f32(w3) + _f32(b3)))
    h = h @ _f32(w4)
    deltas = jnp.linspace(HYENA_MIN_DECAY, HYENA_MAX_DECAY, D_HYENA, dtype=jnp.float32)
    decay = jnp.exp(-t[:, None] * jnp.abs(deltas)[None, :])
    h_fwd = h[:, :D_HYENA] * decay
    h_bwd = h[:, D_HYENA:] * decay
    two_sided = jnp.concatenate([h_fwd, jnp.zeros((1, D_HYENA), jnp.float32), h_bwd[:0:-1]], axis=0)
    two_sided = two_sided / jnp.sum(jnp.abs(two_sided), axis=0, keepdims=True)
    return jnp.fft.rfft(two_sided, axis=0)


def _long_conv(u, spectrum, bias):
    L = u.shape[1]
    u32 = _f32(u)
    y = jnp.fft.irfft(jnp.fft.rfft(u32, n=2 * L, axis=1) * spectrum[None], n=2 * L, axis=1)[:, :L]
    return (y + u32 * _f32(bias)).astype(u.dtype)


def _fourier_hyena_mixer(h, w_in, w_out, conv_w, conv_b, w1, b1, w2, b2, w3, b3, w4, freq, hy_bias):
    B, L, _ = h.shape
    proj = h @ w_in
    u_f = _f32(proj[..., :D_FOURIER]).reshape(B, L, N_FOURIER_GROUPS, FOURIER_GROUP_DIM)
    y_f = jnp.fft.fft2(u_f, axes=(1, 3), norm='ortho').real.reshape(B, L, D_FOURIER).astype(h.dtype)
    u_h = _dwconv3(proj[..., D_FOURIER:], conv_w, conv_b)
    x0, x1, v = jnp.split(u_h, 3, axis=-1)
    spectrum = _hyena_filter_spectrum(L, w1, b1, w2, b2, w3, b3, w4, freq)
    y_h = x0 * _long_conv(v * x1, spectrum, hy_bias)
    return jnp.concatenate([y_f, y_h], axis=-1) @ w_out


def _rope_2d(x, rope):
    cos_r, sin_r, cos_c, sin_c = rope
    q4 = QK_ROPE_DIM // 4
    half = QK_ROPE_DIM // 2

    def rot(xa, cos, sin):
        x1, x2 = xa[..., :q4], xa[..., q4:]
        cos = cos[None, :, None, :]
        sin = sin[None, :, None, :]
        return jnp.concatenate([x1 * cos - x2 * sin, x1 * sin + x2 * cos], axis=-1)

    out = jnp.concatenate([rot(x[..., :half], cos_r, sin_r), rot(x[..., half:], cos_c, sin_c)], axis=-1)
    return out.astype(x.dtype)


def _mla_queries(a_q, q_a_norm, w_uq, q_norm, rope):
    B, L, _ = a_q.shape
    q = (_rms_norm(a_q, q_a_norm) @ w_uq).reshape(B, L, MLA_HEADS, QK_HEAD_DIM)
    q = _rms_norm(q, q_norm)
    if rope is not None:
        q = jnp.concatenate([q[..., :QK_NOPE_DIM], _rope_2d(q[..., QK_NOPE_DIM:], rope)], axis=-1)
    return q


def _mla_keys_values(a_kv, kv_a_norm, w_ukv, k_norm, rope):
    B, L, _ = a_kv.shape
    ckv = _rms_norm(a_kv[..., :KV_LORA_RANK], kv_a_norm)
    k_pe = a_kv[..., KV_LORA_RANK:]
    kv = (ckv @ w_ukv).reshape(B, L, MLA_HEADS, QK_NOPE_DIM + V_HEAD_DIM)
    k_nope, v = kv[..., :QK_NOPE_DIM], kv[..., QK_NOPE_DIM:]
    k_pe = jnp.broadcast_to(k_pe[:, :, None, :], (B, L, MLA_HEADS, QK_ROPE_DIM))
    k = _rms_norm(jnp.concatenate([k_nope, k_pe], axis=-1), k_norm)
    if rope is not None:
        k = jnp.concatenate([k[..., :QK_NOPE_DIM], _rope_2d(k[..., QK_NOPE_DIM:], rope)], axis=-1)
    return k, v


def _attend_blocks(q, k, v):
    B, S, H, Dq = q.shape
    nb = S // Q_BLOCK
    qb = q.reshape(B, nb, Q_BLOCK, H, Dq).transpose(1, 0, 2, 3, 4)
    scale = QK_HEAD_DIM ** -0.5

    def one(qblk):
        s = jnp.einsum('bqhd,bkhd->bhqk', qblk, k, preferred_element_type=jnp.float32) * scale
        p = jax.nn.softmax(s, axis=-1).astype(v.dtype)
        return jnp.einsum('bhqk,bkhv->bqhv', p, v)

    o = lax.map(one, qb)
    return o.transpose(1, 0, 2, 3, 4).reshape(B, S, H * V_HEAD_DIM)


def _conv_ffn(h, w_up, conv_w, conv_b, w_down):
    gate, val = jnp.split(h @ w_up, 2, axis=-1)
    return (jax.nn.silu(_dwconv3(gate, conv_w, conv_b)) * val) @ w_down


def setup_inputs(seed: int = 0) -> dict:
    key = jax.random.key(seed)
    ks = iter(jax.random.split(key, 40))

    def nrm(shape, scale):
        return scale * jax.random.normal(next(ks), shape, jnp.float32)

    def gain(shape):
        return 1.0 + 0.02 * jax.random.normal(next(ks), shape, jnp.float32)

    D = D_MODEL
    return {
        'x': nrm((BATCH, SEQ, D), 1.0),
        'c': nrm((BATCH, D), 1.0),
        'ctx': nrm((BATCH, CTX_LEN, D), 1.0),
        'c_ctx': nrm((D,), 1.0),
        'norm1': gain((DEPTH, D)),
        'norm2': gain((DEPTH, D)),
        'w_mod': nrm((DEPTH, D, 6 * D), 0.5 * D ** -0.5),
        'b_mod': nrm((DEPTH, 6 * D), 0.01),
        'ffn_w_up': nrm((DEPTH, D, 2 * D_FF), D ** -0.5),
        'ffn_conv_w': nrm((DEPTH, 3, D_FF), 3 ** -0.5),
        'ffn_conv_b': nrm((DEPTH, D_FF), 0.01),
        'ffn_w_down': nrm((DEPTH, D_FF, D), D_FF ** -0.5),
        'fh_w_in': nrm((N_EVEN, D, D_FOURIER + 3 * D_HYENA), D ** -0.5),
        'fh_w_out': nrm((N_EVEN, D_FOURIER + D_HYENA, D), (D_FOURIER + D_HYENA) ** -0.5),
        'hy_conv_w': nrm((N_EVEN, 3, 3 * D_HYENA), 3 ** -0.5),
        'hy_conv_b': nrm((N_EVEN, 3 * D_HYENA), 0.01),
        'hy_filt_w1': nrm((N_EVEN, HYENA_EMB_DIM, HYENA_FILTER_WIDTH), HYENA_EMB_DIM ** -0.5),
        'hy_filt_b1': nrm((N_EVEN, HYENA_FILTER_WIDTH), 0.1),
        'hy_filt_w2': nrm((N_EVEN, HYENA_FILTER_WIDTH, HYENA_FILTER_WIDTH), HYENA_FILTER_WIDTH ** -0.5),
        'hy_filt_b2': nrm((N_EVEN, HYENA_FILTER_WIDTH), 0.1),
        'hy_filt_w3': nrm((N_EVEN, HYENA_FILTER_WIDTH, HYENA_FILTER_WIDTH), HYENA_FILTER_WIDTH ** -0.5),
        'hy_filt_b3': nrm((N_EVEN, HYENA_FILTER_WIDTH), 0.1),
        'hy_filt_w4': nrm((N_EVEN, HYENA_FILTER_WIDTH, 2 * D_HYENA), HYENA_FILTER_WIDTH ** -0.5),
        'hy_freq': gain((N_EVEN, HYENA_FILTER_WIDTH)),
        'hy_bias': nrm((N_EVEN, D_HYENA), 0.5),
        'mla_w_in': nrm((N_ODD, D, Q_LORA_RANK + KV_LORA_RANK + QK_ROPE_DIM), D ** -0.5),
        'mla_q_a_norm': gain((N_ODD, Q_LORA_RANK)),
        'mla_w_uq': nrm((N_ODD, Q_LORA_RANK, MLA_HEADS * QK_HEAD_DIM), Q_LORA_RANK ** -0.5),
        'mla_kv_a_norm': gain((N_ODD, KV_LORA_RANK)),
        'mla_w_ukv': nrm((N_ODD, KV_LORA_RANK, MLA_HEADS * (QK_NOPE_DIM + V_HEAD_DIM)), KV_LORA_RANK ** -0.5),
        'mla_q_norm': gain((N_ODD, QK_HEAD_DIM)),
        'mla_k_norm': gain((N_ODD, QK_HEAD_DIM)),
        'mla_w_o': nrm((N_ODD, MLA_HEADS * V_HEAD_DIM, D), (MLA_HEADS * V_HEAD_DIM) ** -0.5),
    }


def reference(x, c, ctx, c_ctx, norm1, norm2, w_mod, b_mod, ffn_w_up, ffn_conv_w, ffn_conv_b, ffn_w_down,
              fh_w_in, fh_w_out, hy_conv_w, hy_conv_b, hy_filt_w1, hy_filt_b1, hy_filt_w2, hy_filt_b2,
              hy_filt_w3, hy_filt_b3, hy_filt_w4, hy_freq, hy_bias, mla_w_in, mla_q_a_norm, mla_w_uq,
              mla_kv_a_norm, mla_w_ukv, mla_q_norm, mla_k_norm, mla_w_o):
    L = x.shape[1]
    ROWS = L // GRID_W
    rows = jnp.broadcast_to(jnp.arange(ROWS, dtype=jnp.float32)[:, None], (ROWS, GRID_W)).reshape(-1)
    cols = jnp.broadcast_to(jnp.arange(GRID_W, dtype=jnp.float32)[None, :], (ROWS, GRID_W)).reshape(-1)
    n_freq = QK_ROPE_DIM // 4
    inv_freq = ROPE_THETA ** (-jnp.arange(n_freq, dtype=jnp.float32) / n_freq)
    ang_r = rows[:, None] * inv_freq[None, :]
    ang_c = cols[:, None] * inv_freq[None, :]
    rope = (jnp.cos(ang_r), jnp.sin(ang_r), jnp.cos(ang_c), jnp.sin(ang_c))

    for i in range(DEPTH):
        last = i == DEPTH - 1
        odd = i % 2 == 1
        j = i // 2
        sh1, sc1, g1, sh2, sc2, g2 = _adaln(c, w_mod[i], b_mod[i])
        hx = _modulate(_rms_norm(x, norm1[i]), sh1, sc1)
        if (not last) or odd:
            csh1, csc1, cg1, csh2, csc2, cg2 = _adaln(c_ctx[None, :], w_mod[i], b_mod[i])
            hc = _modulate(_rms_norm(ctx, norm1[i]), csh1, csc1)
        if not odd:
            fh = (fh_w_in[j], fh_w_out[j], hy_conv_w[j], hy_conv_b[j], hy_filt_w1[j], hy_filt_b1[j],
                  hy_filt_w2[j], hy_filt_b2[j], hy_filt_w3[j], hy_filt_b3[j], hy_filt_w4[j], hy_freq[j], hy_bias[j])
            yx = _fourier_hyena_mixer(hx, *fh)
            if not last:
                yc = _fourier_hyena_mixer(hc, *fh)
        else:
            w_in = mla_w_in[j]
            q_args = (mla_q_a_norm[j], mla_w_uq[j], mla_q_norm[j])
            kv_args = (mla_kv_a_norm[j], mla_w_ukv[j], mla_k_norm[j])
            a_x = hx @ w_in
            q_x = _mla_queries(a_x[..., :Q_LORA_RANK], *q_args, rope)
            k_x, v_x = _mla_keys_values(a_x[..., Q_LORA_RANK:], *kv_args, rope)
            a_c = hc @ (w_in if not last else w_in[:, Q_LORA_RANK:])
            k_c, v_c = _mla_keys_values(a_c[..., -(KV_LORA_RANK + QK_ROPE_DIM):], *kv_args, None)
            k_all = jnp.concatenate([k_x, k_c], axis=1)
            v_all = jnp.concatenate([v_x, v_c], axis=1)
            yx = _attend_blocks(q_x, k_all, v_all) @ mla_w_o[j]
            if not last:
                q_c = _mla_queries(a_c[..., :Q_LORA_RANK], *q_args, None)
                yc = _attend_blocks(q_c, k_c, v_c) @ mla_w_o[j]
        ffn = (ffn_w_up[i], ffn_conv_w[i], ffn_conv_b[i], ffn_w_down[i])
        x = x + g1 * yx
        x = x + g2 * _conv_ffn(_modulate(_rms_norm(x, norm2[i]), sh2, sc2), *ffn)
        if not last:
            ctx = ctx + cg1 * yc
            ctx = ctx + cg2 * _conv_ffn(_modulate(_rms_norm(ctx, norm2[i]), csh2, csc2), *ffn)
    return x
```

```python
import numpy as np
import ml_dtypes
from contextlib import ExitStack
import concourse.bass as bass
import concourse.mybir as mybir
from concourse.bass_utils import run_bass_kernel_spmd

F32 = mybir.dt.float32
BF16 = mybir.dt.bfloat16
I32 = mybir.dt.int32
AF = mybir.ActivationFunctionType
ALU = mybir.AluOpType
AX = mybir.AxisListType

import os as _os
SAME_ENGINE_SYNC = _os.environ.get("SES", "1") == "1"
N_DMA_SEM = 8
NCORES = 8
LX = 4096
LCX = 256
D = 1024
DFF = 2816
EPS = 1e-6
PI = float(np.pi)


class Trk:
    __slots__ = ("w", "r")

    def __init__(self):
        self.w = None
        self.r = {}


def trks(n):
    return [Trk() for _ in range(n)]


class DTrk(Trk):
    __slots__ = ()
    nowaw = True


class KB:
    def __init__(self, nc, es):
        self.nc = nc
        self.engs = {"pe": nc.tensor, "act": nc.scalar, "dve": nc.vector,
                     "pool": nc.gpsimd, "sp": nc.sync}
        self.sem = {}
        self.cnt = {}
        self.seen = {e: {} for e in self.engs}
        self.semobj = {}
        for e in ("pe", "act", "dve", "pool"):
            s = es.enter_context(nc.semaphore("s_" + e))
            self.sem[e] = s
            self.cnt[e] = 0
            self.semobj[id(s)] = s
        self.dsem = {}
        self.dcnt = {}
        for q in ("sp", "pool"):
            self.dsem[q] = []
            for i in range(N_DMA_SEM):
                s = es.enter_context(nc.semaphore("d_%s_%d" % (q, i)))
                self.dsem[q].append(s)
                self.semobj[id(s)] = s
            self.dcnt[q] = 0
        self.dlast = {}

    def _wait(self, e, s, v):
        if self.seen[e].get(id(s), 0) >= v:
            return
        self.engs[e].wait_ge(s, v)
        self.seen[e][id(s)] = v

    def deps(self, e, reads, writes):
        need = {}
        for t in reads:
            if t.w is not None:
                s, v = t.w
                need[id(s)] = max(need.get(id(s), 0), v)
        for t in writes:
            if t.w is not None and not getattr(t, "nowaw", False):
                s, v = t.w
                need[id(s)] = max(need.get(id(s), 0), v)
            for sid, v in t.r.items():
                need[sid] = max(need.get(sid, 0), v)
        own = id(self.sem[e]) if e in self.sem else None
        for sid, v in need.items():
            if sid == own and (e == "pe" or not SAME_ENGINE_SYNC):
                continue
            self._wait(e, self.semobj[sid], v)

    def done(self, ev, reads, writes):
        s, v = ev
        for t in reads:
            t.r[id(s)] = max(t.r.get(id(s), 0), v)
        for t in writes:
            t.w = ev
            t.r = {}

    def op(self, e, fn, reads=(), writes=(), inc=True):
        self.deps(e, reads, writes)
        ins = fn(self.engs[e])
        if inc:
            self.cnt[e] += 1
            ins.then_inc(self.sem[e], 1)
            ev = (self.sem[e], self.cnt[e])
        else:
            ev = (self.sem[e], self.cnt[e] + 1)
        self.done(ev, reads, writes)
        return ins

    def dma(self, q, out, in_, reads=(), writes=(), **kw):
        n = self.dcnt[q]
        s = self.dsem[q][n % N_DMA_SEM]
        k = n // N_DMA_SEM
        if k > 0:
            self._wait(q, s, 16 * k)
        self.deps(q, reads, writes)
        self.engs[q].dma_start(out=out, in_=in_, **kw).then_inc(s, 16)
        self.dcnt[q] = n + 1
        ev = (s, 16 * (k + 1))
        self.dlast[id(s)] = 16 * (k + 1)
        self.done(ev, reads, writes)

    def barrier(self, engines=("pe", "act", "dve", "pool", "sp")):
        for e in engines:
            for e2 in ("pe", "act", "dve", "pool"):
                if self.cnt[e2] > 0 and e != e2:
                    self._wait(e, self.sem[e2], self.cnt[e2])
            for sid, v in self.dlast.items():
                self._wait(e, self.semobj[sid], v)

    def final_wait(self, e="sp"):
        for sid, v in self.dlast.items():
            self._wait(e, self.semobj[sid], v)


class G:
    pass


class StopBuild(Exception):
    pass


def cut(g, n):
    if getattr(g, "cutpt", 0) == n:
        raise StopBuild()


def SB(g, es, name, shape, dt):
    return es.enter_context(g.nc.sbuf_tensor("sb_" + name, shape, dt))


def phase0(g):
    K, nc = g.K, g.nc
    es = g.es
    g.ident = SB(g, es, "ident", [128, 128], F32)
    g.T_ident = Trk()
    K.dma("sp", g.ident[:], g.d["ident"], writes=[g.T_ident])
    g.normT = SB(g, es, "normT", [128, 2, 2, 8], F32)
    g.T_const = Trk()
    K.dma("sp", g.normT[:], g.d["normT"], writes=[g.T_const])
    g.fconv = SB(g, es, "fconv", [128, 2, 22, 4], F32)
    K.dma("sp", g.fconv[:], g.d["fconv"], writes=[g.T_const])
    g.modT = SB(g, es, "modT", [128, 2, 48, 2], F32)
    g.T_modT = Trk()
    g.AB = SB(g, es, "ABmod", [128, 2, 2, 2, 8], F32)
    g.T_AB = Trk()
    g.T_modscr = trks(2)
    g.epsc = SB(g, es, "epsc", [128, 1], F32)
    K.op("pool", lambda e: e.memset(g.epsc[:], EPS))

    g.T_wbf = Trk()
    for (dst, src) in g.castlist:
        K.dma("pool", dst, src, writes=[g.T_wbf])


def phase0_mods(g):
    K, nc = g.K, g.nc
    with ExitStack() as ph:
        cct = SB(g, ph, "cct", [128, 8, 2], F32)
        s = SB(g, ph, "silu_c", [128, 8, 2], F32)
        T_cc, T_s = Trk(), Trk()
        K.dma("sp", cct[:], g.d["cc"], writes=[T_cc])
        K.op("act", lambda e: e.activation(out=s[:], in_=cct[:], func=AF.Silu), reads=[T_cc], writes=[T_s])
        wmb = [SB(g, ph, "wmb%d" % i, [128, 8, 512], F32) for i in range(2)]
        T_wmb = trks(2)
        modrow = SB(g, ph, "modrow", [2, 6144], F32)
        bmt = [SB(g, ph, "bmt%d" % i, [2, 512], F32) for i in range(2)]
        T_modrow = Trk()
        T_bmt = trks(2)
        jj = 0
        for i in range(2):
            wsrc = g.d["w_mod"][i].rearrange("(kc p) n -> p kc n", p=128)
            for j in range(12):
                b = jj % 2
                pbk = 5 + b
                jj += 1
                K.dma("sp", wmb[b][:], wsrc[:, :, j * 512:(j + 1) * 512], writes=[T_wmb[b]])
                K.dma("sp", bmt[b][:], g.d["b_mod"][i][:, j * 512:(j + 1) * 512].broadcast_to([2, 512]), writes=[T_bmt[b]])
                for kc in range(8):
                    K.op("pe", lambda e: e.matmul(g.ps[pbk][0:2, :], lhsT=s[:, kc, :], rhs=wmb[b][:, kc, :],
                                                  start=(kc == 0), stop=(kc == 7)),
                         reads=[T_s, T_wmb[b]], writes=[g.pst[pbk]], inc=(kc == 7))
                K.op("dve", lambda e: e.tensor_tensor(out=modrow[:, j * 512:(j + 1) * 512], in0=g.ps[pbk][0:2, :],
                                                      in1=bmt[b][:], op=ALU.add),
                     reads=[g.pst[pbk], T_bmt[b]], writes=[T_modrow])
                yield
            K.dma("sp", g.d["modscr"][i], modrow[:], reads=[T_modrow], writes=[g.T_modscr[i]])
            for c in range(48):
                K.op("pe", lambda e: e.transpose(g.ps[7][:, 2 * c:2 * c + 2], modrow[:, c * 128:(c + 1) * 128],
                                                 g.ident[0:2, 0:2]),
                     reads=[T_modrow, g.T_ident], writes=[g.pst[7]], inc=(c == 47))
            K.op("dve", lambda e: e.tensor_copy(out=g.modT[:, i, :, :], in_=g.ps[7][:, 0:96].rearrange("p (c r) -> p c r", r=2)),
                 reads=[g.pst[7]], writes=[g.T_modT])
            for r in range(2):
                for w in range(2):
                    sc_chunk = 8 if w == 0 else 32
                    K.op("dve", lambda e: e.scalar_tensor_tensor(out=g.AB[:, i, r, w, :], in0=g.modT[:, i, sc_chunk:sc_chunk + 8, r],
                                                                 scalar=1.0, in1=g.normT[:, i, w, :],
                                                                 op0=ALU.add, op1=ALU.mult),
                         reads=[g.T_modT, g.T_const], writes=[g.T_AB])
            yield


def mod_scalars(g, layer, stream, which):
    sh_chunk = 0 if which == 0 else 24

    def A(kc):
        return g.AB[:, layer, stream, which, kc:kc + 1]

    def B(kc):
        return g.modT[:, layer, sh_chunk + kc, stream:stream + 1]
    return A, B


def load_gate_bc(g, es, name, layer, stream, which):
    K = g.K
    t = SB(g, es, name, [128, 1024], F32)
    T = Trk()
    c0 = 2048 if which == 0 else 5120
    K.dma("sp", t[:], g.d["modscr"][layer][stream:stream + 1, c0:c0 + 1024].broadcast_to([128, 1024]),
          reads=[g.T_modscr[layer]], writes=[T])
    return t, T


class NormBufs:
    def __init__(self, g, es, tag, nsub):
        self.nsub = nsub
        self.xs = SB(g, es, tag + "_xs", [128, nsub, 1024], F32)
        self.T_xs = Trk()
        self.junk = SB(g, es, tag + "_junk", [128, 1024], BF16)
        self.T_junk = Trk()
        self.ss = SB(g, es, tag + "_ss", [128, nsub], F32)
        self.rs = SB(g, es, tag + "_rs", [128, nsub], F32)
        self.T_ss = Trk()
        self.T_rs = Trk()


def norm_A(g, nb, xt, T_xt, nrows, nsub):
    K = g.K
    R = nrows
    for s in range(nsub):
        K.op("act", lambda e: e.activation(out=nb.junk[0:R, :], in_=xt[0:R, s, :], func=AF.Square,
                                           accum_out=nb.ss[0:R, s:s + 1]),
             reads=[T_xt], writes=[nb.T_junk, nb.T_ss])
    K.op("dve", lambda e: e.tensor_scalar(out=nb.rs[0:R, 0:nsub], in0=nb.ss[0:R, 0:nsub], scalar1=1.0 / D, scalar2=EPS,
                                          op0=ALU.mult, op1=ALU.add), reads=[nb.T_ss], writes=[nb.T_rs])
    K.op("act", lambda e: e.activation(out=nb.rs[0:R, 0:nsub], in_=nb.rs[0:R, 0:nsub], func=AF.Sqrt),
         reads=[nb.T_rs], writes=[nb.T_rs])
    K.op("dve", lambda e: e.reciprocal(out=nb.rs[0:R, 0:nsub], in_=nb.rs[0:R, 0:nsub]), reads=[nb.T_rs], writes=[nb.T_rs])
    for s in range(nsub):
        if s % 2 == 0:
            K.op("dve", lambda e: e.tensor_scalar(out=nb.xs[0:R, s, :], in0=xt[0:R, s, :], scalar1=nb.rs[0:R, s:s + 1],
                                                  scalar2=None, op0=ALU.mult),
                 reads=[T_xt, nb.T_rs], writes=[nb.T_xs])
        else:
            K.op("act", lambda e: e.activation(out=nb.xs[0:R, s, :], in_=xt[0:R, s, :], func=AF.Copy, scale=nb.rs[0:R, s:s + 1]),
                 reads=[T_xt, nb.T_rs], writes=[nb.T_xs])


def norm_B(g, nb, nrows, nsub, A, B, out_fn, T_out, psb=(0, 1), kcs=range(8), eng_force=None):
    K = g.K
    R = nrows
    for kc in kcs:
        b = psb[kc % len(psb)]
        for s in range(nsub):
            K.op("pe", lambda e: e.transpose(g.ps[b][:, s * R:(s + 1) * R], nb.xs[0:R, s, kc * 128:(kc + 1) * 128],
                                             g.ident[0:R, 0:R]),
                 reads=[nb.T_xs, g.T_ident], writes=[g.pst[b]], inc=(s == nsub - 1))
        eng = eng_force or ("act" if b % 2 == 0 else "dve")
        if eng == "act":
            K.op("act", lambda e: e.activation(out=out_fn(kc), in_=g.ps[b][:, 0:nsub * R], func=AF.Identity,
                                               scale=A(kc), bias=B(kc)),
                 reads=[g.pst[b], g.T_AB, g.T_modT], writes=[T_out])
        else:
            K.op("dve", lambda e: e.tensor_scalar(out=out_fn(kc), in0=g.ps[b][:, 0:nsub * R], scalar1=A(kc), scalar2=B(kc),
                                                  op0=ALU.mult, op1=ALU.add),
                 reads=[g.pst[b], g.T_AB, g.T_modT], writes=[T_out])


def norm_T(g, nb, xt, T_xt, nrows, nsub, A, B, out_fn, T_out, psb=(0, 1)):
    norm_A(g, nb, xt, T_xt, nrows, nsub)
    norm_B(g, nb, nrows, nsub, A, B, out_fn, T_out, psb)


def ffn_phase(g, layer, stream, L, src, T_src, dst, T_dst, tag):
    K, nc = g.K, g.nc
    T = min(512, L)
    NT = L // T
    nsub = T // 128
    A, B = mod_scalars(g, layer, stream, 1)
    wup = g.d["wup_bf"][layer]
    wdn = g.d["wdown_bf"][layer]
    with ExitStack() as ph:
        g2bc, T_g2 = load_gate_bc(g, ph, tag + "_g2", layer, stream, 1)
        wd = SB(g, ph, tag + "_wd", [128, 22, 1024], BF16)
        T_wd = Trk()
        K.dma("sp", wd[:], wdn.rearrange("k p n -> p k n"), reads=[g.T_wbf], writes=[T_wd])
        NWB = 4
        wu = [SB(g, ph, tag + "_wu%d" % i, [128, 2, 8, 128], BF16) for i in range(NWB)]
        T_wu = trks(NWB)
        xt = [SB(g, ph, tag + "_xt%d" % i, [128, nsub, 1024], F32) for i in range(2)]
        T_xt = trks(2)
        hr = SB(g, ph, tag + "_hr", [2, 1, 1024], F32)
        T_hr = Trk()
        nb = NormBufs(g, ph, tag + "_nb", nsub)
        nbh = NormBufs(g, ph, tag + "_nbh", 1)
        hT = SB(g, ph, tag + "_hT", [128, 8, T + 2], BF16)
        T_hT = Trk()
        actb = SB(g, ph, tag + "_act", [128, 22, T], BF16)
        T_act = Trk()
        Gs = [SB(g, ph, tag + "_G%d" % i, [128, T + 2], F32) for i in range(2)]
        T_G = trks(2)
        tb = [SB(g, ph, tag + "_tb%d" % i, [128, T], F32) for i in range(2)]
        T_tb = trks(2)
        vb = [SB(g, ph, tag + "_vb%d" % i, [128, T], F32) for i in range(2)]
        T_vb = trks(2)
        ob = [SB(g, ph, tag + "_ob%d" % i, [128, 512], F32) for i in range(2)]
        T_ob = trks(2)
        hps = g.ps[2]
        T_hps = g.pst[2]
        wcount = 0
        oc = 0
        hTs = [hT, SB(g, ph, tag + "_hTb", [128, 8, T + 2], BF16)]
        T_hTs = [T_hT, Trk()]

        def prep_A(ti):
            t0 = ti * T
            xb = ti % 2
            K.dma("sp", xt[xb][:], src[t0:t0 + T].rearrange("(s p) d -> p s d", p=128), reads=[T_src], writes=[T_xt[xb]])
            K.op("pool", lambda e: e.memset(hr[:], 0.0), writes=[T_hr])
            if t0 > 0:
                K.dma("sp", hr[0:1, 0, :], src[t0 - 1:t0, :], reads=[T_src], writes=[T_hr])
            if t0 + T < L:
                K.dma("sp", hr[1:2, 0, :], src[t0 + T:t0 + T + 1, :], reads=[T_src], writes=[T_hr])
            norm_A(g, nb, xt[xb], T_xt[xb], 128, nsub)
            norm_A(g, nbh, hr, T_hr, 2, 1)

        def prep_B(ti, kcs=range(8)):
            t0 = ti * T
            hTc, T_hTc = hTs[ti % 2], T_hTs[ti % 2]
            norm_B(g, nb, 128, nsub, A, B, lambda kc: hTc[:, kc, 1:T + 1], T_hTc, psb=(0,), kcs=kcs)
            norm_B(g, nbh, 2, 1, A, B, lambda kc: hTc[:, kc, 0:T + 2:T + 1], T_hTc, psb=(2,), kcs=kcs, eng_force="dve")
            if 7 not in kcs:
                return
            if t0 == 0:
                K.op("pool", lambda e: e.memset(hTc[:, :, 0:1], 0.0), writes=[T_hTc])
            if t0 + T >= L:
                K.op("pool", lambda e: e.memset(hTc[:, :, T + 1:T + 2], 0.0), writes=[T_hTc])

        wissued = [0]
        ftail = []
        prep_A(0)
        prep_B(0)
        for ti in range(NT):
            t0 = ti * T
            xb = ti % 2
            hT, T_hT = hTs[ti % 2], T_hTs[ti % 2]
            for c in range(22):
                if ti + 1 < NT and c == 3:
                    prep_A(ti + 1)
                if ti + 1 < NT and 10 <= c < 18:
                    prep_B(ti + 1, kcs=[c - 10])
                wb = wcount % NWB
                wcount += 1
                while wissued[0] < min(NT * 22, wcount + NWB - 1):
                    wi_ = wissued[0]
                    wissued[0] += 1
                    cc_ = wi_ % 22
                    K.dma("sp", wu[wi_ % NWB][:].rearrange("p a k n -> p a (k n)"),
                          wup[2 * cc_:2 * cc_ + 2].rearrange("a p n -> p a n"), reads=[g.T_wbf], writes=[T_wu[wi_ % NWB]])
                pg, pv = g.ps[3 + c % 2], g.ps[5 + c % 2]
                Tpg, Tpv = g.pst[3 + c % 2], g.pst[5 + c % 2]
                hcol = 32 + 2 * (c % 2)
                for kc in range(8):
                    K.op("pe", lambda e: e.matmul(pg[:, 0:T], lhsT=wu[wb][:, 0, kc, :], rhs=hT[:, kc, 1:T + 1],
                                                  start=(kc == 0), stop=(kc == 7)),
                         reads=[T_wu[wb], T_hT], writes=[Tpg], inc=(kc == 7))
                for kc in range(8):
                    K.op("pe", lambda e: e.matmul(hps[:, hcol:hcol + 2], lhsT=wu[wb][:, 0, kc, :], rhs=hT[:, kc, 0:T + 2:T + 1],
                                                  start=(kc == 0), stop=(kc == 7)),
                         reads=[T_wu[wb], T_hT], writes=[T_hps], inc=(kc == 7))
                for kc in range(8):
                    K.op("pe", lambda e: e.matmul(pv[:, 0:T], lhsT=wu[wb][:, 1, kc, :], rhs=hT[:, kc, 1:T + 1],
                                                  start=(kc == 0), stop=(kc == 7)),
                         reads=[T_wu[wb], T_hT], writes=[Tpv], inc=(kc == 7))
                gb = c % 2
                Gt = Gs[gb]
                K.op("act", lambda e: e.activation(out=Gt[:, 1:T + 1], in_=pg[:, 0:T], func=AF.Copy), reads=[Tpg], writes=[T_G[gb]])
                K.op("dve", lambda e: e.tensor_copy(out=Gt[:, 0:T + 2:T + 1], in_=hps[:, hcol:hcol + 2]), reads=[T_hps], writes=[T_G[gb]])
                K.op("act", lambda e: e.activation(out=vb[gb][:], in_=pv[:, 0:T], func=AF.Copy), reads=[Tpv], writes=[T_vb[gb]])
                cw = lambda j: g.fconv[:, layer, c, j:j + 1]
                K.op("dve", lambda e: e.tensor_scalar(out=tb[gb][:], in0=Gt[:, 1:T + 1], scalar1=cw(1), scalar2=cw(3),
                                                      op0=ALU.mult, op1=ALU.add),
                     reads=[T_G[gb], g.T_const], writes=[T_tb[gb]])
                K.op("dve", lambda e: e.scalar_tensor_tensor(out=tb[gb][:], in0=Gt[:, 0:T], scalar=cw(0), in1=tb[gb][:],
                                                             op0=ALU.mult, op1=ALU.add),
                     reads=[T_G[gb], g.T_const, T_tb[gb]], writes=[T_tb[gb]])
                K.op("dve", lambda e: e.scalar_tensor_tensor(out=tb[gb][:], in0=Gt[:, 2:T + 2], scalar=cw(2), in1=tb[gb][:],
                                                             op0=ALU.mult, op1=ALU.add),
                     reads=[T_G[gb], g.T_const, T_tb[gb]], writes=[T_tb[gb]])

                def tail(gb=gb, c=c):
                    K.op("act", lambda e: e.activation(out=tb[gb][:], in_=tb[gb][:], func=AF.Silu), reads=[T_tb[gb]], writes=[T_tb[gb]])
                    K.op("pool", lambda e: e.tensor_tensor(out=actb[:, c, :], in0=tb[gb][:], in1=vb[gb][:], op=ALU.mult),
                         reads=[T_tb[gb], T_vb[gb]], writes=[T_act])
                ftail.append(tail)
                if len(ftail) > 1:
                    ftail.pop(0)()
            while ftail:
                ftail.pop(0)()
            for m in range(nsub):
                for nh in range(2):
                    pb = 7 if (oc % 2 == 0) else 1
                    o = oc % 2
                    oc += 1
                    for kc in range(22):
                        K.op("pe", lambda e: e.matmul(g.ps[pb][:, :], lhsT=actb[:, kc, m * 128:(m + 1) * 128],
                                                      rhs=wd[:, kc, nh * 512:(nh + 1) * 512],
                                                      start=(kc == 0), stop=(kc == 21)),
                             reads=[T_act, T_wd], writes=[g.pst[pb]], inc=(kc == 21))
                    K.op("dve", lambda e: e.tensor_tensor(out=ob[o][:], in0=g.ps[pb][:, :], in1=g2bc[:, nh * 512:(nh + 1) * 512],
                                                          op=ALU.mult),
                         reads=[g.pst[pb], T_g2], writes=[T_ob[o]])
                    K.op("pool", lambda e: e.tensor_tensor(out=ob[o][:], in0=ob[o][:], in1=xt[xb][:, m, nh * 512:(nh + 1) * 512],
                                                           op=ALU.add),
                         reads=[T_ob[o], T_xt[xb]], writes=[T_ob[o]])
                    K.dma("sp", dst[t0 + m * 128:t0 + (m + 1) * 128, nh * 512:(nh + 1) * 512], ob[o][:],
                          reads=[T_ob[o]], writes=[T_dst])
        K.barrier()


class PiecePool:
    def __init__(self, g, es, tag, n, shape, dt):
        self.bufs = [SB(g, es, "%s%d" % (tag, i), shape, dt) for i in range(n)]
        self.T = trks(n)
        self.i = 0

    def next(self):
        b = self.i % len(self.bufs)
        self.i += 1
        return self.bufs[b], self.T[b]


def dft_bankcol(part, gi, wide):
    if wide:
        return 4 * part + gi, 0
    return 2 * part + gi // 2, (gi % 2) * 256


def fwd_dft(g, L, CT, ST, lhs_c, lhs_s, T_lhs, consumer, pp, alt, T_alt, W=None):
    K = g.K
    NS = L // 128
    W = W or min(256, L)
    wide = (W == 512)
    PSC = min(16, NS)
    ftiles = [(f0, W) for f0 in range(0, L, W)] + [(L, 1)]
    for (f0, Wt) in ftiles:
        nyq = (f0 == L)
        for part, (tab, lhs, bank0) in enumerate([(CT, lhs_c, 0), (ST, lhs_s, 2)]):
            if nyq and part == 1:
                continue
            pieces = []
            if not nyq:
                for pc in range(NS // PSC):
                    piece, T_piece = pp.next()
                    K.dma("sp", piece[:, 0:PSC, 0:Wt],
                          tab[pc * PSC * 128:(pc + 1) * PSC * 128, f0:f0 + Wt].rearrange("(c p) f -> p c f", p=128),
                          writes=[T_piece])
                    pieces.append((piece, T_piece))
            for gi in range(4):
                bank, col = dft_bankcol(part, gi, wide)
                for sc in range(NS):
                    if nyq:
                        rhs, T_r = alt[:, 0:1], T_alt
                    else:
                        piece, T_piece = pieces[sc // PSC]
                        rhs, T_r = piece[:, sc % PSC, 0:Wt], T_piece
                    K.op("pe", lambda e: e.matmul(g.ps[bank][:, col:col + Wt], lhsT=lhs(sc, gi), rhs=rhs,
                                                  start=(sc == 0), stop=(sc == NS - 1)),
                         reads=[T_r] + T_lhs, writes=[g.pst[bank]], inc=(sc % PSC == PSC - 1))
        consumer(f0, Wt, nyq)


def mixer0(g, stream, L, src, T_src, dst, T_dst, tag):
    K, nc, d = g.K, g.nc, g.d
    NS = L // 128
    T = min(512, L)
    NT = L // T
    nsub = T // 128
    sfx = "_%d" % L
    CT, ST, C4, S4n = d["CT" + sfx], d["ST" + sfx], d["C4" + sfx], d["S4n" + sfx]
    Kre, Ksn = d["Kre" + sfx], d["Ksn" + sfx]
    ABd = d["ABd" + sfx]
    X0d, Zd = d["X0d" + sfx], d["Zd" + sfx]
    YTd, YTn = d["YTd" + sfx], d["YTn" + sfx]
    T_K, T_ABd, T_X0d, T_Zd, T_YTd = DTrk(), DTrk(), DTrk(), DTrk(), DTrk()
    A, B = mod_scalars(g, 0, stream, 0)
    W = min(256, L)
    PSC = min(16, NS)
    with ExitStack() as mx:
        cst = SB(g, mx, tag + "_alt", [128, 1], BF16)
        T_cst = Trk()
        K.dma("sp", cst[:], d["alt"], writes=[T_cst])
        rn2 = SB(g, mx, tag + "_rn2", [128, 4], F32)
        T_rn2 = Trk()
        with ExitStack() as ph1:
            fsT = SB(g, ph1, tag + "_fsT", [128, NS, 512], BF16)
            fdT = SB(g, ph1, tag + "_fdT", [128, NS, 512], BF16)
            T_fs, T_fd = Trk(), Trk()
            with ExitStack() as ph:
                zT = SB(g, ph, tag + "_zT", [33, L], F32)
                w1 = SB(g, ph, tag + "_w1", [33, 64], F32)
                w2 = SB(g, ph, tag + "_w2", [64, 64], F32)
                w3 = SB(g, ph, tag + "_w3", [64, 64], F32)
                w4 = SB(g, ph, tag + "_w4", [64, 1024], F32)
                hyp = SB(g, ph, tag + "_hyp", [64, 4], F32)
                sc1 = SB(g, ph, tag + "_sc1", [64, 4], F32)
                negt = SB(g, ph, tag + "_negt", [128, NS], F32)
                dbc = SB(g, ph, tag + "_dbc", [128, 512], F32)
                acc = SB(g, ph, tag + "_acc", [128, 512], F32)
                ones = SB(g, ph, tag + "_ones", [128, 1], F32)
                h3 = SB(g, ph, tag + "_h3", [64, L], F32)
                T_w, T_sc1, T_acc, T_h3, T_ones = Trk(), Trk(), Trk(), Trk(), Trk()
                K.dma("sp", zT[:], d["hyz" + sfx], writes=[T_w])
                K.dma("sp", w1[:], d["hy_w1"], writes=[T_w])
                K.dma("sp", w2[:], d["hy_w2"], writes=[T_w])
                K.dma("sp", w3[:], d["hy_w3"], writes=[T_w])
                K.dma("sp", w4[:], d["hy_w4"], writes=[T_w])
                K.dma("sp", hyp[:], d["hyp"], writes=[T_w])
                K.dma("sp", negt[:], d["hynegt" + sfx], writes=[T_w])
                K.dma("sp", dbc[:], d["hydelta"].broadcast_to([128, 512]), writes=[T_w])
                K.op("pool", lambda e: e.memset(acc[:], 0.0), writes=[T_acc])
                K.op("pool", lambda e: e.memset(ones[:], 1.0), writes=[T_ones])
                K.op("dve", lambda e: e.tensor_scalar(out=sc1[:, 0:1], in0=hyp[:, 0:1], scalar1=1.0 / (2 * PI), scalar2=None,
                                                      op0=ALU.mult), reads=[T_w], writes=[T_sc1])
                for i in range(1, 4):
                    K.op("dve", lambda e: e.tensor_scalar(out=sc1[:, i:i + 1], in0=hyp[:, i:i + 1], scalar1=sc1[:, 0:1],
                                                          scalar2=64.0, op0=ALU.mult, op1=ALU.add),
                         reads=[T_w, T_sc1], writes=[T_sc1])
                TW = min(512, L)
                q = SB(g, ph, tag + "_q", [64, TW], F32)
                ki = SB(g, ph, tag + "_ki", [64, TW], I32)
                kf = SB(g, ph, tag + "_kf", [64, TW], F32)
                hb_ = [SB(g, ph, tag + "_hm%d" % i, [64, TW], F32) for i in range(2)]
                T_q, T_ki, T_kf = Trk(), Trk(), Trk()
                T_hm = trks(2)
                ws = [w1, w2, w3]
                dec = [SB(g, ph, tag + "_dec%d" % i, [128, 512], F32) for i in range(2)]
                hf = [SB(g, ph, tag + "_hf%d" % i, [128, 512], F32) for i in range(2)]
                hb = [SB(g, ph, tag + "_hb%d" % i, [128, 512], F32) for i in range(2)]
                T_dec, T_hf, T_hb = trks(2), trks(2), trks(2)
                yield
                for tt in range(L // TW):
                    cur, T_cur = zT[:, tt * TW:(tt + 1) * TW], T_w
                    for ly in range(3):
                        b = ly % 2
                        K.op("pe", lambda e: e.matmul(g.ps[b][0:64, 0:TW], lhsT=ws[ly][:], rhs=cur, start=True, stop=True),
                             reads=[T_w, T_cur], writes=[g.pst[b]])
                        K.op("dve", lambda e: e.tensor_scalar(out=q[:], in0=g.ps[b][0:64, 0:TW], scalar1=sc1[:, 0:1],
                                                              scalar2=sc1[:, ly + 1:ly + 2], op0=ALU.mult, op1=ALU.add),
                             reads=[g.pst[b], T_sc1], writes=[T_q])
                        K.op("dve", lambda e: e.tensor_copy(out=ki[:], in_=q[:]), reads=[T_q], writes=[T_ki])
                        K.op("dve", lambda e: e.tensor_copy(out=kf[:], in_=ki[:]), reads=[T_ki], writes=[T_kf])
                        K.op("dve", lambda e: e.tensor_tensor(out=q[:], in0=q[:], in1=kf[:], op=ALU.subtract),
                             reads=[T_q, T_kf], writes=[T_q])
                        K.op("dve", lambda e: e.scalar_tensor_tensor(out=kf[:], in0=q[:], scalar=0.5, in1=q[:],
                                                                     op0=ALU.is_gt, op1=ALU.subtract),
                             reads=[T_q], writes=[T_kf])
                        if ly < 2:
                            o_ap, T_o = hb_[ly][:], T_hm[ly]
                        else:
                            o_ap, T_o = h3[:, tt * TW:(tt + 1) * TW], T_h3
                        K.op("act", lambda e: e.activation(out=o_ap, in_=kf[:], func=AF.Sin, scale=-2 * PI * (1 - 1e-6)),
                             reads=[T_kf], writes=[T_o])
                        cur, T_cur = o_ap, T_o
                        yield
                for sc in range(NS):
                    yield
                    b = sc % 2
                    pf, pb_ = g.ps[2 * b], g.ps[2 * b + 1]
                    K.op("pe", lambda e: e.matmul(pf[:, :], lhsT=h3[:, sc * 128:(sc + 1) * 128], rhs=w4[:, 0:512],
                                                  start=True, stop=True), reads=[T_h3, T_w], writes=[g.pst[2 * b]])
                    K.op("pe", lambda e: e.matmul(pb_[:, :], lhsT=h3[:, sc * 128:(sc + 1) * 128], rhs=w4[:, 512:1024],
                                                  start=True, stop=True), reads=[T_h3, T_w], writes=[g.pst[2 * b + 1]])
                    K.op("act", lambda e: e.activation(out=dec[b][:], in_=dbc[:], func=AF.Exp, scale=negt[:, sc:sc + 1]),
                         reads=[T_w], writes=[T_dec[b]])
                    K.op("dve", lambda e: e.tensor_tensor(out=hf[b][:], in0=pf[:, :], in1=dec[b][:], op=ALU.mult),
                         reads=[g.pst[2 * b], T_dec[b]], writes=[T_hf[b]])
                    K.op("dve", lambda e: e.tensor_tensor(out=hb[b][:], in0=pb_[:, :], in1=dec[b][:], op=ALU.mult),
                         reads=[g.pst[2 * b + 1], T_dec[b]], writes=[T_hb[b]])
                    if sc == 0:
                        K.op("dve", lambda e: e.memset(hb[b][0:1, :], 0.0), writes=[T_hb[b]])
                    K.op("pool", lambda e: e.tensor_tensor(out=fsT[:, sc, :], in0=hf[b][:], in1=hb[b][:], op=ALU.add),
                         reads=[T_hf[b], T_hb[b]], writes=[T_fs])
                    K.op("pool", lambda e: e.tensor_tensor(out=fdT[:, sc, :], in0=hf[b][:], in1=hb[b][:], op=ALU.subtract),
                         reads=[T_hf[b], T_hb[b]], writes=[T_fd])
                    for (hsrc, T_hs) in ((hf[b], T_hf[b]), (hb[b], T_hb[b])):
                        K.op("act", lambda e: e.activation(out=dec[b][:], in_=hsrc[:], func=AF.Abs),
                             reads=[T_hs], writes=[T_dec[b]])
                        K.op("pool", lambda e: e.tensor_tensor(out=acc[:], in0=acc[:], in1=dec[b][:], op=ALU.add),
                             reads=[T_dec[b], T_acc], writes=[T_acc])
                for gi in range(4):
                    K.op("pe", lambda e: e.matmul(g.ps[4][:, gi:gi + 1], lhsT=acc[:, gi * 128:(gi + 1) * 128], rhs=ones[:, 0:1],
                                                  start=True, stop=True), reads=[T_acc, T_ones], writes=[g.pst[4]])
                K.op("dve", lambda e: e.reciprocal(out=rn2[:], in_=g.ps[4][:, 0:4]), reads=[g.pst[4]], writes=[T_rn2])
                K.op("dve", lambda e: e.tensor_scalar(out=rn2[:], in0=rn2[:], scalar1=2.0 / (2 * L), scalar2=None, op0=ALU.mult),
                     reads=[T_rn2], writes=[T_rn2])
                K.barrier()
            with ExitStack() as ph:
                W2 = min(512, L)
                wide2 = (W2 == 512)
                pp = PiecePool(g, ph, tag + "_pcA", 4, [128, PSC, W2], BF16)
                kst = [SB(g, ph, tag + "_kst%d" % i, [128, 4, W2], F32) for i in range(4)]
                T_kst = trks(4)
                kcount = [0]

                def kcons(f0, Wt, nyq):
                    for part, (bank0, dstK) in enumerate([(0, Kre), (2, Ksn)]):
                        kb = kcount[0] % 4
                        kcount[0] += 1
                        if nyq and part == 1:
                            K.op("pool", lambda e: e.memset(kst[kb][:, :, 0:1], 0.0), writes=[T_kst[kb]])
                        else:
                            for gi in range(4):
                                bank, col = dft_bankcol(part, gi, wide2)
                                K.op("act", lambda e: e.activation(out=kst[kb][:, gi, 0:Wt], in_=g.ps[bank][:, col:col + Wt],
                                                                   func=AF.Copy, scale=rn2[:, gi:gi + 1]),
                                     reads=[g.pst[bank], T_rn2], writes=[T_kst[kb]])
                            if f0 == 0 or nyq:
                                K.op("dve", lambda e: e.tensor_scalar(out=kst[kb][:, :, 0:1], in0=kst[kb][:, :, 0:1], scalar1=0.5,
                                                                      scalar2=None, op0=ALU.mult),
                                     reads=[T_kst[kb]], writes=[T_kst[kb]])
                        K.dma("sp", dstK[:, :, f0:f0 + Wt].rearrange("g p f -> p g f"), kst[kb][:, :, 0:Wt],
                              reads=[T_kst[kb]], writes=[T_K], allow_slow_non_contiguous=(Wt == 1))
                fwd_dft(g, L, CT, ST, lambda sc, gi: fsT[:, sc, gi * 128:(gi + 1) * 128],
                        lambda sc, gi: fdT[:, sc, gi * 128:(gi + 1) * 128], [T_fs, T_fd], kcons, pp, cst, T_cst, W=W2)
                K.barrier()
        with ExitStack() as ph3:
            zTt = SB(g, ph3, tag + "_zTt", [128, NS, 512], BF16)
            T_zTt = Trk()
            with ExitStack() as ph:
                hxT = SB(g, ph, tag + "_hxT", [128, 8, L], BF16)
                T_hxT = Trk()
                with ExitStack() as pa:
                    xt = [SB(g, pa, tag + "_xt%d" % i, [128, nsub, 1024], F32) for i in range(2)]
                    T_xt = trks(2)
                    nb = NormBufs(g, pa, tag + "_nb", nsub)
                    for ti in range(NT):
                        t0 = ti * T
                        xb = ti % 2
                        K.dma("sp", xt[xb][:], src[t0:t0 + T].rearrange("(s p) d -> p s d", p=128), reads=[T_src],
                              writes=[T_xt[xb]])
                        norm_T(g, nb, xt[xb], T_xt[xb], 128, nsub, A, B, lambda kc: hxT[:, kc, t0:t0 + T], T_hxT)
                    K.barrier()
                win = g.d["fhwin_bf"]
                wpp = PiecePool(g, ph, tag + "_win", 3, [128, 8, 128], BF16)
                cs128 = SB(g, ph, tag + "_cs128", [128, 256], BF16)
                hyc = SB(g, ph, tag + "_hyc", [128, 12, 4], F32)
                T_c3 = Trk()
                K.dma("sp", cs128[:], d["cs128"], writes=[T_c3])
                K.dma("sp", hyc[:], d["hyconv"], writes=[T_c3])
                UT = [SB(g, ph, tag + "_UT%d" % i, [128, T], BF16) for i in range(2)]
                T_UT = trks(2)
                abt = [SB(g, ph, tag + "_abt%d" % i, [128, 2, 256], BF16) for i in range(2)]
                T_abt = trks(2)
                uc = 0
                ac = 0
                for gi in range(4):
                    wt, T_wt = wpp.next()
                    K.dma("sp", wt[:].rearrange("p k n -> p (k n)"), win[gi], reads=[g.T_wbf], writes=[T_wt])
                    for ti in range(NT):
                        t0 = ti * T
                        ub = uc % 2
                        uc += 1
                        pb_ = 0 + ub
                        for kc in range(8):
                            K.op("pe", lambda e: e.matmul(g.ps[pb_][:, 0:T], lhsT=wt[:, kc, :], rhs=hxT[:, kc, t0:t0 + T],
                                                          start=(kc == 0), stop=(kc == 7)),
                                 reads=[T_wt, T_hxT], writes=[g.pst[pb_]], inc=(kc == 7))
                        K.op("act", lambda e: e.activation(out=UT[ub][:], in_=g.ps[pb_][:, 0:T], func=AF.Copy),
                             reads=[g.pst[pb_]], writes=[T_UT[ub]])
                        for s2 in range(nsub // 2):
                            ab = ac % 2
                            ac += 1
                            pb2 = 2 + ab
                            for h in range(2):
                                sub = s2 * 2 + h
                                K.op("pe", lambda e: e.matmul(g.ps[pb2][:, h * 256:(h + 1) * 256],
                                                              lhsT=UT[ub][:, sub * 128:(sub + 1) * 128], rhs=cs128[:],
                                                              start=True, stop=True),
                                     reads=[T_UT[ub], T_c3], writes=[g.pst[pb2]], inc=(h == 1))
                            K.op("dve", lambda e: e.tensor_copy(out=abt[ab][:], in_=g.ps[pb2][:, :].rearrange("p (h c) -> p h c", h=2)),
                                 reads=[g.pst[pb2]], writes=[T_abt[ab]])
                            r0 = t0 + s2 * 256
                            K.dma("sp", ABd[r0:r0 + 256, gi, :].rearrange("(h p) c -> p h c", p=128), abt[ab][:],
                                  reads=[T_abt[ab]], writes=[T_ABd])
                P32 = [SB(g, ph, tag + "_P32%d" % i, [128, L + 2], F32) for i in range(2)]
                T_P32 = trks(2)
                for i in range(2):
                    K.op("pool", lambda e: e.memset(P32[i][:, 0:L + 2:L + 1], 0.0), writes=[T_P32[i]])
                cx1 = SB(g, ph, tag + "_cx1", [128, L], F32)
                z32 = SB(g, ph, tag + "_z32", [128, L], F32)
                x0b = SB(g, ph, tag + "_x0b", [128, L], BF16)
                zb = SB(g, ph, tag + "_zb", [128, L], BF16)
                T_cx1, T_z32, T_x0b, T_zb = Trk(), Trk(), Trk(), Trk()
                pc_ = 0
                zyc = 0
                for gi in range(4):
                    for which, cc in enumerate([4 + gi, 8 + gi, 12 + gi]):
                        wt, T_wt = wpp.next()
                        K.dma("sp", wt[:].rearrange("p k n -> p (k n)"), win[cc], reads=[g.T_wbf], writes=[T_wt])
                        pb32 = pc_ % 2
                        pc_ += 1
                        Pt, T_Pt = P32[pb32], T_P32[pb32]
                        hch = which * 4 + gi
                        cw = lambda j: hyc[:, hch, j:j + 1]
                        if which == 1:
                            o32, T_o = cx1, T_cx1
                        else:
                            o32, T_o = z32, T_z32
                        for ti in range(NT):
                            t0 = ti * T
                            ub = uc % 2
                            uc += 1
                            pb_ = 0 + ub
                            for kc in range(8):
                                K.op("pe", lambda e: e.matmul(g.ps[pb_][:, 0:T], lhsT=wt[:, kc, :], rhs=hxT[:, kc, t0:t0 + T],
                                                              start=(kc == 0), stop=(kc == 7)),
                                     reads=[T_wt, T_hxT], writes=[g.pst[pb_]], inc=(kc == 7))
                            K.op("act", lambda e: e.activation(out=Pt[:, 1 + t0:1 + t0 + T], in_=g.ps[pb_][:, 0:T], func=AF.Copy),
                                 reads=[g.pst[pb_]], writes=[T_Pt])
                            K.op("act", lambda e: e.activation(out=o32[:, t0:t0 + T], in_=g.ps[pb_][:, 0:T], func=AF.Identity,
                                                               scale=cw(1), bias=cw(3)),
                                 reads=[g.pst[pb_], T_c3], writes=[T_o])
                        for hh in range(2):
                            c0 = hh * (L // 2)
                            c1 = c0 + L // 2
                            K.op("dve", lambda e: e.scalar_tensor_tensor(out=o32[:, c0:c1], in0=Pt[:, c0:c1], scalar=cw(0),
                                                                         in1=o32[:, c0:c1], op0=ALU.mult, op1=ALU.add),
                                 reads=[T_Pt, T_c3, T_o], writes=[T_o])
                            K.op("dve", lambda e: e.scalar_tensor_tensor(out=o32[:, c0:c1], in0=Pt[:, 2 + c0:2 + c1], scalar=cw(2),
                                                                         in1=o32[:, c0:c1], op0=ALU.mult, op1=ALU.add),
                                 reads=[T_Pt, T_c3, T_o], writes=[T_o])
                        if which == 0:
                            K.op("pool", lambda e: e.tensor_copy(out=x0b[:], in_=z32[:]), reads=[T_z32], writes=[T_x0b])
                            K.dma("sp", X0d[gi], x0b[:], reads=[T_x0b], writes=[T_X0d])
                        elif which == 2:
                            K.op("pool", lambda e: e.tensor_tensor(out=z32[:], in0=z32[:], in1=cx1[:], op=ALU.mult),
                                 reads=[T_z32, T_cx1], writes=[T_z32])
                            K.op("act", lambda e: e.activation(out=zb[:], in_=z32[:], func=AF.Copy), reads=[T_z32], writes=[T_zb])
                            K.dma("sp", Zd[gi], zb[:], reads=[T_zb], writes=[T_Zd])
                            for s4 in range(NS // 4 if NS >= 4 else 1):
                                nin = min(4, NS)
                                zb_ = 4 + zyc % 4
                                zyc += 1
                                for h in range(nin):
                                    sc = s4 * 4 + h
                                    K.op("pe", lambda e: e.transpose(g.ps[zb_][:, h * 128:(h + 1) * 128],
                                                                     z32[:, sc * 128:(sc + 1) * 128], g.ident[:]),
                                         reads=[T_z32, g.T_ident], writes=[g.pst[zb_]], inc=(h == nin - 1))
                                K.op("dve" if s4 % 2 == 0 else "act",
                                     (lambda e: e.tensor_copy(out=zTt[:, s4 * 4:s4 * 4 + nin, gi * 128:(gi + 1) * 128],
                                                              in_=g.ps[zb_][:, 0:nin * 128].rearrange("p (h c) -> p h c", h=nin)))
                                     if s4 % 2 == 0 else
                                     (lambda e: e.activation(out=zTt[:, s4 * 4:s4 * 4 + nin, gi * 128:(gi + 1) * 128],
                                                             in_=g.ps[zb_][:, 0:nin * 128].rearrange("p (h c) -> p h c", h=nin),
                                                             func=AF.Copy)),
                                     reads=[g.pst[zb_]], writes=[T_zTt])
                K.barrier()
            with ExitStack() as ph:
                pp = PiecePool(g, ph, tag + "_pcB", 4, [128, PSC, W], BF16)
                ktr = [SB(g, ph, tag + "_ktr%d" % i, [128, 4, W], F32) for i in range(2)]
                kts = [SB(g, ph, tag + "_kts%d" % i, [128, 4, W], F32) for i in range(2)]
                T_kt = trks(2)
                ta = [SB(g, ph, tag + "_ta%d" % i, [128, W], F32) for i in range(4)]
                tbb = [SB(g, ph, tag + "_tbb%d" % i, [128, W], F32) for i in range(4)]
                T_ta, T_tbb = trks(4), trks(4)
                yt = [SB(g, ph, tag + "_yt%d" % i, [128, 512], BF16) for i in range(4)]
                T_yt = trks(4)
                cnt = {"f": 0, "e": 0, "y": 0}

                def zcons(f0, Wt, nyq):
                    fb = cnt["f"] % 2
                    cnt["f"] += 1
                    K.dma("sp", ktr[fb][:, :, 0:Wt], Kre[:, :, f0:f0 + Wt].rearrange("g p f -> p g f"), reads=[T_K], writes=[T_kt[fb]],
                          allow_slow_non_contiguous=(Wt == 1))
                    K.dma("sp", kts[fb][:, :, 0:Wt], Ksn[:, :, f0:f0 + Wt].rearrange("g p f -> p g f"), reads=[T_K], writes=[T_kt[fb]],
                          allow_slow_non_contiguous=(Wt == 1))
                    nsb = max(1, Wt // 128)
                    for gi in range(4):
                        Zc = g.ps[gi // 2][:, (gi % 2) * 256:(gi % 2) * 256 + Wt]
                        Zs = g.ps[2 + gi // 2][:, (gi % 2) * 256:(gi % 2) * 256 + Wt]
                        TZc, TZs = g.pst[gi // 2], g.pst[2 + gi // 2]
                        eb = cnt["e"] % 2
                        cnt["e"] += 1
                        yre, T_yre = ta[2 * eb], T_ta[2 * eb]
                        yq, T_yq = ta[2 * eb + 1], T_ta[2 * eb + 1]
                        t1, T_t1 = tbb[2 * eb], T_tbb[2 * eb]
                        t2, T_t2 = tbb[2 * eb + 1], T_tbb[2 * eb + 1]
                        K.op("dve", lambda e: e.tensor_tensor(out=yre[:, 0:Wt], in0=Zc, in1=ktr[fb][:, gi, 0:Wt], op=ALU.mult),
                             reads=[TZc, T_kt[fb]], writes=[T_yre])
                        if not nyq:
                            K.op("dve", lambda e: e.tensor_tensor(out=t1[:, 0:Wt], in0=Zs, in1=kts[fb][:, gi, 0:Wt], op=ALU.mult),
                                 reads=[TZs, T_kt[fb]], writes=[T_t1])
                            K.op("pool", lambda e: e.tensor_tensor(out=yre[:, 0:Wt], in0=yre[:, 0:Wt], in1=t1[:, 0:Wt],
                                                                   op=ALU.subtract), reads=[T_yre, T_t1], writes=[T_yre])
                            K.op("dve", lambda e: e.tensor_tensor(out=yq[:, 0:Wt], in0=Zc, in1=kts[fb][:, gi, 0:Wt], op=ALU.mult),
                                 reads=[TZc, T_kt[fb]], writes=[T_yq])
                            K.op("dve", lambda e: e.tensor_tensor(out=t2[:, 0:Wt], in0=Zs, in1=ktr[fb][:, gi, 0:Wt], op=ALU.mult),
                                 reads=[TZs, T_kt[fb]], writes=[T_t2])
                            K.op("pool", lambda e: e.tensor_tensor(out=yq[:, 0:Wt], in0=yq[:, 0:Wt], in1=t2[:, 0:Wt], op=ALU.add),
                                 reads=[T_yq, T_t2], writes=[T_yq])
                        for part, (ysrc, T_ys) in enumerate([(yre, T_yre), (yq, T_yq)]):
                            if nyq and part == 1:
                                continue
                            for sub in range(nsb):
                                bank = 4 + part * 2 + sub
                                wdt = min(128, Wt)
                                K.op("pe", lambda e: e.transpose(g.ps[bank][0:wdt, gi * 128:(gi + 1) * 128],
                                                                 ysrc[:, sub * 128:sub * 128 + wdt], g.ident[:]),
                                     reads=[T_ys, g.T_ident], writes=[g.pst[bank]])
                    for part in range(2):
                        if nyq and part == 1:
                            continue
                        for sub in range(nsb):
                            bank = 4 + part * 2 + sub
                            wdt = min(128, Wt)
                            yb = cnt["y"] % 4
                            cnt["y"] += 1
                            K.op("act", lambda e: e.activation(out=yt[yb][0:wdt, :], in_=g.ps[bank][0:wdt, :], func=AF.Copy),
                                 reads=[g.pst[bank]], writes=[T_yt[yb]])
                            if nyq:
                                K.dma("sp", YTn[0:1, :], yt[yb][0:1, :], reads=[T_yt[yb]], writes=[T_YTd])
                            else:
                                K.dma("sp", YTd[part, f0 // 128 + sub], yt[yb][:], reads=[T_yt[yb]], writes=[T_YTd])
                fwd_dft(g, L, CT, ST, lambda sc, gi: zTt[:, sc, gi * 128:(gi + 1) * 128],
                        lambda sc, gi: zTt[:, sc, gi * 128:(gi + 1) * 128], [T_zTt], zcons, pp, cst, T_cst)
                K.barrier()
        with ExitStack() as ph5:
            ycat = SB(g, ph5, tag + "_ycat", [128, 8, L], BF16)
            T_ycat = Trk()
            with ExitStack() as ph:
                YT = SB(g, ph, tag + "_YT", [128, 2, NS, 512], BF16)
                YTnr = SB(g, ph, tag + "_YTnr", [1, 512], BF16)
                altrow = SB(g, ph, tag + "_altrow", [1, L], BF16)
                hbias = SB(g, ph, tag + "_hbias", [128, 4], F32)
                T_YT = Trk()
                for part in range(2):
                    K.dma("sp", YT[:, part, :, :], YTd[part].rearrange("c p n -> p c n"), reads=[T_YTd], writes=[T_YT])
                K.dma("sp", YTnr[:], YTn, reads=[T_YTd], writes=[T_YT])
                K.dma("sp", altrow[:], d["altrow" + sfx], writes=[T_YT])
                K.dma("sp", hbias[:], d["hybias"], writes=[T_YT])
                pp = PiecePool(g, ph, tag + "_pcC", 4, [128, 8, 512], BF16)
                PS8 = min(8, NS)
                x0t = [SB(g, ph, tag + "_x0t%d" % i, [128, 4, T], BF16) for i in range(2)]
                zt = [SB(g, ph, tag + "_zt%d" % i, [128, 4, T], BF16) for i in range(2)]
                T_xz = trks(2)
                ut = [SB(g, ph, tag + "_ut%d" % i, [128, T], F32) for i in range(2)]
                T_ut = trks(2)
                ucn = 0
                for ti in range(NT):
                    t0 = ti * T
                    xb = ti % 2
                    K.dma("sp", x0t[xb][:], X0d[:, :, t0:t0 + T].rearrange("g p t -> p g t"), reads=[T_X0d], writes=[T_xz[xb]])
                    K.dma("sp", zt[xb][:], Zd[:, :, t0:t0 + T].rearrange("g p t -> p g t"), reads=[T_Zd], writes=[T_xz[xb]])
                    bk0 = 4 * (ti % 2)
                    first = [True] * 4
                    for part, tab in enumerate([CT, ST]):
                        for pc in range(NS // PS8):
                            piece, T_piece = pp.next()
                            K.dma("sp", piece[:, 0:PS8, 0:T],
                                  tab[pc * PS8 * 128:(pc + 1) * PS8 * 128, t0:t0 + T].rearrange("(c p) f -> p c f", p=128),
                                  writes=[T_piece])
                            for gi in range(4):
                                for j in range(PS8):
                                    fc = pc * PS8 + j
                                    K.op("pe", lambda e: e.matmul(g.ps[bk0 + gi][:, 0:T], lhsT=YT[:, part, fc, gi * 128:(gi + 1) * 128],
                                                                  rhs=piece[:, j, 0:T], start=first[gi], stop=False),
                                         reads=[T_YT, T_piece], writes=[g.pst[bk0 + gi]], inc=(j == PS8 - 1))
                                    first[gi] = False
                    for gi in range(4):
                        K.op("pe", lambda e: e.matmul(g.ps[bk0 + gi][:, 0:T], lhsT=YTnr[0:1, gi * 128:(gi + 1) * 128],
                                                      rhs=altrow[0:1, t0:t0 + T], start=False, stop=True),
                             reads=[T_YT], writes=[g.pst[bk0 + gi]])
                        ub = ucn % 2
                        ucn += 1
                        K.op("dve", lambda e: e.scalar_tensor_tensor(out=ut[ub][:], in0=zt[xb][:, gi, :], scalar=hbias[:, gi:gi + 1],
                                                                     in1=g.ps[bk0 + gi][:, 0:T], op0=ALU.mult, op1=ALU.add),
                             reads=[T_xz[xb], T_YT, g.pst[bk0 + gi]], writes=[T_ut[ub]])
                        K.op("pool", lambda e: e.tensor_tensor(out=ycat[:, 4 + gi, t0:t0 + T], in0=ut[ub][:], in1=x0t[xb][:, gi, :],
                                                               op=ALU.mult),
                             reads=[T_ut[ub], T_xz[xb]], writes=[T_ycat])
                K.barrier()
            with ExitStack() as ph:
                ABs = SB(g, ph, tag + "_ABs", [128, NS, 4, 256], BF16)
                T_ABs = Trk()
                for sc in range(0, NS, 8):
                    n = min(8, NS - sc)
                    K.dma("sp", ABs[:, sc:sc + n, :, :].rearrange("p c g k -> p c (g k)"),
                          ABd[sc * 128:(sc + n) * 128].rearrange("(c p) g k -> p c (g k)", p=128), reads=[T_ABd], writes=[T_ABs])
                half = L // 2
                Th = min(512, half)
                pp = PiecePool(g, ph, tag + "_pcD", 4, [128, 8, Th], BF16)
                PS8 = min(8, NS)
                fsc = 1.0 / float(np.sqrt(128.0 * L))
                qs = [SB(g, ph, tag + "_qs%d" % i, [128, Th], F32) for i in range(2)]
                T_qs = trks(2)
                qc = 0
                it = 0
                for gp in range(2):
                    for k0 in range(0, half, Th):
                        bk0 = 4 * (it % 2)
                        it += 1
                        for part, tab in enumerate([C4, S4n]):
                            pieces = []
                            for pc in range(NS // PS8):
                                piece, T_piece = pp.next()
                                K.dma("sp", piece[:, 0:PS8, 0:Th],
                                      tab[pc * PS8 * 128:(pc + 1) * PS8 * 128, k0:k0 + Th].rearrange("(c p) f -> p c f", p=128),
                                      writes=[T_piece])
                                pieces.append((piece, T_piece))
                            for gl in range(2):
                                gi = 2 * gp + gl
                                bank = bk0 + 2 * gl + part
                                for sc in range(NS):
                                    piece, T_piece = pieces[sc // PS8]
                                    K.op("pe", lambda e: e.matmul(g.ps[bank][:, 0:Th], lhsT=ABs[:, sc, gi, part * 128:(part + 1) * 128],
                                                                  rhs=piece[:, sc % PS8, 0:Th], start=(sc == 0), stop=(sc == NS - 1)),
                                         reads=[T_ABs, T_piece], writes=[g.pst[bank]], inc=(sc % PS8 == PS8 - 1))
                        for gl in range(2):
                            gi = 2 * gp + gl
                            bP, bQ = bk0 + 2 * gl, bk0 + 2 * gl + 1
                            q_ = qc % 2
                            qc += 1
                            K.op("act", lambda e: e.activation(out=qs[q_][:, 0:Th], in_=g.ps[bQ][:, 0:Th], func=AF.Copy, scale=fsc),
                                 reads=[g.pst[bQ]], writes=[T_qs[q_]])
                            K.op("dve", lambda e: e.scalar_tensor_tensor(out=ycat[:, gi, k0:k0 + Th], in0=g.ps[bP][:, 0:Th], scalar=fsc,
                                                                         in1=qs[q_][:, 0:Th], op0=ALU.mult, op1=ALU.add),
                                 reads=[g.pst[bP], T_qs[q_]], writes=[T_ycat])
                            lo = 1 if k0 == 0 else 0
                            m_hi = L - (k0 + lo)
                            m_lo = L - (k0 + Th - 1)
                            stop = m_lo - 1
                            out_m = ycat[:, gi, m_hi:stop:-1] if stop >= 0 else ycat[:, gi, m_hi::-1]
                            K.op("dve", lambda e: e.scalar_tensor_tensor(out=out_m, in0=g.ps[bP][:, lo:Th], scalar=fsc,
                                                                         in1=qs[q_][:, lo:Th], op0=ALU.mult, op1=ALU.subtract),
                                 reads=[g.pst[bP], T_qs[q_]], writes=[T_ycat])
                for gi in range(4):
                    for sc in range(NS):
                        K.op("pe", lambda e: e.matmul(g.ps[0][:, gi:gi + 1], lhsT=ABs[:, sc, gi, 0:128], rhs=cst[:, 0:1],
                                                      start=(sc == 0), stop=(sc == NS - 1)),
                             reads=[T_ABs, T_cst], writes=[g.pst[0]], inc=(sc == NS - 1))
                    K.op("act", lambda e: e.activation(out=ycat[:, gi, half:half + 1], in_=g.ps[0][:, gi:gi + 1], func=AF.Copy, scale=fsc),
                         reads=[g.pst[0]], writes=[T_ycat])
                K.barrier()
            with ExitStack() as ph:
                wo = SB(g, ph, tag + "_wo", [128, 8, 1024], BF16)
                T_wo = Trk()
                K.dma("sp", wo[:], g.d["fhwout_bf"].rearrange("k p n -> p k n"), reads=[g.T_wbf], writes=[T_wo])
                g1bc, T_g1 = load_gate_bc(g, ph, tag + "_g1", 0, stream, 0)
                xt = [SB(g, ph, tag + "_xo%d" % i, [128, 1024], F32) for i in range(2)]
                T_xt = trks(2)
                ob = [SB(g, ph, tag + "_ob%d" % i, [128, 1024], F32) for i in range(2)]
                T_ob = trks(2)
                for m in range(NS):
                    xb = m % 2
                    K.dma("sp", xt[xb][:], src[m * 128:(m + 1) * 128, :], reads=[T_src], writes=[T_xt[xb]])
                    for nh in range(2):
                        pb_ = (2 * m + nh) % 4
                        for kc in range(8):
                            K.op("pe", lambda e: e.matmul(g.ps[pb_][:, :], lhsT=ycat[:, kc, m * 128:(m + 1) * 128],
                                                          rhs=wo[:, kc, nh * 512:(nh + 1) * 512], start=(kc == 0), stop=(kc == 7)),
                                 reads=[T_ycat, T_wo], writes=[g.pst[pb_]], inc=(kc == 7))
                        K.op("dve", lambda e: e.tensor_tensor(out=ob[xb][:, nh * 512:(nh + 1) * 512], in0=g.ps[pb_][:, :],
                                                              in1=g1bc[:, nh * 512:(nh + 1) * 512], op=ALU.mult),
                             reads=[g.pst[pb_], T_g1], writes=[T_ob[xb]])
                    K.op("pool", lambda e: e.tensor_tensor(out=ob[xb][:], in0=ob[xb][:], in1=xt[xb][:], op=ALU.add),
                         reads=[T_ob[xb], T_xt[xb]], writes=[T_ob[xb]])
                    K.dma("sp", dst[m * 128:(m + 1) * 128, :], ob[xb][:], reads=[T_ob[xb]], writes=[T_dst])
                K.barrier()


NH = 16
NKC = (LX + LCX) // 128


def mla(g, src, T_src, csrc, T_csrc, dst, T_dst, tag="ml"):
    K, nc, d = g.K, g.nc, g.d
    L = LX
    Qd, Kd, Vd, Od = d["Qd"], d["Kd"], d["Vd"], d["Od"]
    T_Qd, T_Kd, T_Vd, T_Od = DTrk(), DTrk(), DTrk(), DTrk()
    with ExitStack() as mx:
        with ExitStack() as ph:
            wi = SB(g, ph, tag + "_wi", [128, 8, 416], BF16)
            wuq = SB(g, ph, tag + "_wuq", [128, 2, 1568], BF16)
            wk = SB(g, ph, tag + "_wk", [128, NH, 64], BF16)
            wv = SB(g, ph, tag + "_wv", [128, 1024], BF16)
            mnorm = SB(g, ph, tag + "_mnorm", [128, 4], F32)
            mqk = SB(g, ph, tag + "_mqk", [96, 2], F32)
            prot = SB(g, ph, tag + "_prot", [96, 96], BF16)
            ones = SB(g, ph, tag + "_ones", [128, 128], BF16)
            T_w = Trk()
            K.dma("sp", wi[:].rearrange("p k n -> p (k n)"), d["mwin_bf"], reads=[g.T_wbf], writes=[T_w])
            K.op("pool", lambda e: e.memset(wuq[:, :, 1536:1568], 0.0), writes=[T_w])
            K.dma("sp", wuq[:, :, 0:1536], d["mwuq_bf"].rearrange("p (k n) -> p k n", k=2), reads=[g.T_wbf], writes=[T_w])
            K.dma("sp", wk[:].rearrange("p h n -> p (h n)"), d["mwk_bf"], reads=[g.T_wbf], writes=[T_w])
            K.dma("sp", wv[:], d["mwv_bf"], reads=[g.T_wbf], writes=[T_w])
            K.dma("sp", mnorm[:], d["mnorm"], writes=[T_w])
            K.dma("sp", mqk[:], d["mqk"], writes=[T_w])
            K.dma("sp", prot[:], d["prot"], writes=[T_w])
            K.dma("sp", ones[:], d["ones128"], writes=[T_w])
            T = 512
            xt = [SB(g, ph, tag + "_xt%d" % i, [128, 4, 1024], F32) for i in range(2)]
            T_xt = trks(2)
            nb = NormBufs(g, ph, tag + "_nb", 4)
            hT = SB(g, ph, tag + "_hT", [128, 8, T], BF16)
            T_hT = Trk()
            rc = [SB(g, ph, tag + "_rc%d" % i, [96, T], F32) for i in range(2)]
            rs_ = [SB(g, ph, tag + "_rs%d" % i, [96, T], F32) for i in range(2)]
            T_rope = trks(2)
            sqa = [SB(g, ph, tag + "_sqa%d" % i, [128, T], BF16) for i in range(3)]
            T_sqa = trks(3)
            rq = SB(g, ph, tag + "_rq", [128, T], F32)
            rkv = SB(g, ph, tag + "_rkv", [128, T], F32)
            T_rq, T_rkv = Trk(), Trk()
            aqn = SB(g, ph, tag + "_aqn", [128, 2, T], BF16)
            ckvn = SB(g, ph, tag + "_ckvn", [128, T], BF16)
            T_aqn, T_ckvn = Trk(), Trk()
            NKH = 4
            khs = [SB(g, ph, tag + "_kh%d" % i, [96, T], F32) for i in range(NKH)]
            T_khs = trks(NKH)
            kcnt = 0
            kpe32 = SB(g, ph, tag + "_kpe32", [32, T], F32)
            T_kpe = Trk()
            esel = SB(g, ph, tag + "_esel", [32, 96], F32)
            K.dma("sp", esel[:], d["esel"], writes=[T_w])
            NB = 4
            sqh = [SB(g, ph, tag + "_sqh%d" % i, [96, T], BF16) for i in range(NB)]
            qg = [SB(g, ph, tag + "_qg%d" % i, [96, T], BF16) for i in range(NB)]
            rst = [SB(g, ph, tag + "_rst%d" % i, [96, T], F32) for i in range(NB)]
            t1 = [SB(g, ph, tag + "_t1%d" % i, [96, T], F32) for i in range(NB)]
            t2 = [SB(g, ph, tag + "_t2%d" % i, [96, T], F32) for i in range(NB)]
            qf = [SB(g, ph, tag + "_qf%d" % i, [96, T], BF16) for i in range(NB)]
            T_sqh, T_qg, T_rst, T_t1, T_t2, T_qf = trks(NB), trks(NB), trks(NB), trks(NB), trks(NB), trks(NB)
            vt = [SB(g, ph, tag + "_vt%d" % i, [128, NH, 64], BF16) for i in range(2)]
            T_vt = trks(2)
            bc = 0
            vc = 0
            deferred = []
            tiles = [(0, ti * 512, 512, ti) for ti in range(L // 512)] + [(1, 0, LCX, 8)]
            for (is_ctx, t0, Tw, ti) in tiles:
                nsub = Tw // 128
                xb = ti % 2
                the_src, T_the = (csrc, T_csrc) if is_ctx else (src, T_src)
                A, B = mod_scalars(g, 1, is_ctx, 0)
                K.dma("sp", xt[xb][:, 0:nsub, :], the_src[t0:t0 + Tw].rearrange("(s p) d -> p s d", p=128), reads=[T_the],
                      writes=[T_xt[xb]])
                norm_T(g, nb, xt[xb], T_xt[xb], 128, nsub, A, B, lambda kc: hT[:, kc, 0:Tw], T_hT)
                kpos0 = (LX + t0) if is_ctx else t0
                if not is_ctx:
                    rb = ti % 2
                    K.dma("sp", rc[rb][:, 0:Tw], d["ropeC"][:, t0:t0 + Tw], writes=[T_rope[rb]])
                    K.dma("sp", rs_[rb][:, 0:Tw], d["ropeS"][:, t0:t0 + Tw], writes=[T_rope[rb]])
                cols = [(0, 0, 128, 0), (1, 128, 128, 0), (2, 256, 128, 0), (3, 384, 32, 0)]
                for (bank, c0, cw_, prow) in cols:
                    if is_ctx and bank < 2:
                        continue
                    for kc in range(8):
                        K.op("pe", lambda e: e.matmul(g.ps[bank][prow:prow + cw_, 0:Tw], lhsT=wi[:, kc, c0:c0 + cw_], rhs=hT[:, kc, 0:Tw],
                                                      start=(kc == 0), stop=(kc == 7)),
                             reads=[T_w, T_hT], writes=[g.pst[bank]], inc=(kc == 7))
                K.op("dve", lambda e: e.tensor_copy(out=kpe32[:, 0:Tw], in_=g.ps[3][0:32, 0:Tw]), reads=[g.pst[3]], writes=[T_kpe])
                K.op("pe", lambda e: e.matmul(g.ps[3][0:96, 0:Tw], lhsT=esel[:, :], rhs=kpe32[:, 0:Tw], start=True, stop=True),
                     reads=[T_w, T_kpe], writes=[g.pst[3]])
                for i in range(NKH):
                    K.op("dve", lambda e: e.tensor_copy(out=khs[i][64:96, 0:Tw], in_=g.ps[3][64:96, 0:Tw]), reads=[g.pst[3]], writes=[T_khs[i]])
                if getattr(g, "cutpt", 0) == 1:
                    K.barrier()
                    return
                if not is_ctx:
                    for c in range(2):
                        K.op("act", lambda e: e.activation(out=sqa[c][:, 0:Tw], in_=g.ps[c][:, 0:Tw], func=AF.Square),
                             reads=[g.pst[c]], writes=[T_sqa[c]])
                    for c in range(2):
                        K.op("pe", lambda e: e.matmul(g.ps[4][:, 0:Tw], lhsT=ones[:, :], rhs=sqa[c][:, 0:Tw], start=(c == 0), stop=(c == 1)),
                             reads=[T_w, T_sqa[c]], writes=[g.pst[4]], inc=(c == 1))
                    K.op("act", lambda e: e.activation(out=rq[:, 0:Tw], in_=g.ps[4][:, 0:Tw], func=AF.Ln, scale=1.0 / 256, bias=g.epsc[:, 0:1]),
                         reads=[g.pst[4]], writes=[T_rq])
                    K.op("act", lambda e: e.activation(out=rq[:, 0:Tw], in_=rq[:, 0:Tw], func=AF.Exp, scale=-0.5), reads=[T_rq], writes=[T_rq])
                    for c in range(2):
                        K.op("dve", lambda e: e.scalar_tensor_tensor(out=aqn[:, c, 0:Tw], in0=g.ps[c][:, 0:Tw], scalar=mnorm[:, c:c + 1],
                                                                     in1=rq[:, 0:Tw], op0=ALU.mult, op1=ALU.mult),
                             reads=[g.pst[c], T_w, T_rq], writes=[T_aqn])
                K.op("act", lambda e: e.activation(out=sqa[2][:, 0:Tw], in_=g.ps[2][:, 0:Tw], func=AF.Square),
                     reads=[g.pst[2]], writes=[T_sqa[2]])
                K.op("pe", lambda e: e.matmul(g.ps[5][:, 0:Tw], lhsT=ones[:, :], rhs=sqa[2][:, 0:Tw], start=True, stop=True),
                     reads=[T_w, T_sqa[2]], writes=[g.pst[5]])
                K.op("act", lambda e: e.activation(out=rkv[:, 0:Tw], in_=g.ps[5][:, 0:Tw], func=AF.Ln, scale=1.0 / 128, bias=g.epsc[:, 0:1]),
                     reads=[g.pst[5]], writes=[T_rkv])
                K.op("act", lambda e: e.activation(out=rkv[:, 0:Tw], in_=rkv[:, 0:Tw], func=AF.Exp, scale=-0.5), reads=[T_rkv], writes=[T_rkv])
                K.op("dve", lambda e: e.scalar_tensor_tensor(out=ckvn[:, 0:Tw], in0=g.ps[2][:, 0:Tw], scalar=mnorm[:, 2:3],
                                                             in1=rkv[:, 0:Tw], op0=ALU.mult, op1=ALU.mult),
                     reads=[g.pst[2], T_w, T_rkv], writes=[T_ckvn])
                if getattr(g, "cutpt", 0) == 2:
                    K.barrier()
                    return
                for sub in range(nsub):
                    vb_ = vc % 2
                    vc += 1
                    for nh in range(2):
                        K.op("pe", lambda e: e.matmul(g.ps[6 + nh][:, :], lhsT=ckvn[:, sub * 128:(sub + 1) * 128], rhs=wv[:, nh * 512:(nh + 1) * 512],
                                                      start=True, stop=True), reads=[T_ckvn, T_w], writes=[g.pst[6 + nh]])
                        K.op("act" if nh == 0 else "dve",
                             (lambda e: e.activation(out=vt[vb_][:, nh * 8:(nh + 1) * 8, 0:64],
                                                     in_=g.ps[6 + nh][:, :].rearrange("p (h v) -> p h v", v=64), func=AF.Copy))
                             if nh == 0 else
                             (lambda e: e.tensor_copy(out=vt[vb_][:, nh * 8:(nh + 1) * 8, 0:64],
                                                      in_=g.ps[6 + nh][:, :].rearrange("p (h v) -> p h v", v=64))),
                             reads=[g.pst[6 + nh]], writes=[T_vt[vb_]])
                    K.dma("sp", Vd[(kpos0 // 128) + sub], vt[vb_][:].rearrange("p h v -> p (h v)"), reads=[T_vt[vb_]], writes=[T_Vd])
                if getattr(g, "cutpt", 0) == 3:
                    K.barrier()
                    return
                for h in range(NH):
                    for isk in ([1] if is_ctx else [0, 1]):
                        b = bc % NB
                        bc += 1
                        pq = (4, 5, 0, 1)[bc % 4]
                        pr = (6, 7, 2, 3)[bc % 4]
                        if isk == 0:
                            for c in range(2):
                                K.op("pe", lambda e: e.matmul(g.ps[pq][0:128, 0:Tw], lhsT=wuq[:, c, h * 96:h * 96 + 128], rhs=aqn[:, c, 0:Tw],
                                                              start=(c == 0), stop=(c == 1)),
                                     reads=[T_w, T_aqn], writes=[g.pst[pq]], inc=(c == 1))
                            srcv, T_srcv = g.ps[pq][0:96, 0:Tw], g.pst[pq]
                        else:
                            K.op("pe", lambda e: e.matmul(g.ps[pq][0:64, 0:Tw], lhsT=wk[:, h, :], rhs=ckvn[:, 0:Tw], start=True, stop=True),
                                 reads=[T_w, T_ckvn], writes=[g.pst[pq]])
                            kh, T_kh = khs[kcnt % NKH], T_khs[kcnt % NKH]
                            kcnt += 1
                            K.op("act", lambda e: e.activation(out=kh[0:64, 0:Tw], in_=g.ps[pq][0:64, 0:Tw], func=AF.Copy),
                                 reads=[g.pst[pq]], writes=[T_kh])
                            srcv, T_srcv = kh[:, 0:Tw], T_kh
                        K.op("act", lambda e: e.activation(out=sqh[b][:, 0:Tw], in_=srcv, func=AF.Square), reads=[T_srcv], writes=[T_sqh[b]])
                        K.op("pe", lambda e: e.matmul(g.ps[pr][0:96, 0:Tw], lhsT=ones[0:96, 0:96], rhs=sqh[b][:, 0:Tw], start=True, stop=True),
                             reads=[T_w, T_sqh[b]], writes=[g.pst[pr]])

                        def stage_a2(b=b, pr=pr, Tw=Tw):
                            K.op("act", lambda e: e.activation(out=rst[b][:, 0:Tw], in_=g.ps[pr][0:96, 0:Tw], func=AF.Ln, scale=1.0 / 96,
                                                               bias=g.epsc[0:96, 0:1]), reads=[g.pst[pr]], writes=[T_rst[b]])
                            K.op("act", lambda e: e.activation(out=rst[b][:, 0:Tw], in_=rst[b][:, 0:Tw], func=AF.Exp, scale=-0.5),
                                 reads=[T_rst[b]], writes=[T_rst[b]])

                        def stage_b(b=b, pq=pq, h=h, isk=isk, is_ctx=is_ctx, Tw=Tw, t0=t0, kpos0=kpos0, rb=(0 if is_ctx else rb),
                                    srcv=srcv, T_srcv=T_srcv):
                            dst_t, T_dst_t = (qf[b], T_qf[b]) if is_ctx else (qg[b], T_qg[b])
                            K.op("dve", lambda e: e.scalar_tensor_tensor(out=dst_t[:, 0:Tw], in0=srcv, scalar=mqk[:, isk:isk + 1],
                                                                         in1=rst[b][:, 0:Tw], op0=ALU.mult, op1=ALU.mult),
                                 reads=[T_srcv, T_w, T_rst[b], T_sqh[b]], writes=[T_dst_t])
                            if is_ctx:
                                K.dma("sp", Kd[h][:, kpos0:kpos0 + Tw], qf[b][:, 0:Tw], reads=[T_qf[b]], writes=[T_Kd])
                                return
                            K.op("pe", lambda e: e.matmul(g.ps[pq][0:96, 0:Tw], lhsT=prot[:, :], rhs=qg[b][:, 0:Tw], start=True, stop=True),
                                 reads=[T_w, T_qg[b]], writes=[g.pst[pq]])
                            K.op("pool", lambda e: e.tensor_tensor(out=t1[b][:, 0:Tw], in0=qg[b][:, 0:Tw], in1=rc[rb][:, 0:Tw], op=ALU.mult),
                                 reads=[T_qg[b], T_rope[rb]], writes=[T_t1[b]])
                            K.op("dve", lambda e: e.tensor_tensor(out=t2[b][:, 0:Tw], in0=g.ps[pq][0:96, 0:Tw], in1=rs_[rb][:, 0:Tw], op=ALU.mult),
                                 reads=[g.pst[pq], T_rope[rb]], writes=[T_t2[b]])
                            K.op("pool", lambda e: e.tensor_tensor(out=qf[b][:, 0:Tw], in0=t1[b][:, 0:Tw], in1=t2[b][:, 0:Tw], op=ALU.add),
                                 reads=[T_t1[b], T_t2[b]], writes=[T_qf[b]])
                            if isk == 0:
                                K.dma("sp", Qd[h][:, t0:t0 + Tw], qf[b][:, 0:Tw], reads=[T_qf[b]], writes=[T_Qd])
                            else:
                                K.dma("sp", Kd[h][:, kpos0:kpos0 + Tw], qf[b][:, 0:Tw], reads=[T_qf[b]], writes=[T_Kd])

                        deferred.append([stage_a2, stage_b])
                        if len(deferred) > 1:
                            deferred[-2][0]()
                        if len(deferred) > 2:
                            deferred.pop(0)[1]()
                if deferred:
                    deferred[-1][0]()
                while deferred:
                    deferred.pop(0)[1]()
            K.barrier()
        if getattr(g, "mla_stop", 0) == 1:
            return
        with ExitStack() as ph:
            Qh = [SB(g, ph, tag + "_Qh%d" % i, [96, L], BF16) for i in range(2)]
            Kh = [SB(g, ph, tag + "_Kh%d" % i, [96, NKC * 128], BF16) for i in range(2)]
            Vh = [SB(g, ph, tag + "_Vh%d" % i, [128, NKC, 128], BF16) for i in range(2)]
            T_h = trks(2)
            for i in range(2):
                K.op("pool", lambda e: e.memset(Vh[i][:, :, 64:128], 1.0), writes=[T_h[i]])
            NPB = 3
            pb = [SB(g, ph, tag + "_pb%d" % i, [128, 1024], BF16) for i in range(NPB)]
            T_pb = trks(NPB)
            dn = SB(g, ph, tag + "_dn", [65, 512], F32)
            onesf = SB(g, ph, tag + "_onesf", [65, 64], F32)
            T_dn, T_onesf = Trk(), Trk()
            K.op("pool", lambda e: e.memset(onesf[:], 1.0), writes=[T_onesf])
            obuf = [SB(g, ph, tag + "_ob%d" % i, [64, 512], BF16) for i in range(2)]
            T_obuf = trks(2)
            SCALE = 96.0 ** -0.5
            NP = NKC // 2

            def load_head(h):
                hb_ = h % 2
                K.dma("sp", Qh[hb_][:], Qd[h], reads=[T_Qd], writes=[T_h[hb_]])
                K.dma("sp", Kh[hb_][:], Kd[h], reads=[T_Kd], writes=[T_h[hb_]])
                for c0 in range(0, NKC, 17):
                    K.dma("sp", Vh[hb_][:, c0:c0 + 17, 0:64], Vd[c0:c0 + 17, :, h * 64:(h + 1) * 64].rearrange("c p v -> p c v"),
                          reads=[T_Vd], writes=[T_h[hb_]])

            items = [(h, qt, j) for h in range(NH) for qt in range(L // 512) for j in range(NP)]
            spc = [0]

            def emit_S(it):
                h, qt, j = it
                hb_ = h % 2
                a = spc[0] % 2
                spc[0] += 1
                for u in range(2):
                    kc = 2 * j + u
                    K.op("pe", lambda e: e.matmul(g.ps[2 * a + u][:, :], lhsT=Kh[hb_][:, kc * 128:(kc + 1) * 128],
                                                  rhs=Qh[hb_][:, qt * 512:(qt + 1) * 512], start=True, stop=True),
                         reads=[T_h[hb_]], writes=[g.pst[2 * a + u]])
                return a

            pending = []

            def normalise(h, qt, ob_, o):
                K.op("dve", lambda e: e.reciprocal(out=dn[64:65, :], in_=g.ps[ob_][64:65, :]), reads=[g.pst[ob_]], writes=[T_dn])
                K.op("pe", lambda e: e.matmul(g.ps[6 + o][0:64, :], lhsT=onesf[64:65, 0:64], rhs=dn[64:65, :], start=True, stop=True),
                     reads=[T_dn, T_onesf], writes=[g.pst[6 + o]])
                K.op("dve", lambda e: e.tensor_copy(out=dn[0:64, :], in_=g.ps[6 + o][0:64, :]), reads=[g.pst[6 + o]], writes=[T_dn])
                K.op("dve", lambda e: e.tensor_tensor(out=obuf[o][:], in0=g.ps[ob_][0:64, :], in1=dn[0:64, :], op=ALU.mult),
                     reads=[g.pst[ob_], T_dn], writes=[T_obuf[o]])
                K.dma("sp", Od[h][:, qt * 512:(qt + 1) * 512], obuf[o][:], reads=[T_obuf[o]], writes=[T_Od])

            load_head(0)
            a_next = emit_S(items[0])
            pbc = 0
            oc = 0
            for idx, (h, qt, j) in enumerate(items):
                hb_ = h % 2
                if qt == 0 and j == 0 and h + 1 < NH:
                    load_head(h + 1)
                if j == 0:
                    ob_ = 4 + (oc % 2)
                    o = oc % 2
                    oc += 1
                a = a_next
                p = pbc % NPB
                pbc += 1
                K.op("act", lambda e: e.activation(out=pb[p][:, :], in_=g.psd[a][:, :], func=AF.Exp, scale=SCALE),
                     reads=[g.pst[2 * a], g.pst[2 * a + 1]], writes=[T_pb[p]])
                if idx + 1 < len(items):
                    a_next = emit_S(items[idx + 1])
                for u in range(2):
                    kc = 2 * j + u
                    K.op("pe", lambda e: e.matmul(g.ps[ob_][:, :], lhsT=Vh[hb_][:, kc, :], rhs=pb[p][:, u * 512:(u + 1) * 512],
                                                  start=(kc == 0), stop=(kc == NKC - 1)),
                         reads=[T_h[hb_], T_pb[p]], writes=[g.pst[ob_]], inc=True)
                if j == 3 and pending:
                    normalise(*pending.pop(0))
                if j == NP - 1:
                    pending.append((h, qt, ob_, o))
            while pending:
                normalise(*pending.pop(0))
            K.barrier()
        with ExitStack() as ph:
            wo = SB(g, ph, tag + "_wo", [128, NH // 2, 1024], BF16)
            T_wo = Trk()
            wsrc = d["mwo_bf"].rearrange("p (j two n) -> p j two n", two=2, n=1024)
            for two in range(2):
                K.dma("sp", wo[two * 64:(two + 1) * 64, :, :], wsrc[:, :, two, :], reads=[g.T_wbf], writes=[T_wo])
            g1bc, T_g1 = load_gate_bc(g, ph, tag + "_g1", 1, 0, 0)
            ot = [SB(g, ph, tag + "_ot%d" % i, [128, NH // 2, 512], BF16) for i in range(2)]
            T_ot = trks(2)
            xt = [SB(g, ph, tag + "_xo%d" % i, [128, 1024], F32) for i in range(2)]
            T_xt = trks(2)
            ob = [SB(g, ph, tag + "_oo%d" % i, [128, 1024], F32) for i in range(2)]
            T_ob = trks(2)
            mc = 0
            osrc = Od.rearrange("(j two) p t -> two p j t", two=2)
            for ti in range(L // 512):
                tb_ = ti % 2
                for two in range(2):
                    K.dma("sp", ot[tb_][two * 64:(two + 1) * 64, :, :], osrc[two][:, :, ti * 512:(ti + 1) * 512], reads=[T_Od],
                          writes=[T_ot[tb_]])
                for sub in range(4):
                    m = ti * 4 + sub
                    xb = mc % 2
                    mc += 1
                    K.dma("sp", xt[xb][:], src[m * 128:(m + 1) * 128, :], reads=[T_src], writes=[T_xt[xb]])
                    for nh in range(2):
                        pb_ = (2 * m + nh) % 4
                        for j in range(NH // 2):
                            K.op("pe", lambda e: e.matmul(g.ps[pb_][:, :], lhsT=ot[tb_][:, j, sub * 128:(sub + 1) * 128],
                                                          rhs=wo[:, j, nh * 512:(nh + 1) * 512], start=(j == 0), stop=(j == NH // 2 - 1)),
                                 reads=[T_ot[tb_], T_wo], writes=[g.pst[pb_]], inc=(j == NH // 2 - 1))
                        K.op("dve", lambda e: e.tensor_tensor(out=ob[xb][:, nh * 512:(nh + 1) * 512], in0=g.ps[pb_][:, :],
                                                              in1=g1bc[:, nh * 512:(nh + 1) * 512], op=ALU.mult),
                             reads=[g.pst[pb_], T_g1], writes=[T_ob[xb]])
                    K.op("pool", lambda e: e.tensor_tensor(out=ob[xb][:], in0=ob[xb][:], in1=xt[xb][:], op=ALU.add),
                         reads=[T_ob[xb], T_xt[xb]], writes=[T_ob[xb]])
                    K.dma("sp", dst[m * 128:(m + 1) * 128, :], ob[xb][:], reads=[T_ob[xb]], writes=[T_dst])
            K.barrier()


def declare_inputs(g, nc, shapes):
    g.d = {}
    for name, (shape, dt) in shapes.items():
        g.d[name] = nc.dram_tensor(name, list(shape), dt, kind="ExternalInput").ap()


def in_shapes():
    return {
        "x": ((LX, D), F32), "ctx": ((LCX, D), F32), "cc": ((128, 8, 2), F32),
        "w_mod": ((2, 1024, 6144), F32), "b_mod": ((2, 1, 6144), F32),
        "normT": ((128, 2, 2, 8), F32), "ident": ((128, 128), F32),
        "wup": ((2, 44, 128, 1024), F32), "wdown": ((2, 22, 128, 1024), F32),
        "fconv": ((128, 2, 22, 4), F32),
        "hydelta": ((1, 512), F32), "alt": ((128, 1), BF16), "cs128": ((128, 256), BF16),
        "hy_w1": ((33, 64), F32), "hy_w2": ((64, 64), F32), "hy_w3": ((64, 64), F32), "hy_w4": ((64, 1024), F32),
        "hyp": ((64, 4), F32), "hyconv": ((128, 12, 4), F32), "hybias": ((128, 4), F32),
        "fhwin": ((16, 128, 1024), F32), "fhwout": ((8, 128, 1024), F32),
        "mwin": ((128, 8 * 416), F32), "mwuq": ((128, 2 * 1536), F32), "mwk": ((128, 1024), F32), "mwv": ((128, 1024), F32),
        "mwo": ((64, 16 * 1024), F32), "mnorm": ((128, 4), F32), "mqk": ((96, 2), F32),
        "esel": ((32, 96), F32), "prot": ((96, 96), BF16), "ones128": ((128, 128), BF16), "ropeC": ((96, LX), F32), "ropeS": ((96, LX), F32),
        **{k + "_%d" % L: v for L in (LX, LCX) for k, v in {
            "hyz": ((33, L), F32), "hynegt": ((128, L // 128), F32), "altrow": ((1, L), BF16),
            "CT": ((L, L), BF16), "ST": ((L, L), BF16), "C4": ((L, L // 2), BF16), "S4n": ((L, L // 2), BF16)}.items()},
    }


def build(mode="full"):
    nc = bass.Bass("TRN2", target_bir_lowering=False)
    g = G()
    g.nc = nc
    declare_inputs(g, nc, in_shapes())
    d = g.d
    d["out"] = nc.dram_tensor("out", [LX, D], F32, kind="ExternalOutput").ap()
    d["modscr"] = nc.dram_tensor("modscr", [2, 2, 6144], F32).ap()
    d["wup_bf"] = nc.dram_tensor("wup_bf", [2, 44, 128, 1024], BF16).ap()
    d["wdown_bf"] = nc.dram_tensor("wdown_bf", [2, 22, 128, 1024], BF16).ap()
    d["xa"] = nc.dram_tensor("xa", [LX, D], F32).ap()
    d["ca"] = nc.dram_tensor("ca", [LCX, D], F32).ap()
    d["cb"] = nc.dram_tensor("cb", [LCX, D], F32).ap()
    d["fhwin_bf"] = nc.dram_tensor("fhwin_bf", [16, 128, 1024], BF16).ap()
    d["fhwout_bf"] = nc.dram_tensor("fhwout_bf", [8, 128, 1024], BF16).ap()
    for L in (LX, LCX):
        sfx = "_%d" % L
        kw = {"kind": "ExternalOutput"} if mode.endswith("dbg") else {}
        d["Kre" + sfx] = nc.dram_tensor("Kre" + sfx, [4, 128, L + 1], F32, **kw).ap()
        d["Ksn" + sfx] = nc.dram_tensor("Ksn" + sfx, [4, 128, L + 1], F32, **kw).ap()
        d["ABd" + sfx] = nc.dram_tensor("ABd" + sfx, [L, 4, 256], BF16, **kw).ap()
        d["X0d" + sfx] = nc.dram_tensor("X0d" + sfx, [4, 128, L], BF16, **kw).ap()
        d["Zd" + sfx] = nc.dram_tensor("Zd" + sfx, [4, 128, L], BF16, **kw).ap()
        d["YTd" + sfx] = nc.dram_tensor("YTd" + sfx, [2, L // 128, 128, 512], BF16, **kw).ap()
        d["YTn" + sfx] = nc.dram_tensor("YTn" + sfx, [1, 512], BF16, **kw).ap()
    d["Qd"] = nc.dram_tensor("Qd", [16, 96, LX], BF16).ap()
    d["Kd"] = nc.dram_tensor("Kd", [16, 96, LX + LCX], BF16).ap()
    d["Vd"] = nc.dram_tensor("Vd", [(LX + LCX) // 128, 128, 16 * 64], BF16).ap()
    d["Od"] = nc.dram_tensor("Od", [16, 64, LX], BF16).ap()
    d["xb"] = nc.dram_tensor("xb", [LX, D], F32).ap()
    d["xc"] = nc.dram_tensor("xc", [LX, D], F32).ap()
    g.castlist = []
    for nm, shp in (("mwin", [128, 8 * 416]), ("mwuq", [128, 2 * 1536]), ("mwk", [128, 1024]), ("mwv", [128, 1024]), ("mwo", [64, 16 * 1024])):
        d[nm + "_bf"] = nc.dram_tensor(nm + "_bf", shp, BF16).ap()
        g.castlist.append((d[nm + "_bf"], d[nm]))
    g.castlist.append((d["fhwin_bf"][0:8], d["fhwin"][0:8]))
    g.castlist.append((d["fhwin_bf"][8:16], d["fhwin"][8:16]))
    g.castlist.append((d["fhwout_bf"], d["fhwout"]))
    for i in range(2):
        for j in range(0, 44, 11):
            g.castlist.append((d["wup_bf"][i, j:j + 11], d["wup"][i, j:j + 11]))
        for j in range(0, 22, 11):
            g.castlist.append((d["wdown_bf"][i, j:j + 11], d["wdown"][i, j:j + 11]))
    with ExitStack() as es:
        g.es = es
        g.K = KB(nc, es)
        g.psd = [es.enter_context(nc.psum_tensor("psd%d" % i, [128, 1024], F32)) for i in range(4)]
        g.ps = [g.psd[i // 2][:, (i % 2) * 512:(i % 2 + 1) * 512] for i in range(8)]
        g.pst = trks(8)
        phase0(g)
        if mode != "full":
            for _ in phase0_mods(g):
                pass
            g.K.barrier()
        T_x, T_ctx, T_out, T_ca = Trk(), Trk(), DTrk(), DTrk()
        T_xa, T_xb, T_xc, T_cb = DTrk(), DTrk(), DTrk(), DTrk()
        if mode == "ffn_ctx":
            ffn_phase(g, 0, 1, LCX, d["ctx"], T_ctx, d["out"], T_out, "fc")
        elif mode == "mix_ctx":
            for _ in mixer0(g, 1, LCX, d["ctx"], T_ctx, d["out"], T_out, "mc"):
                pass
        elif mode.startswith("mla"):
            g.mla_stop = int(mode[3:4]) if len(mode) > 3 else 0
            g.cutpt = int(mode[5:]) if len(mode) > 5 else 0
            try:
                mla(g, d["x"], T_x, d["ctx"], T_ctx, d["out"], T_out)
            except StopBuild:
                pass
        elif mode == "full":
            mxg = mixer0(g, 0, LX, d["x"], T_x, d["xa"], T_xa, "mx")
            next(mxg)
            p0 = phase0_mods(g)
            p0_done = False
            for _ in mxg:
                if not p0_done:
                    try:
                        next(p0)
                    except StopIteration:
                        p0_done = True
            assert p0_done
            for _ in mixer0(g, 1, LCX, d["ctx"], T_ctx, d["ca"], T_ca, "mc"):
                pass
            ffn_phase(g, 0, 0, LX, d["xa"], T_xa, d["xb"], T_xb, "f0x")
            ffn_phase(g, 0, 1, LCX, d["ca"], T_ca, d["cb"], T_cb, "f0c")
            mla(g, d["xb"], T_xb, d["cb"], T_cb, d["xc"], T_xc)
            ffn_phase(g, 1, 0, LX, d["xc"], T_xc, d["out"], T_out, "f1x")
        elif mode == "mix_x_dbg":
            for _ in mixer0(g, 0, LX, d["x"], T_x, d["out"], T_out, "mx"):
                pass
        elif mode == "mix_x":
            for _ in mixer0(g, 0, LX, d["x"], T_x, d["out"], T_out, "mx"):
                pass
        elif mode == "ffn_x":
            ffn_phase(g, 1, 0, LX, d["x"], T_x, d["out"], T_out, "fx")
        g.K.barrier()
        g.K.final_wait("sp")
    return nc


def prep_common(inp):
    d = {}
    f = lambda a: np.ascontiguousarray(np.asarray(a, dtype=np.float32))
    d["w_mod"] = f(inp["w_mod"])
    d["b_mod"] = f(inp["b_mod"]).reshape(2, 1, 6144)
    n1 = f(inp["norm1"]).reshape(2, 8, 128)
    n2 = f(inp["norm2"]).reshape(2, 8, 128)
    d["normT"] = np.ascontiguousarray(np.stack([n1, n2], axis=1).transpose(3, 0, 1, 2))
    d["ident"] = np.eye(128, dtype=np.float32)
    wu = f(inp["ffn_w_up"]).reshape(2, 8, 128, 44, 128).transpose(0, 3, 2, 1, 4)
    order = [j for c in range(22) for j in (c, 22 + c)]
    d["wup"] = np.ascontiguousarray(wu[:, order]).reshape(2, 44, 128, 1024)
    d["wdown"] = f(inp["ffn_w_down"]).reshape(2, 22, 128, 1024)
    cw = f(inp["ffn_conv_w"]).reshape(2, 3, 22, 128)
    cb = f(inp["ffn_conv_b"]).reshape(2, 1, 22, 128)
    d["fconv"] = np.ascontiguousarray(np.concatenate([cw, cb], axis=1).transpose(3, 0, 2, 1))
    wi = f(inp["fh_w_in"][0]).reshape(8, 128, 16, 128).transpose(2, 1, 0, 3)
    d["fhwin"] = np.ascontiguousarray(wi).reshape(16, 128, 1024)
    d["fhwout"] = f(inp["fh_w_out"][0]).reshape(8, 128, 1024)
    hw = f(inp["hy_conv_w"][0]).reshape(3, 12, 128)
    hb = f(inp["hy_conv_b"][0]).reshape(1, 12, 128)
    d["hyconv"] = np.ascontiguousarray(np.concatenate([hw, hb], axis=0).transpose(2, 1, 0))
    d["hy_w1"] = f(inp["hy_filt_w1"][0]); d["hy_w2"] = f(inp["hy_filt_w2"][0])
    d["hy_w3"] = f(inp["hy_filt_w3"][0]); d["hy_w4"] = f(inp["hy_filt_w4"][0])
    d["hyp"] = np.ascontiguousarray(np.stack([f(inp["hy_freq"][0]), f(inp["hy_filt_b1"][0]), f(inp["hy_filt_b2"][0]),
                                              f(inp["hy_filt_b3"][0])], axis=1))
    d["hybias"] = np.ascontiguousarray(f(inp["hy_bias"][0]).reshape(4, 128).T)
    d["mwin"] = np.ascontiguousarray(f(inp["mla_w_in"][0]).reshape(8, 128, 416).transpose(1, 0, 2)).reshape(128, 8 * 416)
    d["mwuq"] = np.ascontiguousarray(f(inp["mla_w_uq"][0]).reshape(2, 128, 1536).transpose(1, 0, 2)).reshape(128, 2 * 1536)
    wkv = f(inp["mla_w_ukv"][0]).reshape(128, 16, 128)
    d["mwk"] = np.ascontiguousarray(wkv[:, :, :64]).reshape(128, 1024)
    d["mwv"] = np.ascontiguousarray(wkv[:, :, 64:]).reshape(128, 1024)
    d["mwo"] = np.ascontiguousarray(f(inp["mla_w_o"][0]).reshape(16, 64, 1024).transpose(1, 0, 2)).reshape(64, 16 * 1024)
    qa = f(inp["mla_q_a_norm"][0]).reshape(2, 128)
    d["mnorm"] = np.ascontiguousarray(np.stack([qa[0], qa[1], f(inp["mla_kv_a_norm"][0]), np.zeros(128, np.float32)], axis=1))
    d["mqk"] = np.ascontiguousarray(np.stack([f(inp["mla_q_norm"][0]), f(inp["mla_k_norm"][0])], axis=1))
    d.update(const_tables())
    return d


_TABLES = {}


def const_tables():
    if _TABLES:
        return _TABLES
    bf = ml_dtypes.bfloat16
    t = {}
    import math
    min_decay = math.log(1e-2) / 0.3
    max_decay = math.log(1e-2) / 1.5
    deltas = np.linspace(min_decay, max_decay, 512, dtype=np.float32)
    t["hydelta"] = np.abs(deltas).reshape(1, 512).astype(np.float32)
    t["alt"] = ((-1.0) ** np.arange(128)).reshape(128, 1).astype(bf)
    dk = (np.arange(128)[:, None] * np.arange(128)[None, :]) % 128
    ang = 2 * np.pi * dk / 128.0
    t["cs128"] = np.concatenate([np.cos(ang), np.sin(ang)], axis=1).astype(bf)
    prot = np.zeros((96, 96), np.float32)
    for base in (64, 80):
        for i in range(8):
            prot[base + 8 + i, base + i] = -1.0
            prot[base + i, base + 8 + i] = 1.0
    t["prot"] = prot.astype(bf)
    esel = np.zeros((32, 96), np.float32)
    esel[np.arange(32), 64 + np.arange(32)] = 1.0
    t["esel"] = esel
    t["ones128"] = np.ones((128, 128), np.float32).astype(bf)
    posi = np.arange(LX)
    rows = (posi // 64).astype(np.float32)
    colsv = (posi % 64).astype(np.float32)
    inv_freq = (np.float32(10000.0) ** (-np.arange(8, dtype=np.float32) / np.float32(8))).astype(np.float32)
    ang_r = rows[:, None] * inv_freq[None, :]
    ang_c = colsv[:, None] * inv_freq[None, :]
    rc = np.ones((96, LX), np.float32)
    rs = np.zeros((96, LX), np.float32)
    rc[64:72] = np.cos(ang_r).T; rc[72:80] = np.cos(ang_r).T; rc[80:88] = np.cos(ang_c).T; rc[88:96] = np.cos(ang_c).T
    rs[64:72] = np.sin(ang_r).T; rs[72:80] = np.sin(ang_r).T; rs[80:88] = np.sin(ang_c).T; rs[88:96] = np.sin(ang_c).T
    t["ropeC"] = rc
    t["ropeS"] = rs
    for L in (LX, LCX):
        sfx = "_%d" % L
        pos = np.arange(L, dtype=np.float32)
        tt = pos / max(L - 1, 1)
        bands = np.linspace(1e-4, 15, 16, dtype=np.float32)
        a = (np.float32(2.0 * math.pi / L) * pos[:, None] * bands[None, :]).astype(np.float32)
        z = np.concatenate([tt[:, None], np.cos(a), -np.sin(a)], axis=-1).astype(np.float32)
        t["hyz" + sfx] = np.ascontiguousarray(z.T)
        t["hynegt" + sfx] = np.ascontiguousarray((-tt).reshape(L // 128, 128).T).astype(np.float32)
        t["altrow" + sfx] = ((-1.0) ** np.arange(L)).reshape(1, L).astype(bf)
        idx = np.arange(L, dtype=np.int64)
        m8 = (idx[:, None] * idx[None, :]) % (2 * L)
        a8 = (2 * np.pi / (2 * L)) * m8
        t["CT" + sfx] = np.cos(a8).astype(bf)
        t["ST" + sfx] = np.sin(a8).astype(bf)
        m4 = (idx[:, None] * idx[None, :L // 2]) % L
        a4 = (2 * np.pi / L) * m4
        t["C4" + sfx] = np.cos(a4).astype(bf)
        t["S4n" + sfx] = (-np.sin(a4)).astype(bf)
    _TABLES.update(t)
    return _TABLES


def prep_core(inp, b):
    d = {}
    d["x"] = np.ascontiguousarray(np.asarray(inp["x"][b], dtype=np.float32))
    d["ctx"] = np.ascontiguousarray(np.asarray(inp["ctx"][b], dtype=np.float32))
    c = np.asarray(inp["c"][b], dtype=np.float32).reshape(8, 128)
    cx = np.asarray(inp["c_ctx"], dtype=np.float32).reshape(8, 128)
    d["cc"] = np.ascontiguousarray(np.stack([c, cx], axis=-1).transpose(1, 0, 2))
    return d


def kernel(**inputs):
    common = prep_common(inputs)
    nc = build("full")
    in_maps = []
    for b in range(NCORES):
        m = dict(common)
        m.update(prep_core(inputs, b))
        in_maps.append(m)
    res = run_bass_kernel_spmd(nc, in_maps, core_ids=list(range(NCORES)))
    return np.stack([np.asarray(r["out"]) for r in res.results], axis=0).astype(np.float32)
```

```python
import numpy as np
import ml_dtypes
from contextlib import ExitStack
import concourse.bass as bass
import concourse.mybir as mybir
from concourse.bass_utils import run_bass_kernel_spmd

F32 = mybir.dt.float32
BF16 = mybir.dt.bfloat16
I32 = mybir.dt.int32
AF = mybir.ActivationFunctionType
ALU = mybir.AluOpType
AX = mybir.AxisListType

import os as _os
SAME_ENGINE_SYNC = _os.environ.get("SES", "1") == "1"
N_DMA_SEM = 8
NCORES = 8
LX = 4096
LCX = 256
D = 1024
DFF = 2816
EPS = 1e-6
PI = float(np.pi)


class Trk:
    __slots__ = ("w", "r")

    def __init__(self):
        self.w = None
        self.r = {}


def trks(n):
    return [Trk() for _ in range(n)]


class DTrk(Trk):
    __slots__ = ()
    nowaw = True


class KB:
    def __init__(self, nc, es):
        self.nc = nc
        self.engs = {"pe": nc.tensor, "act": nc.scalar, "dve": nc.vector,
                     "pool": nc.gpsimd, "sp": nc.sync}
        self.sem = {}
        self.cnt = {}
        self.seen = {e: {} for e in self.engs}
        self.semobj = {}
        for e in ("pe", "act", "dve", "pool"):
            s = es.enter_context(nc.semaphore("s_" + e))
            self.sem[e] = s
            self.cnt[e] = 0
            self.semobj[id(s)] = s
        self.dsem = {}
        self.dcnt = {}
        for q in ("sp", "pool"):
            self.dsem[q] = []
            for i in range(N_DMA_SEM):
                s = es.enter_context(nc.semaphore("d_%s_%d" % (q, i)))
                self.dsem[q].append(s)
                self.semobj[id(s)] = s
            self.dcnt[q] = 0
        self.dlast = {}

    def _wait(self, e, s, v):
        if self.seen[e].get(id(s), 0) >= v:
            return
        self.engs[e].wait_ge(s, v)
        self.seen[e][id(s)] = v

    def deps(self, e, reads, writes):
        need = {}
        for t in reads:
            if t.w is not None:
                s, v = t.w
                need[id(s)] = max(need.get(id(s), 0), v)
        for t in writes:
            if t.w is not None and not getattr(t, "nowaw", False):
                s, v = t.w
                need[id(s)] = max(need.get(id(s), 0), v)
            for sid, v in t.r.items():
                need[sid] = max(need.get(sid, 0), v)
        own = id(self.sem[e]) if e in self.sem else None
        for sid, v in need.items():
            if sid == own and (e == "pe" or not SAME_ENGINE_SYNC):
                continue
            self._wait(e, self.semobj[sid], v)

    def done(self, ev, reads, writes):
        s, v = ev
        for t in reads:
            t.r[id(s)] = max(t.r.get(id(s), 0), v)
        for t in writes:
            t.w = ev
            t.r = {}

    def op(self, e, fn, reads=(), writes=(), inc=True):
        self.deps(e, reads, writes)
        ins = fn(self.engs[e])
        if inc:
            self.cnt[e] += 1
            ins.then_inc(self.sem[e], 1)
            ev = (self.sem[e], self.cnt[e])
        else:
            ev = (self.sem[e], self.cnt[e] + 1)
        self.done(ev, reads, writes)
        return ins

    def dma(self, q, out, in_, reads=(), writes=(), **kw):
        n = self.dcnt[q]
        s = self.dsem[q][n % N_DMA_SEM]
        k = n // N_DMA_SEM
        if k > 0:
            self._wait(q, s, 16 * k)
        self.deps(q, reads, writes)
        self.engs[q].dma_start(out=out, in_=in_, **kw).then_inc(s, 16)
        self.dcnt[q] = n + 1
        ev = (s, 16 * (k + 1))
        self.dlast[id(s)] = 16 * (k + 1)
        self.done(ev, reads, writes)

    def barrier(self, engines=("pe", "act", "dve", "pool", "sp")):
        for e in engines:
            for e2 in ("pe", "act", "dve", "pool"):
                if self.cnt[e2] > 0 and e != e2:
                    self._wait(e, self.sem[e2], self.cnt[e2])
            for sid, v in self.dlast.items():
                self._wait(e, self.semobj[sid], v)

    def final_wait(self, e="sp"):
        for sid, v in self.dlast.items():
            self._wait(e, self.semobj[sid], v)


class G:
    pass


class StopBuild(Exception):
    pass


def cut(g, n):
    if getattr(g, "cutpt", 0) == n:
        raise StopBuild()


def SB(g, es, name, shape, dt):
    return es.enter_context(g.nc.sbuf_tensor("sb_" + name, shape, dt))


def phase0(g):
    K, nc = g.K, g.nc
    es = g.es
    g.ident = SB(g, es, "ident", [128, 128], F32)
    g.T_ident = Trk()
    K.dma("sp", g.ident[:], g.d["ident"], writes=[g.T_ident])
    g.normT = SB(g, es, "normT", [128, 2, 2, 8], F32)
    g.T_const = Trk()
    K.dma("sp", g.normT[:], g.d["normT"], writes=[g.T_const])
    g.fconv = SB(g, es, "fconv", [128, 2, 22, 4], F32)
    K.dma("sp", g.fconv[:], g.d["fconv"], writes=[g.T_const])
    g.modT = SB(g, es, "modT", [128, 2, 48, 2], F32)
    g.T_modT = Trk()
    g.AB = SB(g, es, "ABmod", [128, 2, 2, 2, 8], F32)
    g.T_AB = Trk()
    g.T_modscr = trks(2)
    g.epsc = SB(g, es, "epsc", [128, 1], F32)
    K.op("pool", lambda e: e.memset(g.epsc[:], EPS))

    g.T_wbf = Trk()
    for (dst, src) in g.castlist:
        K.dma("pool", dst, src, writes=[g.T_wbf])


def phase0_mods(g):
    K, nc = g.K, g.nc
    with ExitStack() as ph:
        cct = SB(g, ph, "cct", [128, 8, 2], F32)
        s = SB(g, ph, "silu_c", [128, 8, 2], F32)
        T_cc, T_s = Trk(), Trk()
        K.dma("sp", cct[:], g.d["cc"], writes=[T_cc])
        K.op("act", lambda e: e.activation(out=s[:], in_=cct[:], func=AF.Silu), reads=[T_cc], writes=[T_s])
        wmb = [SB(g, ph, "wmb%d" % i, [128, 8, 512], F32) for i in range(2)]
        T_wmb = trks(2)
        modrow = SB(g, ph, "modrow", [2, 6144], F32)
        bmt = [SB(g, ph, "bmt%d" % i, [2, 512], F32) for i in range(2)]
        T_modrow = Trk()
        T_bmt = trks(2)
        jj = 0
        for i in range(2):
            wsrc = g.d["w_mod"][i].rearrange("(kc p) n -> p kc n", p=128)
            for j in range(12):
                b = jj % 2
                pbk = 5 + b
                jj += 1
                K.dma("sp", wmb[b][:], wsrc[:, :, j * 512:(j + 1) * 512], writes=[T_wmb[b]])
                K.dma("sp", bmt[b][:], g.d["b_mod"][i][:, j * 512:(j + 1) * 512].broadcast_to([2, 512]), writes=[T_bmt[b]])
                for kc in range(8):
                    K.op("pe", lambda e: e.matmul(g.ps[pbk][0:2, :], lhsT=s[:, kc, :], rhs=wmb[b][:, kc, :],
                                                  start=(kc == 0), stop=(kc == 7)),
                         reads=[T_s, T_wmb[b]], writes=[g.pst[pbk]], inc=(kc == 7))
                K.op("dve", lambda e: e.tensor_tensor(out=modrow[:, j * 512:(j + 1) * 512], in0=g.ps[pbk][0:2, :],
                                                      in1=bmt[b][:], op=ALU.add),
                     reads=[g.pst[pbk], T_bmt[b]], writes=[T_modrow])
                yield
            K.dma("sp", g.d["modscr"][i], modrow[:], reads=[T_modrow], writes=[g.T_modscr[i]])
            for c in range(48):
                K.op("pe", lambda e: e.transpose(g.ps[7][:, 2 * c:2 * c + 2], modrow[:, c * 128:(c + 1) * 128],
                                                 g.ident[0:2, 0:2]),
                     reads=[T_modrow, g.T_ident], writes=[g.pst[7]], inc=(c == 47))
            K.op("dve", lambda e: e.tensor_copy(out=g.modT[:, i, :, :], in_=g.ps[7][:, 0:96].rearrange("p (c r) -> p c r", r=2)),
                 reads=[g.pst[7]], writes=[g.T_modT])
            for r in range(2):
                for w in range(2):
                    sc_chunk = 8 if w == 0 else 32
                    K.op("dve", lambda e: e.scalar_tensor_tensor(out=g.AB[:, i, r, w, :], in0=g.modT[:, i, sc_chunk:sc_chunk + 8, r],
                                                                 scalar=1.0, in1=g.normT[:, i, w, :],
                                                                 op0=ALU.add, op1=ALU.mult),
                         reads=[g.T_modT, g.T_const], writes=[g.T_AB])
            yield


def mod_scalars(g, layer, stream, which):
    sh_chunk = 0 if which == 0 else 24

    def A(kc):
        return g.AB[:, layer, stream, which, kc:kc + 1]

    def B(kc):
        return g.modT[:, layer, sh_chunk + kc, stream:stream + 1]
    return A, B


def load_gate_bc(g, es, name, layer, stream, which):
    K = g.K
    t = SB(g, es, name, [128, 1024], F32)
    T = Trk()
    c0 = 2048 if which == 0 else 5120
    K.dma("sp", t[:], g.d["modscr"][layer][stream:stream + 1, c0:c0 + 1024].broadcast_to([128, 1024]),
          reads=[g.T_modscr[layer]], writes=[T])
    return t, T


class NormBufs:
    def __init__(self, g, es, tag, nsub):
        self.nsub = nsub
        self.xs = SB(g, es, tag + "_xs", [128, nsub, 1024], F32)
        self.T_xs = Trk()
        self.junk = SB(g, es, tag + "_junk", [128, 1024], BF16)
        self.T_junk = Trk()
        self.ss = SB(g, es, tag + "_ss", [128, nsub], F32)
        self.rs = SB(g, es, tag + "_rs", [128, nsub], F32)
        self.T_ss = Trk()
        self.T_rs = Trk()


def norm_A(g, nb, xt, T_xt, nrows, nsub):
    K = g.K
    R = nrows
    for s in range(nsub):
        K.op("act", lambda e: e.activation(out=nb.junk[0:R, :], in_=xt[0:R, s, :], func=AF.Square,
                                           accum_out=nb.ss[0:R, s:s + 1]),
             reads=[T_xt], writes=[nb.T_junk, nb.T_ss])
    K.op("dve", lambda e: e.tensor_scalar(out=nb.rs[0:R, 0:nsub], in0=nb.ss[0:R, 0:nsub], scalar1=1.0 / D, scalar2=EPS,
                                          op0=ALU.mult, op1=ALU.add), reads=[nb.T_ss], writes=[nb.T_rs])
    K.op("act", lambda e: e.activation(out=nb.rs[0:R, 0:nsub], in_=nb.rs[0:R, 0:nsub], func=AF.Sqrt),
         reads=[nb.T_rs], writes=[nb.T_rs])
    K.op("dve", lambda e: e.reciprocal(out=nb.rs[0:R, 0:nsub], in_=nb.rs[0:R, 0:nsub]), reads=[nb.T_rs], writes=[nb.T_rs])
    for s in range(nsub):
        if s % 2 == 0:
            K.op("dve", lambda e: e.tensor_scalar(out=nb.xs[0:R, s, :], in0=xt[0:R, s, :], scalar1=nb.rs[0:R, s:s + 1],
                                                  scalar2=None, op0=ALU.mult),
                 reads=[T_xt, nb.T_rs], writes=[nb.T_xs])
        else:
            K.op("act", lambda e: e.activation(out=nb.xs[0:R, s, :], in_=xt[0:R, s, :], func=AF.Copy, scale=nb.rs[0:R, s:s + 1]),
                 reads=[T_xt, nb.T_rs], writes=[nb.T_xs])


def norm_B(g, nb, nrows, nsub, A, B, out_fn, T_out, psb=(0, 1), kcs=range(8), eng_force=None):
    K = g.K
    R = nrows
    for kc in kcs:
        b = psb[kc % len(psb)]
        for s in range(nsub):
            K.op("pe", lambda e: e.transpose(g.ps[b][:, s * R:(s + 1) * R], nb.xs[0:R, s, kc * 128:(kc + 1) * 128],
                                             g.ident[0:R, 0:R]),
                 reads=[nb.T_xs, g.T_ident], writes=[g.pst[b]], inc=(s == nsub - 1))
        eng = eng_force or ("act" if b % 2 == 0 else "dve")
        if eng == "act":
            K.op("act", lambda e: e.activation(out=out_fn(kc), in_=g.ps[b][:, 0:nsub * R], func=AF.Identity,
                                               scale=A(kc), bias=B(kc)),
                 reads=[g.pst[b], g.T_AB, g.T_modT], writes=[T_out])
        else:
            K.op("dve", lambda e: e.tensor_scalar(out=out_fn(kc), in0=g.ps[b][:, 0:nsub * R], scalar1=A(kc), scalar2=B(kc),
                                                  op0=ALU.mult, op1=ALU.add),
                 reads=[g.pst[b], g.T_AB, g.T_modT], writes=[T_out])


def norm_T(g, nb, xt, T_xt, nrows, nsub, A, B, out_fn, T_out, psb=(0, 1)):
    norm_A(g, nb, xt, T_xt, nrows, nsub)
    norm_B(g, nb, nrows, nsub, A, B, out_fn, T_out, psb)


def ffn_phase(g, layer, stream, L, src, T_src, dst, T_dst, tag):
    K, nc = g.K, g.nc
    T = min(512, L)
    NT = L // T
    nsub = T // 128
    A, B = mod_scalars(g, layer, stream, 1)
    wup = g.d["wup_bf"][layer]
    wdn = g.d["wdown_bf"][layer]
    with ExitStack() as ph:
        g2bc, T_g2 = load_gate_bc(g, ph, tag + "_g2", layer, stream, 1)
        wd = SB(g, ph, tag + "_wd", [128, 22, 1024], BF16)
        T_wd = Trk()
        K.dma("sp", wd[:], wdn.rearrange("k p n -> p k n"), reads=[g.T_wbf], writes=[T_wd])
        NWB = 4
        wu = [SB(g, ph, tag + "_wu%d" % i, [128, 2, 8, 128], BF16) for i in range(NWB)]
        T_wu = trks(NWB)
        xt = [SB(g, ph, tag + "_xt%d" % i, [128, nsub, 1024], F32) for i in range(2)]
        T_xt = trks(2)
        hr = SB(g, ph, tag + "_hr", [2, 1, 1024], F32)
        T_hr = Trk()
        nb = NormBufs(g, ph, tag + "_nb", nsub)
        nbh = NormBufs(g, ph, tag + "_nbh", 1)
        hT = SB(g, ph, tag + "_hT", [128, 8, T + 2], BF16)
        T_hT = Trk()
        actb = SB(g, ph, tag + "_act", [128, 22, T], BF16)
        T_act = Trk()
        Gs = [SB(g, ph, tag + "_G%d" % i, [128, T + 2], F32) for i in range(2)]
        T_G = trks(2)
        tb = [SB(g, ph, tag + "_tb%d" % i, [128, T], F32) for i in range(2)]
        T_tb = trks(2)
        vb = [SB(g, ph, tag + "_vb%d" % i, [128, T], F32) for i in range(2)]
        T_vb = trks(2)
        ob = [SB(g, ph, tag + "_ob%d" % i, [128, 512], F32) for i in range(2)]
        T_ob = trks(2)
        hps = g.ps[2]
        T_hps = g.pst[2]
        wcount = 0
        oc = 0
        hTs = [hT, SB(g, ph, tag + "_hTb", [128, 8, T + 2], BF16)]
        T_hTs = [T_hT, Trk()]

        def prep_A(ti):
            t0 = ti * T
            xb = ti % 2
            K.dma("sp", xt[xb][:], src[t0:t0 + T].rearrange("(s p) d -> p s d", p=128), reads=[T_src], writes=[T_xt[xb]])
            K.op("pool", lambda e: e.memset(hr[:], 0.0), writes=[T_hr])
            if t0 > 0:
                K.dma("sp", hr[0:1, 0, :], src[t0 - 1:t0, :], reads=[T_src], writes=[T_hr])
            if t0 + T < L:
                K.dma("sp", hr[1:2, 0, :], src[t0 + T:t0 + T + 1, :], reads=[T_src], writes=[T_hr])
            norm_A(g, nb, xt[xb], T_xt[xb], 128, nsub)
            norm_A(g, nbh, hr, T_hr, 2, 1)

        def prep_B(ti, kcs=range(8)):
            t0 = ti * T
            hTc, T_hTc = hTs[ti % 2], T_hTs[ti % 2]
            norm_B(g, nb, 128, nsub, A, B, lambda kc: hTc[:, kc, 1:T + 1], T_hTc, psb=(0,), kcs=kcs)
            norm_B(g, nbh, 2, 1, A, B, lambda kc: hTc[:, kc, 0:T + 2:T + 1], T_hTc, psb=(2,), kcs=kcs, eng_force="dve")
            if 7 not in kcs:
                return
            if t0 == 0:
                K.op("pool", lambda e: e.memset(hTc[:, :, 0:1], 0.0), writes=[T_hTc])
            if t0 + T >= L:
                K.op("pool", lambda e: e.memset(hTc[:, :, T + 1:T + 2], 0.0), writes=[T_hTc])

        wissued = [0]
        ftail = []
        prep_A(0)
        prep_B(0)
        for ti in range(NT):
            t0 = ti * T
            xb = ti % 2
            hT, T_hT = hTs[ti % 2], T_hTs[ti % 2]
            for c in range(22):
                if ti + 1 < NT and c == 3:
                    prep_A(ti + 1)
                if ti + 1 < NT and 10 <= c < 18:
                    prep_B(ti + 1, kcs=[c - 10])
                wb = wcount % NWB
                wcount += 1
                while wissued[0] < min(NT * 22, wcount + NWB - 1):
                    wi_ = wissued[0]
                    wissued[0] += 1
                    cc_ = wi_ % 22
                    K.dma("sp", wu[wi_ % NWB][:].rearrange("p a k n -> p a (k n)"),
                          wup[2 * cc_:2 * cc_ + 2].rearrange("a p n -> p a n"), reads=[g.T_wbf], writes=[T_wu[wi_ % NWB]])
                pg, pv = g.ps[3 + c % 2], g.ps[5 + c % 2]
                Tpg, Tpv = g.pst[3 + c % 2], g.pst[5 + c % 2]
                hcol = 32 + 2 * (c % 2)
                for kc in range(8):
                    K.op("pe", lambda e: e.matmul(pg[:, 0:T], lhsT=wu[wb][:, 0, kc, :], rhs=hT[:, kc, 1:T + 1],
                                                  start=(kc == 0), stop=(kc == 7)),
                         reads=[T_wu[wb], T_hT], writes=[Tpg], inc=(kc == 7))
                for kc in range(8):
                    K.op("pe", lambda e: e.matmul(hps[:, hcol:hcol + 2], lhsT=wu[wb][:, 0, kc, :], rhs=hT[:, kc, 0:T + 2:T + 1],
                                                  start=(kc == 0), stop=(kc == 7)),
                         reads=[T_wu[wb], T_hT], writes=[T_hps], inc=(kc == 7))
                for kc in range(8):
                    K.op("pe", lambda e: e.matmul(pv[:, 0:T], lhsT=wu[wb][:, 1, kc, :], rhs=hT[:, kc, 1:T + 1],
                                                  start=(kc == 0), stop=(kc == 7)),
                         reads=[T_wu[wb], T_hT], writes=[Tpv], inc=(kc == 7))
                gb = c % 2
                Gt = Gs[gb]
                K.op("act", lambda e: e.activation(out=Gt[:, 1:T + 1], in_=pg[:, 0:T], func=AF.Copy), reads=[Tpg], writes=[T_G[gb]])
                K.op("dve", lambda e: e.tensor_copy(out=Gt[:, 0:T + 2:T + 1], in_=hps[:, hcol:hcol + 2]), reads=[T_hps], writes=[T_G[gb]])
                K.op("act", lambda e: e.activation(out=vb[gb][:], in_=pv[:, 0:T], func=AF.Copy), reads=[Tpv], writes=[T_vb[gb]])
                cw = lambda j: g.fconv[:, layer, c, j:j + 1]
                K.op("dve", lambda e: e.tensor_scalar(out=tb[gb][:], in0=Gt[:, 1:T + 1], scalar1=cw(1), scalar2=cw(3),
                                                      op0=ALU.mult, op1=ALU.add),
                     reads=[T_G[gb], g.T_const], writes=[T_tb[gb]])
                K.op("dve", lambda e: e.scalar_tensor_tensor(out=tb[gb][:], in0=Gt[:, 0:T], scalar=cw(0), in1=tb[gb][:],
                                                             op0=ALU.mult, op1=ALU.add),
                     reads=[T_G[gb], g.T_const, T_tb[gb]], writes=[T_tb[gb]])
                K.op("dve", lambda e: e.scalar_tensor_tensor(out=tb[gb][:], in0=Gt[:, 2:T + 2], scalar=cw(2), in1=tb[gb][:],
                                                             op0=ALU.mult, op1=ALU.add),
                     reads=[T_G[gb], g.T_const, T_tb[gb]], writes=[T_tb[gb]])

                def tail(gb=gb, c=c):
                    K.op("act", lambda e: e.activation(out=tb[gb][:], in_=tb[gb][:], func=AF.Silu), reads=[T_tb[gb]], writes=[T_tb[gb]])
                    K.op("pool", lambda e: e.tensor_tensor(out=actb[:, c, :], in0=tb[gb][:], in1=vb[gb][:], op=ALU.mult),
                         reads=[T_tb[gb], T_vb[gb]], writes=[T_act])
                ftail.append(tail)
                if len(ftail) > 1:
                    ftail.pop(0)()
            while ftail:
                ftail.pop(0)()
            for m in range(nsub):
                for nh in range(2):
                    pb = 7 if (oc % 2 == 0) else 1
                    o = oc % 2
                    oc += 1
                    for kc in range(22):
                        K.op("pe", lambda e: e.matmul(g.ps[pb][:, :], lhsT=actb[:, kc, m * 128:(m + 1) * 128],
                                                      rhs=wd[:, kc, nh * 512:(nh + 1) * 512],
                                                      start=(kc == 0), stop=(kc == 21)),
                             reads=[T_act, T_wd], writes=[g.pst[pb]], inc=(kc == 21))
                    K.op("dve", lambda e: e.tensor_tensor(out=ob[o][:], in0=g.ps[pb][:, :], in1=g2bc[:, nh * 512:(nh + 1) * 512],
                                                          op=ALU.mult),
                         reads=[g.pst[pb], T_g2], writes=[T_ob[o]])
                    K.op("pool", lambda e: e.tensor_tensor(out=ob[o][:], in0=ob[o][:], in1=xt[xb][:, m, nh * 512:(nh + 1) * 512],
                                                           op=ALU.add),
                         reads=[T_ob[o], T_xt[xb]], writes=[T_ob[o]])
                    K.dma("sp", dst[t0 + m * 128:t0 + (m + 1) * 128, nh * 512:(nh + 1) * 512], ob[o][:],
                          reads=[T_ob[o]], writes=[T_dst])
        K.barrier()


class PiecePool:
    def __init__(self, g, es, tag, n, shape, dt):
        self.bufs = [SB(g, es, "%s%d" % (tag, i), shape, dt) for i in range(n)]
        self.T = trks(n)
        self.i = 0

    def next(self):
        b = self.i % len(self.bufs)
        self.i += 1
        return self.bufs[b], self.T[b]


def dft_bankcol(part, gi, wide):
    if wide:
        return 4 * part + gi, 0
    return 2 * part + gi // 2, (gi % 2) * 256


def fwd_dft(g, L, CT, ST, lhs_c, lhs_s, T_lhs, consumer, pp, alt, T_alt, W=None):
    K = g.K
    NS = L // 128
    W = W or min(256, L)
    wide = (W == 512)
    PSC = min(16, NS)
    ftiles = [(f0, W) for f0 in range(0, L, W)] + [(L, 1)]
    for (f0, Wt) in ftiles:
        nyq = (f0 == L)
        for part, (tab, lhs, bank0) in enumerate([(CT, lhs_c, 0), (ST, lhs_s, 2)]):
            if nyq and part == 1:
                continue
            pieces = []
            if not nyq:
                for pc in range(NS // PSC):
                    piece, T_piece = pp.next()
                    K.dma("sp", piece[:, 0:PSC, 0:Wt],
                          tab[pc * PSC * 128:(pc + 1) * PSC * 128, f0:f0 + Wt].rearrange("(c p) f -> p c f", p=128),
                          writes=[T_piece])
                    pieces.append((piece, T_piece))
            for gi in range(4):
                bank, col = dft_bankcol(part, gi, wide)
                for sc in range(NS):
                    if nyq:
                        rhs, T_r = alt[:, 0:1], T_alt
                    else:
                        piece, T_piece = pieces[sc // PSC]
                        rhs, T_r = piece[:, sc % PSC, 0:Wt], T_piece
                    K.op("pe", lambda e: e.matmul(g.ps[bank][:, col:col + Wt], lhsT=lhs(sc, gi), rhs=rhs,
                                                  start=(sc == 0), stop=(sc == NS - 1)),
                         reads=[T_r] + T_lhs, writes=[g.pst[bank]], inc=(sc % PSC == PSC - 1))
        consumer(f0, Wt, nyq)


def mixer0(g, stream, L, src, T_src, dst, T_dst, tag):
    K, nc, d = g.K, g.nc, g.d
    NS = L // 128
    T = min(512, L)
    NT = L // T
    nsub = T // 128
    sfx = "_%d" % L
    CT, ST, C4, S4n = d["CT" + sfx], d["ST" + sfx], d["C4" + sfx], d["S4n" + sfx]
    Kre, Ksn = d["Kre" + sfx], d["Ksn" + sfx]
    ABd = d["ABd" + sfx]
    X0d, Zd = d["X0d" + sfx], d["Zd" + sfx]
    YTd, YTn = d["YTd" + sfx], d["YTn" + sfx]
    T_K, T_ABd, T_X0d, T_Zd, T_YTd = DTrk(), DTrk(), DTrk(), DTrk(), DTrk()
    A, B = mod_scalars(g, 0, stream, 0)
    W = min(256, L)
    PSC = min(16, NS)
    with ExitStack() as mx:
        cst = SB(g, mx, tag + "_alt", [128, 1], BF16)
        T_cst = Trk()
        K.dma("sp", cst[:], d["alt"], writes=[T_cst])
        rn2 = SB(g, mx, tag + "_rn2", [128, 4], F32)
        T_rn2 = Trk()
        with ExitStack() as ph1:
            fsT = SB(g, ph1, tag + "_fsT", [128, NS, 512], BF16)
            fdT = SB(g, ph1, tag + "_fdT", [128, NS, 512], BF16)
            T_fs, T_fd = Trk(), Trk()
            with ExitStack() as ph:
                zT = SB(g, ph, tag + "_zT", [33, L], F32)
                w1 = SB(g, ph, tag + "_w1", [33, 64], F32)
                w2 = SB(g, ph, tag + "_w2", [64, 64], F32)
                w3 = SB(g, ph, tag + "_w3", [64, 64], F32)
                w4 = SB(g, ph, tag + "_w4", [64, 1024], F32)
                hyp = SB(g, ph, tag + "_hyp", [64, 4], F32)
                sc1 = SB(g, ph, tag + "_sc1", [64, 4], F32)
                negt = SB(g, ph, tag + "_negt", [128, NS], F32)
                dbc = SB(g, ph, tag + "_dbc", [128, 512], F32)
                acc = SB(g, ph, tag + "_acc", [128, 512], F32)
                ones = SB(g, ph, tag + "_ones", [128, 1], F32)
                h3 = SB(g, ph, tag + "_h3", [64, L], F32)
                T_w, T_sc1, T_acc, T_h3, T_ones = Trk(), Trk(), Trk(), Trk(), Trk()
                K.dma("sp", zT[:], d["hyz" + sfx], writes=[T_w])
                K.dma("sp", w1[:], d["hy_w1"], writes=[T_w])
                K.dma("sp", w2[:], d["hy_w2"], writes=[T_w])
                K.dma("sp", w3[:], d["hy_w3"], writes=[T_w])
                K.dma("sp", w4[:], d["hy_w4"], writes=[T_w])
                K.dma("sp", hyp[:], d["hyp"], writes=[T_w])
                K.dma("sp", negt[:], d["hynegt" + sfx], writes=[T_w])
                K.dma("sp", dbc[:], d["hydelta"].broadcast_to([128, 512]), writes=[T_w])
                K.op("pool", lambda e: e.memset(acc[:], 0.0), writes=[T_acc])
                K.op("pool", lambda e: e.memset(ones[:], 1.0), writes=[T_ones])
                K.op("dve", lambda e: e.tensor_scalar(out=sc1[:, 0:1], in0=hyp[:, 0:1], scalar1=1.0 / (2 * PI), scalar2=None,
                                                      op0=ALU.mult), reads=[T_w], writes=[T_sc1])
                for i in range(1, 4):
                    K.op("dve", lambda e: e.tensor_scalar(out=sc1[:, i:i + 1], in0=hyp[:, i:i + 1], scalar1=sc1[:, 0:1],
                                                          scalar2=64.0, op0=ALU.mult, op1=ALU.add),
                         reads=[T_w, T_sc1], writes=[T_sc1])
                TW = min(512, L)
                q = SB(g, ph, tag + "_q", [64, TW], F32)
                ki = SB(g, ph, tag + "_ki", [64, TW], I32)
                kf = SB(g, ph, tag + "_kf", [64, TW], F32)
                hb_ = [SB(g, ph, tag + "_hm%d" % i, [64, TW], F32) for i in range(2)]
                T_q, T_ki, T_kf = Trk(), Trk(), Trk()
                T_hm = trks(2)
                ws = [w1, w2, w3]
                dec = [SB(g, ph, tag + "_dec%d" % i, [128, 512], F32) for i in range(2)]
                hf = [SB(g, ph, tag + "_hf%d" % i, [128, 512], F32) for i in range(2)]
                hb = [SB(g, ph, tag + "_hb%d" % i, [128, 512], F32) for i in range(2)]
                T_dec, T_hf, T_hb = trks(2), trks(2), trks(2)
                yield
                for tt in range(L // TW):
                    cur, T_cur = zT[:, tt * TW:(tt + 1) * TW], T_w
                    for ly in range(3):
                        b = ly % 2
                        K.op("pe", lambda e: e.matmul(g.ps[b][0:64, 0:TW], lhsT=ws[ly][:], rhs=cur, start=True, stop=True),
                             reads=[T_w, T_cur], writes=[g.pst[b]])
                        K.op("dve", lambda e: e.tensor_scalar(out=q[:], in0=g.ps[b][0:64, 0:TW], scalar1=sc1[:, 0:1],
                                                              scalar2=sc1[:, ly + 1:ly + 2], op0=ALU.mult, op1=ALU.add),
                             reads=[g.pst[b], T_sc1], writes=[T_q])
                        K.op("dve", lambda e: e.tensor_copy(out=ki[:], in_=q[:]), reads=[T_q], writes=[T_ki])
                        K.op("dve", lambda e: e.tensor_copy(out=kf[:], in_=ki[:]), reads=[T_ki], writes=[T_kf])
                        K.op("dve", lambda e: e.tensor_tensor(out=q[:], in0=q[:], in1=kf[:], op=ALU.subtract),
                             reads=[T_q, T_kf], writes=[T_q])
                        K.op("dve", lambda e: e.scalar_tensor_tensor(out=kf[:], in0=q[:], scalar=0.5, in1=q[:],
                                                                     op0=ALU.is_gt, op1=ALU.subtract),
                             reads=[T_q], writes=[T_kf])
                        if ly < 2:
                            o_ap, T_o = hb_[ly][:], T_hm[ly]
                        else:
                            o_ap, T_o = h3[:, tt * TW:(tt + 1) * TW], T_h3
                        K.op("act", lambda e: e.activation(out=o_ap, in_=kf[:], func=AF.Sin, scale=-2 * PI * (1 - 1e-6)),
                             reads=[T_kf], writes=[T_o])
                        cur, T_cur = o_ap, T_o
                        yield
                for sc in range(NS):
                    yield
                    b = sc % 2
                    pf, pb_ = g.ps[2 * b], g.ps[2 * b + 1]
                    K.op("pe", lambda e: e.matmul(pf[:, :], lhsT=h3[:, sc * 128:(sc + 1) * 128], rhs=w4[:, 0:512],
                                                  start=True, stop=True), reads=[T_h3, T_w], writes=[g.pst[2 * b]])
                    K.op("pe", lambda e: e.matmul(pb_[:, :], lhsT=h3[:, sc * 128:(sc + 1) * 128], rhs=w4[:, 512:1024],
                                                  start=True, stop=True), reads=[T_h3, T_w], writes=[g.pst[2 * b + 1]])
                    K.op("act", lambda e: e.activation(out=dec[b][:], in_=dbc[:], func=AF.Exp, scale=negt[:, sc:sc + 1]),
                         reads=[T_w], writes=[T_dec[b]])
                    K.op("dve", lambda e: e.tensor_tensor(out=hf[b][:], in0=pf[:, :], in1=dec[b][:], op=ALU.mult),
                         reads=[g.pst[2 * b], T_dec[b]], writes=[T_hf[b]])
                    K.op("dve", lambda e: e.tensor_tensor(out=hb[b][:], in0=pb_[:, :], in1=dec[b][:], op=ALU.mult),
                         reads=[g.pst[2 * b + 1], T_dec[b]], writes=[T_hb[b]])
                    if sc == 0:
                        K.op("dve", lambda e: e.memset(hb[b][0:1, :], 0.0), writes=[T_hb[b]])
                    K.op("pool", lambda e: e.tensor_tensor(out=fsT[:, sc, :], in0=hf[b][:], in1=hb[b][:], op=ALU.add),
                         reads=[T_hf[b], T_hb[b]], writes=[T_fs])
                    K.op("pool", lambda e: e.tensor_tensor(out=fdT[:, sc, :], in0=hf[b][:], in1=hb[b][:], op=ALU.subtract),
                         reads=[T_hf[b], T_hb[b]], writes=[T_fd])
                    for (hsrc, T_hs) in ((hf[b], T_hf[b]), (hb[b], T_hb[b])):
                        K.op("act", lambda e: e.activation(out=dec[b][:], in_=hsrc[:], func=AF.Abs),
                             reads=[T_hs], writes=[T_dec[b]])
                        K.op("pool", lambda e: e.tensor_tensor(out=acc[:], in0=acc[:], in1=dec[b][:], op=ALU.add),
                             reads=[T_dec[b], T_acc], writes=[T_acc])
                for gi in range(4):
                    K.op("pe", lambda e: e.matmul(g.ps[4][:, gi:gi + 1], lhsT=acc[:, gi * 128:(gi + 1) * 128], rhs=ones[:, 0:1],
                                                  start=True, stop=True), reads=[T_acc, T_ones], writes=[g.pst[4]])
                K.op("dve", lambda e: e.reciprocal(out=rn2[:], in_=g.ps[4][:, 0:4]), reads=[g.pst[4]], writes=[T_rn2])
                K.op("dve", lambda e: e.tensor_scalar(out=rn2[:], in0=rn2[:], scalar1=2.0 / (2 * L), scalar2=None, op0=ALU.mult),
                     reads=[T_rn2], writes=[T_rn2])
                K.barrier()
            with ExitStack() as ph:
                W2 = min(512, L)
                wide2 = (W2 == 512)
                pp = PiecePool(g, ph, tag + "_pcA", 4, [128, PSC, W2], BF16)
                kst = [SB(g, ph, tag + "_kst%d" % i, [128, 4, W2], F32) for i in range(4)]
                T_kst = trks(4)
                kcount = [0]

                def kcons(f0, Wt, nyq):
                    for part, (bank0, dstK) in enumerate([(0, Kre), (2, Ksn)]):
                        kb = kcount[0] % 4
                        kcount[0] += 1
                        if nyq and part == 1:
                            K.op("pool", lambda e: e.memset(kst[kb][:, :, 0:1], 0.0), writes=[T_kst[kb]])
                        else:
                            for gi in range(4):
                                bank, col = dft_bankcol(part, gi, wide2)
                                K.op("act", lambda e: e.activation(out=kst[kb][:, gi, 0:Wt], in_=g.ps[bank][:, col:col + Wt],
                                                                   func=AF.Copy, scale=rn2[:, gi:gi + 1]),
                                     reads=[g.pst[bank], T_rn2], writes=[T_kst[kb]])
                            if f0 == 0 or nyq:
                                K.op("dve", lambda e: e.tensor_scalar(out=kst[kb][:, :, 0:1], in0=kst[kb][:, :, 0:1], scalar1=0.5,
                                                                      scalar2=None, op0=ALU.mult),
                                     reads=[T_kst[kb]], writes=[T_kst[kb]])
                        K.dma("sp", dstK[:, :, f0:f0 + Wt].rearrange("g p f -> p g f"), kst[kb][:, :, 0:Wt],
                              reads=[T_kst[kb]], writes=[T_K], allow_slow_non_contiguous=(Wt == 1))
                fwd_dft(g, L, CT, ST, lambda sc, gi: fsT[:, sc, gi * 128:(gi + 1) * 128],
                        lambda sc, gi: fdT[:, sc, gi * 128:(gi + 1) * 128], [T_fs, T_fd], kcons, pp, cst, T_cst, W=W2)
                K.barrier()
        with ExitStack() as ph3:
            zTt = SB(g, ph3, tag + "_zTt", [128, NS, 512], BF16)
            T_zTt = Trk()
            with ExitStack() as ph:
                hxT = SB(g, ph, tag + "_hxT", [128, 8, L], BF16)
                T_hxT = Trk()
                with ExitStack() as pa:
                    xt = [SB(g, pa, tag + "_xt%d" % i, [128, nsub, 1024], F32) for i in range(2)]
                    T_xt = trks(2)
                    nb = NormBufs(g, pa, tag + "_nb", nsub)
                    for ti in range(NT):
                        t0 = ti * T
                        xb = ti % 2
                        K.dma("sp", xt[xb][:], src[t0:t0 + T].rearrange("(s p) d -> p s d", p=128), reads=[T_src],
                              writes=[T_xt[xb]])
                        norm_T(g, nb, xt[xb], T_xt[xb], 128, nsub, A, B, lambda kc: hxT[:, kc, t0:t0 + T], T_hxT)
                    K.barrier()
                win = g.d["fhwin_bf"]
                wpp = PiecePool(g, ph, tag + "_win", 3, [128, 8, 128], BF16)
                cs128 = SB(g, ph, tag + "_cs128", [128, 256], BF16)
                hyc = SB(g, ph, tag + "_hyc", [128, 12, 4], F32)
                T_c3 = Trk()
                K.dma("sp", cs128[:], d["cs128"], writes=[T_c3])
                K.dma("sp", hyc[:], d["hyconv"], writes=[T_c3])
                UT = [SB(g, ph, tag + "_UT%d" % i, [128, T], BF16) for i in range(2)]
                T_UT = trks(2)
                abt = [SB(g, ph, tag + "_abt%d" % i, [128, 2, 256], BF16) for i in range(2)]
                T_abt = trks(2)
                uc = 0
                ac = 0
                for gi in range(4):
                    wt, T_wt = wpp.next()
                    K.dma("sp", wt[:].rearrange("p k n -> p (k n)"), win[gi], reads=[g.T_wbf], writes=[T_wt])
                    for ti in range(NT):
                        t0 = ti * T
                        ub = uc % 2
                        uc += 1
                        pb_ = 0 + ub
                        for kc in range(8):
                            K.op("pe", lambda e: e.matmul(g.ps[pb_][:, 0:T], lhsT=wt[:, kc, :], rhs=hxT[:, kc, t0:t0 + T],
                                                          start=(kc == 0), stop=(kc == 7)),
                                 reads=[T_wt, T_hxT], writes=[g.pst[pb_]], inc=(kc == 7))
                        K.op("act", lambda e: e.activation(out=UT[ub][:], in_=g.ps[pb_][:, 0:T], func=AF.Copy),
                             reads=[g.pst[pb_]], writes=[T_UT[ub]])
                        for s2 in range(nsub // 2):
                            ab = ac % 2
                            ac += 1
                            pb2 = 2 + ab
                            for h in range(2):
                                sub = s2 * 2 + h
                                K.op("pe", lambda e: e.matmul(g.ps[pb2][:, h * 256:(h + 1) * 256],
                                                              lhsT=UT[ub][:, sub * 128:(sub + 1) * 128], rhs=cs128[:],
                                                              start=True, stop=True),
                                     reads=[T_UT[ub], T_c3], writes=[g.pst[pb2]], inc=(h == 1))
                            K.op("dve", lambda e: e.tensor_copy(out=abt[ab][:], in_=g.ps[pb2][:, :].rearrange("p (h c) -> p h c", h=2)),
                                 reads=[g.pst[pb2]], writes=[T_abt[ab]])
                            r0 = t0 + s2 * 256
                            K.dma("sp", ABd[r0:r0 + 256, gi, :].rearrange("(h p) c -> p h c", p=128), abt[ab][:],
                                  reads=[T_abt[ab]], writes=[T_ABd])
                P32 = [SB(g, ph, tag + "_P32%d" % i, [128, L + 2], F32) for i in range(2)]
                T_P32 = trks(2)
                for i in range(2):
                    K.op("pool", lambda e: e.memset(P32[i][:, 0:L + 2:L + 1], 0.0), writes=[T_P32[i]])
                cx1 = SB(g, ph, tag + "_cx1", [128, L], F32)
                z32 = SB(g, ph, tag + "_z32", [128, L], F32)
                x0b = SB(g, ph, tag + "_x0b", [128, L], BF16)
                zb = SB(g, ph, tag + "_zb", [128, L], BF16)
                T_cx1, T_z32, T_x0b, T_zb = Trk(), Trk(), Trk(), Trk()
                pc_ = 0
                zyc = 0
                for gi in range(4):
                    for which, cc in enumerate([4 + gi, 8 + gi, 12 + gi]):
                        wt, T_wt = wpp.next()
                        K.dma("sp", wt[:].rearrange("p k n -> p (k n)"), win[cc], reads=[g.T_wbf], writes=[T_wt])
                        pb32 = pc_ % 2
                        pc_ += 1
                        Pt, T_Pt = P32[pb32], T_P32[pb32]
                        hch = which * 4 + gi
                        cw = lambda j: hyc[:, hch, j:j + 1]
                        if which == 1:
                            o32, T_o = cx1, T_cx1
                        else:
                            o32, T_o = z32, T_z32
                        for ti in range(NT):
                            t0 = ti * T
                            ub = uc % 2
                            uc += 1
                            pb_ = 0 + ub
                            for kc in range(8):
                                K.op("pe", lambda e: e.matmul(g.ps[pb_][:, 0:T], lhsT=wt[:, kc, :], rhs=hxT[:, kc, t0:t0 + T],
                                                              start=(kc == 0), stop=(kc == 7)),
                                     reads=[T_wt, T_hxT], writes=[g.pst[pb_]], inc=(kc == 7))
                            K.op("act", lambda e: e.activation(out=Pt[:, 1 + t0:1 + t0 + T], in_=g.ps[pb_][:, 0:T], func=AF.Copy),
                                 reads=[g.pst[pb_]], writes=[T_Pt])
                            K.op("act", lambda e: e.activation(out=o32[:, t0:t0 + T], in_=g.ps[pb_][:, 0:T], func=AF.Identity,
                                                               scale=cw(1), bias=cw(3)),
                                 reads=[g.pst[pb_], T_c3], writes=[T_o])
                        for hh in range(2):
                            c0 = hh * (L // 2)
                            c1 = c0 + L // 2
                            K.op("dve", lambda e: e.scalar_tensor_tensor(out=o32[:, c0:c1], in0=Pt[:, c0:c1], scalar=cw(0),
                                                                         in1=o32[:, c0:c1], op0=ALU.mult, op1=ALU.add),
                                 reads=[T_Pt, T_c3, T_o], writes=[T_o])
                            K.op("dve", lambda e: e.scalar_tensor_tensor(out=o32[:, c0:c1], in0=Pt[:, 2 + c0:2 + c1], scalar=cw(2),
                                                                         in1=o32[:, c0:c1], op0=ALU.mult, op1=ALU.add),
                                 reads=[T_Pt, T_c3, T_o], writes=[T_o])
                        if which == 0:
                            K.op("pool", lambda e: e.tensor_copy(out=x0b[:], in_=z32[:]), reads=[T_z32], writes=[T_x0b])
                            K.dma("sp", X0d[gi], x0b[:], reads=[T_x0b], writes=[T_X0d])
                        elif which == 2:
                            K.op("pool", lambda e: e.tensor_tensor(out=z32[:], in0=z32[:], in1=cx1[:], op=ALU.mult),
                                 reads=[T_z32, T_cx1], writes=[T_z32])
                            K.op("act", lambda e: e.activation(out=zb[:], in_=z32[:], func=AF.Copy), reads=[T_z32], writes=[T_zb])
                            K.dma("sp", Zd[gi], zb[:], reads=[T_zb], writes=[T_Zd])
                            for s4 in range(NS // 4 if NS >= 4 else 1):
                                nin = min(4, NS)
                                zb_ = 4 + zyc % 4
                                zyc += 1
                                for h in range(nin):
                                    sc = s4 * 4 + h
                                    K.op("pe", lambda e: e.transpose(g.ps[zb_][:, h * 128:(h + 1) * 128],
                                                                     z32[:, sc * 128:(sc + 1) * 128], g.ident[:]),
                                         reads=[T_z32, g.T_ident], writes=[g.pst[zb_]], inc=(h == nin - 1))
                                K.op("dve" if s4 % 2 == 0 else "act",
                                     (lambda e: e.tensor_copy(out=zTt[:, s4 * 4:s4 * 4 + nin, gi * 128:(gi + 1) * 128],
                                                              in_=g.ps[zb_][:, 0:nin * 128].rearrange("p (h c) -> p h c", h=nin)))
                                     if s4 % 2 == 0 else
                                     (lambda e: e.activation(out=zTt[:, s4 * 4:s4 * 4 + nin, gi * 128:(gi + 1) * 128],
                                                             in_=g.ps[zb_][:, 0:nin * 128].rearrange("p (h c) -> p h c", h=nin),
                                                             func=AF.Copy)),
                                     reads=[g.pst[zb_]], writes=[T_zTt])
                K.barrier()
            with ExitStack() as ph:
                pp = PiecePool(g, ph, tag + "_pcB", 4, [128, PSC, W], BF16)
                ktr = [SB(g, ph, tag + "_ktr%d" % i, [128, 4, W], F32) for i in range(2)]
                kts = [SB(g, ph, tag + "_kts%d" % i, [128, 4, W], F32) for i in range(2)]
                T_kt = trks(2)
                ta = [SB(g, ph, tag + "_ta%d" % i, [128, W], F32) for i in range(4)]
                tbb = [SB(g, ph, tag + "_tbb%d" % i, [128, W], F32) for i in range(4)]
                T_ta, T_tbb = trks(4), trks(4)
                yt = [SB(g, ph, tag + "_yt%d" % i, [128, 512], BF16) for i in range(4)]
                T_yt = trks(4)
                cnt = {"f": 0, "e": 0, "y": 0}

                def zcons(f0, Wt, nyq):
                    fb = cnt["f"] % 2
                    cnt["f"] += 1
                    K.dma("sp", ktr[fb][:, :, 0:Wt], Kre[:, :, f0:f0 + Wt].rearrange("g p f -> p g f"), reads=[T_K], writes=[T_kt[fb]],
                          allow_slow_non_contiguous=(Wt == 1))
                    K.dma("sp", kts[fb][:, :, 0:Wt], Ksn[:, :, f0:f0 + Wt].rearrange("g p f -> p g f"), reads=[T_K], writes=[T_kt[fb]],
                          allow_slow_non_contiguous=(Wt == 1))
                    nsb = max(1, Wt // 128)
                    for gi in range(4):
                        Zc = g.ps[gi // 2][:, (gi % 2) * 256:(gi % 2) * 256 + Wt]
                        Zs = g.ps[2 + gi // 2][:, (gi % 2) * 256:(gi % 2) * 256 + Wt]
                        TZc, TZs = g.pst[gi // 2], g.pst[2 + gi // 2]
                        eb = cnt["e"] % 2
                        cnt["e"] += 1
                        yre, T_yre = ta[2 * eb], T_ta[2 * eb]
                        yq, T_yq = ta[2 * eb + 1], T_ta[2 * eb + 1]
                        t1, T_t1 = tbb[2 * eb], T_tbb[2 * eb]
                        t2, T_t2 = tbb[2 * eb + 1], T_tbb[2 * eb + 1]
                        K.op("dve", lambda e: e.tensor_tensor(out=yre[:, 0:Wt], in0=Zc, in1=ktr[fb][:, gi, 0:Wt], op=ALU.mult),
                             reads=[TZc, T_kt[fb]], writes=[T_yre])
                        if not nyq:
                            K.op("dve", lambda e: e.tensor_tensor(out=t1[:, 0:Wt], in0=Zs, in1=kts[fb][:, gi, 0:Wt], op=ALU.mult),
                                 reads=[TZs, T_kt[fb]], writes=[T_t1])
                            K.op("pool", lambda e: e.tensor_tensor(out=yre[:, 0:Wt], in0=yre[:, 0:Wt], in1=t1[:, 0:Wt],
                                                                   op=ALU.subtract), reads=[T_yre, T_t1], writes=[T_yre])
                            K.op("dve", lambda e: e.tensor_tensor(out=yq[:, 0:Wt], in0=Zc, in1=kts[fb][:, gi, 0:Wt], op=ALU.mult),
                                 reads=[TZc, T_kt[fb]], writes=[T_yq])
                            K.op("dve", lambda e: e.tensor_tensor(out=t2[:, 0:Wt], in0=Zs, in1=ktr[fb][:, gi, 0:Wt], op=ALU.mult),
                                 reads=[TZs, T_kt[fb]], writes=[T_t2])
                            K.op("pool", lambda e: e.tensor_tensor(out=yq[:, 0:Wt], in0=yq[:, 0:Wt], in1=t2[:, 0:Wt], op=ALU.add),
                                 reads=[T_yq, T_t2], writes=[T_yq])
                        for part, (ysrc, T_ys) in enumerate([(yre, T_yre), (yq, T_yq)]):
                            if nyq and part == 1:
                                continue
                            for sub in range(nsb):
                                bank = 4 + part * 2 + sub
                                wdt = min(128, Wt)
                                K.op("pe", lambda e: e.transpose(g.ps[bank][0:wdt, gi * 128:(gi + 1) * 128],
                                                                 ysrc[:, sub * 128:sub * 128 + wdt], g.ident[:]),
                                     reads=[T_ys, g.T_ident], writes=[g.pst[bank]])
                    for part in range(2):
                        if nyq and part == 1:
                            continue
                        for sub in range(nsb):
                            bank = 4 + part * 2 + sub
                            wdt = min(128, Wt)
                            yb = cnt["y"] % 4
                            cnt["y"] += 1
                            K.op("act", lambda e: e.activation(out=yt[yb][0:wdt, :], in_=g.ps[bank][0:wdt, :], func=AF.Copy),
                                 reads=[g.pst[bank]], writes=[T_yt[yb]])
                            if nyq:
                                K.dma("sp", YTn[0:1, :], yt[yb][0:1, :], reads=[T_yt[yb]], writes=[T_YTd])
                            else:
                                K.dma("sp", YTd[part, f0 // 128 + sub], yt[yb][:], reads=[T_yt[yb]], writes=[T_YTd])
                fwd_dft(g, L, CT, ST, lambda sc, gi: zTt[:, sc, gi * 128:(gi + 1) * 128],
                        lambda sc, gi: zTt[:, sc, gi * 128:(gi + 1) * 128], [T_zTt], zcons, pp, cst, T_cst)
                K.barrier()
        with ExitStack() as ph5:
            ycat = SB(g, ph5, tag + "_ycat", [128, 8, L], BF16)
            T_ycat = Trk()
            with ExitStack() as ph:
                YT = SB(g, ph, tag + "_YT", [128, 2, NS, 512], BF16)
                YTnr = SB(g, ph, tag + "_YTnr", [1, 512], BF16)
                altrow = SB(g, ph, tag + "_altrow", [1, L], BF16)
                hbias = SB(g, ph, tag + "_hbias", [128, 4], F32)
                T_YT = Trk()
                for part in range(2):
                    K.dma("sp", YT[:, part, :, :], YTd[part].rearrange("c p n -> p c n"), reads=[T_YTd], writes=[T_YT])
                K.dma("sp", YTnr[:], YTn, reads=[T_YTd], writes=[T_YT])
                K.dma("sp", altrow[:], d["altrow" + sfx], writes=[T_YT])
                K.dma("sp", hbias[:], d["hybias"], writes=[T_YT])
                pp = PiecePool(g, ph, tag + "_pcC", 4, [128, 8, 512], BF16)
                PS8 = min(8, NS)
                x0t = [SB(g, ph, tag + "_x0t%d" % i, [128, 4, T], BF16) for i in range(2)]
                zt = [SB(g, ph, tag + "_zt%d" % i, [128, 4, T], BF16) for i in range(2)]
                T_xz = trks(2)
                ut = [SB(g, ph, tag + "_ut%d" % i, [128, T], F32) for i in range(2)]
                T_ut = trks(2)
                ucn = 0
                for ti in range(NT):
                    t0 = ti * T
                    xb = ti % 2
                    K.dma("sp", x0t[xb][:], X0d[:, :, t0:t0 + T].rearrange("g p t -> p g t"), reads=[T_X0d], writes=[T_xz[xb]])
                    K.dma("sp", zt[xb][:], Zd[:, :, t0:t0 + T].rearrange("g p t -> p g t"), reads=[T_Zd], writes=[T_xz[xb]])
                    bk0 = 4 * (ti % 2)
                    first = [True] * 4
                    for part, tab in enumerate([CT, ST]):
                        for pc in range(NS // PS8):
                            piece, T_piece = pp.next()
                            K.dma("sp", piece[:, 0:PS8, 0:T],
                                  tab[pc * PS8 * 128:(pc + 1) * PS8 * 128, t0:t0 + T].rearrange("(c p) f -> p c f", p=128),
                                  writes=[T_piece])
                            for gi in range(4):
                                for j in range(PS8):
                                    fc = pc * PS8 + j
                                    K.op("pe", lambda e: e.matmul(g.ps[bk0 + gi][:, 0:T], lhsT=YT[:, part, fc, gi * 128:(gi + 1) * 128],
                                                                  rhs=piece[:, j, 0:T], start=first[gi], stop=False),
                                         reads=[T_YT, T_piece], writes=[g.pst[bk0 + gi]], inc=(j == PS8 - 1))
                                    first[gi] = False
                    for gi in range(4):
                        K.op("pe", lambda e: e.matmul(g.ps[bk0 + gi][:, 0:T], lhsT=YTnr[0:1, gi * 128:(gi + 1) * 128],
                                                      rhs=altrow[0:1, t0:t0 + T], start=False, stop=True),
                             reads=[T_YT], writes=[g.pst[bk0 + gi]])
                        ub = ucn % 2
                        ucn += 1
                        K.op("dve", lambda e: e.scalar_tensor_tensor(out=ut[ub][:], in0=zt[xb][:, gi, :], scalar=hbias[:, gi:gi + 1],
                                                                     in1=g.ps[bk0 + gi][:, 0:T], op0=ALU.mult, op1=ALU.add),
                             reads=[T_xz[xb], T_YT, g.pst[bk0 + gi]], writes=[T_ut[ub]])
                        K.op("pool", lambda e: e.tensor_tensor(out=ycat[:, 4 + gi, t0:t0 + T], in0=ut[ub][:], in1=x0t[xb][:, gi, :],
                                                               op=ALU.mult),
                             reads=[T_ut[ub], T_xz[xb]], writes=[T_ycat])
                K.barrier()
            with ExitStack() as ph:
                ABs = SB(g, ph, tag + "_ABs", [128, NS, 4, 256], BF16)
                T_ABs = Trk()
                for sc in range(0, NS, 8):
                    n = min(8, NS - sc)
                    K.dma("sp", ABs[:, sc:sc + n, :, :].rearrange("p c g k -> p c (g k)"),
                          ABd[sc * 128:(sc + n) * 128].rearrange("(c p) g k -> p c (g k)", p=128), reads=[T_ABd], writes=[T_ABs])
                half = L // 2
                Th = min(512, half)
                pp = PiecePool(g, ph, tag + "_pcD", 4, [128, 8, Th], BF16)
                PS8 = min(8, NS)
                fsc = 1.0 / float(np.sqrt(128.0 * L))
                qs = [SB(g, ph, tag + "_qs%d" % i, [128, Th], F32) for i in range(2)]
                T_qs = trks(2)
                qc = 0
                it = 0
                for gp in range(2):
                    for k0 in range(0, half, Th):
                        bk0 = 4 * (it % 2)
                        it += 1
                        for part, tab in enumerate([C4, S4n]):
                            pieces = []
                            for pc in range(NS // PS8):
                                piece, T_piece = pp.next()
                                K.dma("sp", piece[:, 0:PS8, 0:Th],
                                      tab[pc * PS8 * 128:(pc + 1) * PS8 * 128, k0:k0 + Th].rearrange("(c p) f -> p c f", p=128),
                                      writes=[T_piece])
                                pieces.append((piece, T_piece))
                            for gl in range(2):
                                gi = 2 * gp + gl
                                bank = bk0 + 2 * gl + part
                                for sc in range(NS):
                                    piece, T_piece = pieces[sc // PS8]
                                    K.op("pe", lambda e: e.matmul(g.ps[bank][:, 0:Th], lhsT=ABs[:, sc, gi, part * 128:(part + 1) * 128],
                                                                  rhs=piece[:, sc % PS8, 0:Th], start=(sc == 0), stop=(sc == NS - 1)),
                                         reads=[T_ABs, T_piece], writes=[g.pst[bank]], inc=(sc % PS8 == PS8 - 1))
                        for gl in range(2):
                            gi = 2 * gp + gl
                            bP, bQ = bk0 + 2 * gl, bk0 + 2 * gl + 1
                            q_ = qc % 2
                            qc += 1
                            K.op("act", lambda e: e.activation(out=qs[q_][:, 0:Th], in_=g.ps[bQ][:, 0:Th], func=AF.Copy, scale=fsc),
                                 reads=[g.pst[bQ]], writes=[T_qs[q_]])
                            K.op("dve", lambda e: e.scalar_tensor_tensor(out=ycat[:, gi, k0:k0 + Th], in0=g.ps[bP][:, 0:Th], scalar=fsc,
                                                                         in1=qs[q_][:, 0:Th], op0=ALU.mult, op1=ALU.add),
                                 reads=[g.pst[bP], T_qs[q_]], writes=[T_ycat])
                            lo = 1 if k0 == 0 else 0
                            m_hi = L - (k0 + lo)
                            m_lo = L - (k0 + Th - 1)
                            stop = m_lo - 1
                            out_m = ycat[:, gi, m_hi:stop:-1] if stop >= 0 else ycat[:, gi, m_hi::-1]
                            K.op("dve", lambda e: e.scalar_tensor_tensor(out=out_m, in0=g.ps[bP][:, lo:Th], scalar=fsc,
                                                                         in1=qs[q_][:, lo:Th], op0=ALU.mult, op1=ALU.subtract),
                                 reads=[g.pst[bP], T_qs[q_]], writes=[T_ycat])
                for gi in range(4):
                    for sc in range(NS):
                        K.op("pe", lambda e: e.matmul(g.ps[0][:, gi:gi + 1], lhsT=ABs[:, sc, gi, 0:128], rhs=cst[:, 0:1],
                                                      start=(sc == 0), stop=(sc == NS - 1)),
                             reads=[T_ABs, T_cst], writes=[g.pst[0]], inc=(sc == NS - 1))
                    K.op("act", lambda e: e.activation(out=ycat[:, gi, half:half + 1], in_=g.ps[0][:, gi:gi + 1], func=AF.Copy, scale=fsc),
                         reads=[g.pst[0]], writes=[T_ycat])
                K.barrier()
            with ExitStack() as ph:
                wo = SB(g, ph, tag + "_wo", [128, 8, 1024], BF16)
                T_wo = Trk()
                K.dma("sp", wo[:], g.d["fhwout_bf"].rearrange("k p n -> p k n"), reads=[g.T_wbf], writes=[T_wo])
                g1bc, T_g1 = load_gate_bc(g, ph, tag + "_g1", 0, stream, 0)
                xt = [SB(g, ph, tag + "_xo%d" % i, [128, 1024], F32) for i in range(2)]
                T_xt = trks(2)
                ob = [SB(g, ph, tag + "_ob%d" % i, [128, 1024], F32) for i in range(2)]
                T_ob = trks(2)
                for m in range(NS):
                    xb = m % 2
                    K.dma("sp", xt[xb][:], src[m * 128:(m + 1) * 128, :], reads=[T_src], writes=[T_xt[xb]])
                    for nh in range(2):
                        pb_ = (2 * m + nh) % 4
                        for kc in range(8):
                            K.op("pe", lambda e: e.matmul(g.ps[pb_][:, :], lhsT=ycat[:, kc, m * 128:(m + 1) * 128],
                                                          rhs=wo[:, kc, nh * 512:(nh + 1) * 512], start=(kc == 0), stop=(kc == 7)),
                                 reads=[T_ycat, T_wo], writes=[g.pst[pb_]], inc=(kc == 7))
                        K.op("dve", lambda e: e.tensor_tensor(out=ob[xb][:, nh * 512:(nh + 1) * 512], in0=g.ps[pb_][:, :],
                                                              in1=g1bc[:, nh * 512:(nh + 1) * 512], op=ALU.mult),
                             reads=[g.pst[pb_], T_g1], writes=[T_ob[xb]])
                    K.op("pool", lambda e: e.tensor_tensor(out=ob[xb][:], in0=ob[xb][:], in1=xt[xb][:], op=ALU.add),
                         reads=[T_ob[xb], T_xt[xb]], writes=[T_ob[xb]])
                    K.dma("sp", dst[m * 128:(m + 1) * 128, :], ob[xb][:], reads=[T_ob[xb]], writes=[T_dst])
                K.barrier()


NH = 16
NKC = (LX + LCX) // 128


def mla(g, src, T_src, csrc, T_csrc, dst, T_dst, tag="ml"):
    K, nc, d = g.K, g.nc, g.d
    L = LX
    Qd, Kd, Vd, Od = d["Qd"], d["Kd"], d["Vd"], d["Od"]
    T_Qd, T_Kd, T_Vd, T_Od = DTrk(), DTrk(), DTrk(), DTrk()
    with ExitStack() as mx:
        with ExitStack() as ph:
            wi = SB(g, ph, tag + "_wi", [128, 8, 416], BF16)
            wuq = SB(g, ph, tag + "_wuq", [128, 2, 1568], BF16)
            wk = SB(g, ph, tag + "_wk", [128, NH, 64], BF16)
            wv = SB(g, ph, tag + "_wv", [128, 1024], BF16)
            mnorm = SB(g, ph, tag + "_mnorm", [128, 4], F32)
            mqk = SB(g, ph, tag + "_mqk", [96, 2], F32)
            prot = SB(g, ph, tag + "_prot", [96, 96], BF16)
            ones = SB(g, ph, tag + "_ones", [128, 128], BF16)
            T_w = Trk()
            K.dma("sp", wi[:].rearrange("p k n -> p (k n)"), d["mwin_bf"], reads=[g.T_wbf], writes=[T_w])
            K.op("pool", lambda e: e.memset(wuq[:, :, 1536:1568], 0.0), writes=[T_w])
            K.dma("sp", wuq[:, :, 0:1536], d["mwuq_bf"].rearrange("p (k n) -> p k n", k=2), reads=[g.T_wbf], writes=[T_w])
            K.dma("sp", wk[:].rearrange("p h n -> p (h n)"), d["mwk_bf"], reads=[g.T_wbf], writes=[T_w])
            K.dma("sp", wv[:], d["mwv_bf"], reads=[g.T_wbf], writes=[T_w])
            K.dma("sp", mnorm[:], d["mnorm"], writes=[T_w])
            K.dma("sp", mqk[:], d["mqk"], writes=[T_w])
            K.dma("sp", prot[:], d["prot"], writes=[T_w])
            K.dma("sp", ones[:], d["ones128"], writes=[T_w])
            T = 512
            xt = [SB(g, ph, tag + "_xt%d" % i, [128, 4, 1024], F32) for i in range(2)]
            T_xt = trks(2)
            nb = NormBufs(g, ph, tag + "_nb", 4)
            hT = SB(g, ph, tag + "_hT", [128, 8, T], BF16)
            T_hT = Trk()
            rc = [SB(g, ph, tag + "_rc%d" % i, [96, T], F32) for i in range(2)]
            rs_ = [SB(g, ph, tag + "_rs%d" % i, [96, T], F32) for i in range(2)]
            T_rope = trks(2)
            sqa = [SB(g, ph, tag + "_sqa%d" % i, [128, T], BF16) for i in range(3)]
            T_sqa = trks(3)
            rq = SB(g, ph, tag + "_rq", [128, T], F32)
            rkv = SB(g, ph, tag + "_rkv", [128, T], F32)
            T_rq, T_rkv = Trk(), Trk()
            aqn = SB(g, ph, tag + "_aqn", [128, 2, T], BF16)
            ckvn = SB(g, ph, tag + "_ckvn", [128, T], BF16)
            T_aqn, T_ckvn = Trk(), Trk()
            NKH = 4
            khs = [SB(g, ph, tag + "_kh%d" % i, [96, T], F32) for i in range(NKH)]
            T_khs = trks(NKH)
            kcnt = 0
            kpe32 = SB(g, ph, tag + "_kpe32", [32, T], F32)
            T_kpe = Trk()
            esel = SB(g, ph, tag + "_esel", [32, 96], F32)
            K.dma("sp", esel[:], d["esel"], writes=[T_w])
            NB = 4
            sqh = [SB(g, ph, tag + "_sqh%d" % i, [96, T], BF16) for i in range(NB)]
            qg = [SB(g, ph, tag + "_qg%d" % i, [96, T], BF16) for i in range(NB)]
            rst = [SB(g, ph, tag + "_rst%d" % i, [96, T], F32) for i in range(NB)]
            t1 = [SB(g, ph, tag + "_t1%d" % i, [96, T], F32) for i in range(NB)]
            t2 = [SB(g, ph, tag + "_t2%d" % i, [96, T], F32) for i in range(NB)]
            qf = [SB(g, ph, tag + "_qf%d" % i, [96, T], BF16) for i in range(NB)]
            T_sqh, T_qg, T_rst, T_t1, T_t2, T_qf = trks(NB), trks(NB), trks(NB), trks(NB), trks(NB), trks(NB)
            vt = [SB(g, ph, tag + "_vt%d" % i, [128, NH, 64], BF16) for i in range(2)]
            T_vt = trks(2)
            bc = 0
            vc = 0
            deferred = []
            tiles = [(0, ti * 512, 512, ti) for ti in range(L // 512)] + [(1, 0, LCX, 8)]
            for (is_ctx, t0, Tw, ti) in tiles:
                nsub = Tw // 128
                xb = ti % 2
                the_src, T_the = (csrc, T_csrc) if is_ctx else (src, T_src)
                A, B = mod_scalars(g, 1, is_ctx, 0)
                K.dma("sp", xt[xb][:, 0:nsub, :], the_src[t0:t0 + Tw].rearrange("(s p) d -> p s d", p=128), reads=[T_the],
                      writes=[T_xt[xb]])
                norm_T(g, nb, xt[xb], T_xt[xb], 128, nsub, A, B, lambda kc: hT[:, kc, 0:Tw], T_hT)
                kpos0 = (LX + t0) if is_ctx else t0
                if not is_ctx:
                    rb = ti % 2
                    K.dma("sp", rc[rb][:, 0:Tw], d["ropeC"][:, t0:t0 + Tw], writes=[T_rope[rb]])
                    K.dma("sp", rs_[rb][:, 0:Tw], d["ropeS"][:, t0:t0 + Tw], writes=[T_rope[rb]])
                cols = [(0, 0, 128, 0), (1, 128, 128, 0), (2, 256, 128, 0), (3, 384, 32, 0)]
                for (bank, c0, cw_, prow) in cols:
                    if is_ctx and bank < 2:
                        continue
                    for kc in range(8):
                        K.op("pe", lambda e: e.matmul(g.ps[bank][prow:prow + cw_, 0:Tw], lhsT=wi[:, kc, c0:c0 + cw_], rhs=hT[:, kc, 0:Tw],
                                                      start=(kc == 0), stop=(kc == 7)),
                             reads=[T_w, T_hT], writes=[g.pst[bank]], inc=(kc == 7))
                K.op("dve", lambda e: e.tensor_copy(out=kpe32[:, 0:Tw], in_=g.ps[3][0:32, 0:Tw]), reads=[g.pst[3]], writes=[T_kpe])
                K.op("pe", lambda e: e.matmul(g.ps[3][0:96, 0:Tw], lhsT=esel[:, :], rhs=kpe32[:, 0:Tw], start=True, stop=True),
                     reads=[T_w, T_kpe], writes=[g.pst[3]])
                for i in range(NKH):
                    K.op("dve", lambda e: e.tensor_copy(out=khs[i][64:96, 0:Tw], in_=g.ps[3][64:96, 0:Tw]), reads=[g.pst[3]], writes=[T_khs[i]])
                if getattr(g, "cutpt", 0) == 1:
                    K.barrier()
                    return
                if not is_ctx:
                    for c in range(2):
                        K.op("act", lambda e: e.activation(out=sqa[c][:, 0:Tw], in_=g.ps[c][:, 0:Tw], func=AF.Square),
                             reads=[g.pst[c]], writes=[T_sqa[c]])
                    for c in range(2):
                        K.op("pe", lambda e: e.matmul(g.ps[4][:, 0:Tw], lhsT=ones[:, :], rhs=sqa[c][:, 0:Tw], start=(c == 0), stop=(c == 1)),
                             reads=[T_w, T_sqa[c]], writes=[g.pst[4]], inc=(c == 1))
                    K.op("act", lambda e: e.activation(out=rq[:, 0:Tw], in_=g.ps[4][:, 0:Tw], func=AF.Ln, scale=1.0 / 256, bias=g.epsc[:, 0:1]),
                         reads=[g.pst[4]], writes=[T_rq])
                    K.op("act", lambda e: e.activation(out=rq[:, 0:Tw], in_=rq[:, 0:Tw], func=AF.Exp, scale=-0.5), reads=[T_rq], writes=[T_rq])
                    for c in range(2):
                        K.op("dve", lambda e: e.scalar_tensor_tensor(out=aqn[:, c, 0:Tw], in0=g.ps[c][:, 0:Tw], scalar=mnorm[:, c:c + 1],
                                                                     in1=rq[:, 0:Tw], op0=ALU.mult, op1=ALU.mult),
                             reads=[g.pst[c], T_w, T_rq], writes=[T_aqn])
                K.op("act", lambda e: e.activation(out=sqa[2][:, 0:Tw], in_=g.ps[2][:, 0:Tw], func=AF.Square),
                     reads=[g.pst[2]], writes=[T_sqa[2]])
                K.op("pe", lambda e: e.matmul(g.ps[5][:, 0:Tw], lhsT=ones[:, :], rhs=sqa[2][:, 0:Tw], start=True, stop=True),
                     reads=[T_w, T_sqa[2]], writes=[g.pst[5]])
                K.op("act", lambda e: e.activation(out=rkv[:, 0:Tw], in_=g.ps[5][:, 0:Tw], func=AF.Ln, scale=1.0 / 128, bias=g.epsc[:, 0:1]),
                     reads=[g.pst[5]], writes=[T_rkv])
                K.op("act", lambda e: e.activation(out=rkv[:, 0:Tw], in_=rkv[:, 0:Tw], func=AF.Exp, scale=-0.5), reads=[T_rkv], writes=[T_rkv])
                K.op("dve", lambda e: e.scalar_tensor_tensor(out=ckvn[:, 0:Tw], in0=g.ps[2][:, 0:Tw], scalar=mnorm[:, 2:3],
                                                             in1=rkv[:, 0:Tw], op0=ALU.mult, op1=ALU.mult),
                     reads=[g.pst[2], T_w, T_rkv], writes=[T_ckvn])
                if getattr(g, "cutpt", 0) == 2:
                    K.barrier()
                    return
                for sub in range(nsub):
                    vb_ = vc % 2
                    vc += 1
                    for nh in range(2):
                        K.op("pe", lambda e: e.matmul(g.ps[6 + nh][:, :], lhsT=ckvn[:, sub * 128:(sub + 1) * 128], rhs=wv[:, nh * 512:(nh + 1) * 512],
                                                      start=True, stop=True), reads=[T_ckvn, T_w], writes=[g.pst[6 + nh]])
                        K.op("act" if nh == 0 else "dve",
                             (lambda e: e.activation(out=vt[vb_][:, nh * 8:(nh + 1) * 8, 0:64],
                                                     in_=g.ps[6 + nh][:, :].rearrange("p (h v) -> p h v", v=64), func=AF.Copy))
                             if nh == 0 else
                             (lambda e: e.tensor_copy(out=vt[vb_][:, nh * 8:(nh + 1) * 8, 0:64],
                                                      in_=g.ps[6 + nh][:, :].rearrange("p (h v) -> p h v", v=64))),
                             reads=[g.pst[6 + nh]], writes=[T_vt[vb_]])
                    K.dma("sp", Vd[(kpos0 // 128) + sub], vt[vb_][:].rearrange("p h v -> p (h v)"), reads=[T_vt[vb_]], writes=[T_Vd])
                if getattr(g, "cutpt", 0) == 3:
                    K.barrier()
                    return
                for h in range(NH):
                    for isk in ([1] if is_ctx else [0, 1]):
                        b = bc % NB
                        bc += 1
                        pq = (4, 5, 0, 1)[bc % 4]
                        pr = (6, 7, 2, 3)[bc % 4]
                        if isk == 0:
                            for c in range(2):
                                K.op("pe", lambda e: e.matmul(g.ps[pq][0:128, 0:Tw], lhsT=wuq[:, c, h * 96:h * 96 + 128], rhs=aqn[:, c, 0:Tw],
                                                              start=(c == 0), stop=(c == 1)),
                                     reads=[T_w, T_aqn], writes=[g.pst[pq]], inc=(c == 1))
                            srcv, T_srcv = g.ps[pq][0:96, 0:Tw], g.pst[pq]
                        else:
                            K.op("pe", lambda e: e.matmul(g.ps[pq][0:64, 0:Tw], lhsT=wk[:, h, :], rhs=ckvn[:, 0:Tw], start=True, stop=True),
                                 reads=[T_w, T_ckvn], writes=[g.pst[pq]])
                            kh, T_kh = khs[kcnt % NKH], T_khs[kcnt % NKH]
                            kcnt += 1
                            K.op("act", lambda e: e.activation(out=kh[0:64, 0:Tw], in_=g.ps[pq][0:64, 0:Tw], func=AF.Copy),
                                 reads=[g.pst[pq]], writes=[T_kh])
                            srcv, T_srcv = kh[:, 0:Tw], T_kh
                        K.op("act", lambda e: e.activation(out=sqh[b][:, 0:Tw], in_=srcv, func=AF.Square), reads=[T_srcv], writes=[T_sqh[b]])
                        K.op("pe", lambda e: e.matmul(g.ps[pr][0:96, 0:Tw], lhsT=ones[0:96, 0:96], rhs=sqh[b][:, 0:Tw], start=True, stop=True),
                             reads=[T_w, T_sqh[b]], writes=[g.pst[pr]])

                        def stage_a2(b=b, pr=pr, Tw=Tw):
                            K.op("act", lambda e: e.activation(out=rst[b][:, 0:Tw], in_=g.ps[pr][0:96, 0:Tw], func=AF.Ln, scale=1.0 / 96,
                                                               bias=g.epsc[0:96, 0:1]), reads=[g.pst[pr]], writes=[T_rst[b]])
                            K.op("act", lambda e: e.activation(out=rst[b][:, 0:Tw], in_=rst[b][:, 0:Tw], func=AF.Exp, scale=-0.5),
                                 reads=[T_rst[b]], writes=[T_rst[b]])

                        def stage_b(b=b, pq=pq, h=h, isk=isk, is_ctx=is_ctx, Tw=Tw, t0=t0, kpos0=kpos0, rb=(0 if is_ctx else rb),
                                    srcv=srcv, T_srcv=T_srcv):
                            dst_t, T_dst_t = (qf[b], T_qf[b]) if is_ctx else (qg[b], T_qg[b])
                            K.op("dve", lambda e: e.scalar_tensor_tensor(out=dst_t[:, 0:Tw], in0=srcv, scalar=mqk[:, isk:isk + 1],
                                                                         in1=rst[b][:, 0:Tw], op0=ALU.mult, op1=ALU.mult),
                                 reads=[T_srcv, T_w, T_rst[b], T_sqh[b]], writes=[T_dst_t])
                            if is_ctx:
                                K.dma("sp", Kd[h][:, kpos0:kpos0 + Tw], qf[b][:, 0:Tw], reads=[T_qf[b]], writes=[T_Kd])
                                return
                            K.op("pe", lambda e: e.matmul(g.ps[pq][0:96, 0:Tw], lhsT=prot[:, :], rhs=qg[b][:, 0:Tw], start=True, stop=True),
                                 reads=[T_w, T_qg[b]], writes=[g.pst[pq]])
                            K.op("pool", lambda e: e.tensor_tensor(out=t1[b][:, 0:Tw], in0=qg[b][:, 0:Tw], in1=rc[rb][:, 0:Tw], op=ALU.mult),
                                 reads=[T_qg[b], T_rope[rb]], writes=[T_t1[b]])
                            K.op("dve", lambda e: e.tensor_tensor(out=t2[b][:, 0:Tw], in0=g.ps[pq][0:96, 0:Tw], in1=rs_[rb][:, 0:Tw], op=ALU.mult),
                                 reads=[g.pst[pq], T_rope[rb]], writes=[T_t2[b]])
                            K.op("pool", lambda e: e.tensor_tensor(out=qf[b][:, 0:Tw], in0=t1[b][:, 0:Tw], in1=t2[b][:, 0:Tw], op=ALU.add),
                                 reads=[T_t1[b], T_t2[b]], writes=[T_qf[b]])
                            if isk == 0:
                                K.dma("sp", Qd[h][:, t0:t0 + Tw], qf[b][:, 0:Tw], reads=[T_qf[b]], writes=[T_Qd])
                            else:
                                K.dma("sp", Kd[h][:, kpos0:kpos0 + Tw], qf[b][:, 0:Tw], reads=[T_qf[b]], writes=[T_Kd])

                        deferred.append([stage_a2, stage_b])
                        if len(deferred) > 1:
                            deferred[-2][0]()
                        if len(deferred) > 2:
                            deferred.pop(0)[1]()
                if deferred:
                    deferred[-1][0]()
                while deferred:
                    deferred.pop(0)[1]()
            K.barrier()
        if getattr(g, "mla_stop", 0) == 1:
            return
        with ExitStack() as ph:
            Qh = [SB(g, ph, tag + "_Qh%d" % i, [96, L], BF16) for i in range(2)]
            Kh = [SB(g, ph, tag + "_Kh%d" % i, [96, NKC * 128], BF16) for i in range(2)]
            Vh = [SB(g, ph, tag + "_Vh%d" % i, [128, NKC, 128], BF16) for i in range(2)]
            T_h = trks(2)
            for i in range(2):
                K.op("pool", lambda e: e.memset(Vh[i][:, :, 64:128], 1.0), writes=[T_h[i]])
            NPB = 4
            pb = [SB(g, ph, tag + "_pb%d" % i, [128, 1024], BF16) for i in range(NPB)]
            T_pb = trks(NPB)
            dn = SB(g, ph, tag + "_dn", [65, 512], F32)
            onesf = SB(g, ph, tag + "_onesf", [65, 64], F32)
            T_dn, T_onesf = Trk(), Trk()
            K.op("pool", lambda e: e.memset(onesf[:], 1.0), writes=[T_onesf])
            obuf = [SB(g, ph, tag + "_ob%d" % i, [64, 512], BF16) for i in range(2)]
            T_obuf = trks(2)
            SCALE = 96.0 ** -0.5
            NP = NKC // 2

            def load_head(h):
                hb_ = h % 2
                K.dma("sp", Qh[hb_][:], Qd[h], reads=[T_Qd], writes=[T_h[hb_]])
                K.dma("sp", Kh[hb_][:], Kd[h], reads=[T_Kd], writes=[T_h[hb_]])
                for c0 in range(0, NKC, 17):
                    K.dma("sp", Vh[hb_][:, c0:c0 + 17, 0:64], Vd[c0:c0 + 17, :, h * 64:(h + 1) * 64].rearrange("c p v -> p c v"),
                          reads=[T_Vd], writes=[T_h[hb_]])

            items = [(h, qt, j) for h in range(NH) for qt in range(L // 512) for j in range(NP)]
            spc = [0]

            SPAIR = (0, 1, 3)

            def emit_S(it):
                h, qt, j = it
                hb_ = h % 2
                a = SPAIR[spc[0] % 3]
                spc[0] += 1
                for u in range(2):
                    kc = 2 * j + u
                    K.op("pe", lambda e: e.matmul(g.ps[2 * a + u][:, :], lhsT=Kh[hb_][:, kc * 128:(kc + 1) * 128],
                                                  rhs=Qh[hb_][:, qt * 512:(qt + 1) * 512], start=True, stop=True),
                         reads=[T_h[hb_]], writes=[g.pst[2 * a + u]])
                return a

            pending = []

            def normalise(h, qt, ob_, o):
                K.op("dve", lambda e: e.reciprocal(out=dn[64:65, :], in_=g.ps[ob_][64:65, :]), reads=[g.pst[ob_]], writes=[T_dn])
                K.op("dve", lambda e: e.tensor_copy(out=dn[0:64, :], in_=g.ps[ob_][0:64, :]), reads=[g.pst[ob_]], writes=[T_dn])
                K.op("pe", lambda e: e.matmul(g.ps[ob_][0:64, :], lhsT=onesf[64:65, 0:64], rhs=dn[64:65, :], start=True, stop=True),
                     reads=[T_dn, T_onesf], writes=[g.pst[ob_]])
                K.op("dve", lambda e: e.tensor_tensor(out=obuf[o][:], in0=g.ps[ob_][0:64, :], in1=dn[0:64, :], op=ALU.mult),
                     reads=[g.pst[ob_], T_dn], writes=[T_obuf[o]])
                K.dma("sp", Od[h][:, qt * 512:(qt + 1) * 512], obuf[o][:], reads=[T_obuf[o]], writes=[T_Od])

            load_head(0)
            squeue = [emit_S(items[0])]
            if len(items) > 1:
                squeue.append(emit_S(items[1]))
            pbc = 0
            oc = 0
            for idx, (h, qt, j) in enumerate(items):
                hb_ = h % 2
                if qt == 0 and j == 0 and h + 1 < NH:
                    load_head(h + 1)
                if j == 0:
                    ob_ = 4 + (oc % 2)
                    o = oc % 2
                    oc += 1
                a = squeue.pop(0)
                p = pbc % NPB
                pbc += 1
                K.op("act", lambda e: e.activation(out=pb[p][:, :], in_=g.psd[a][:, :], func=AF.Exp, scale=SCALE),
                     reads=[g.pst[2 * a], g.pst[2 * a + 1]], writes=[T_pb[p]])
                if idx + 2 < len(items):
                    squeue.append(emit_S(items[idx + 2]))
                for u in range(2):
                    kc = 2 * j + u
                    K.op("pe", lambda e: e.matmul(g.ps[ob_][:, :], lhsT=Vh[hb_][:, kc, :], rhs=pb[p][:, u * 512:(u + 1) * 512],
                                                  start=(kc == 0), stop=(kc == NKC - 1)),
                         reads=[T_h[hb_], T_pb[p]], writes=[g.pst[ob_]], inc=True)
                if j == 3 and pending:
                    normalise(*pending.pop(0))
                if j == NP - 1:
                    pending.append((h, qt, ob_, o))
            while pending:
                normalise(*pending.pop(0))
            K.barrier()
        with ExitStack() as ph:
            wo = SB(g, ph, tag + "_wo", [128, NH // 2, 1024], BF16)
            T_wo = Trk()
            wsrc = d["mwo_bf"].rearrange("p (j two n) -> p j two n", two=2, n=1024)
            for two in range(2):
                K.dma("sp", wo[two * 64:(two + 1) * 64, :, :], wsrc[:, :, two, :], reads=[g.T_wbf], writes=[T_wo])
            g1bc, T_g1 = load_gate_bc(g, ph, tag + "_g1", 1, 0, 0)
            ot = [SB(g, ph, tag + "_ot%d" % i, [128, NH // 2, 512], BF16) for i in range(2)]
            T_ot = trks(2)
            xt = [SB(g, ph, tag + "_xo%d" % i, [128, 1024], F32) for i in range(2)]
            T_xt = trks(2)
            ob = [SB(g, ph, tag + "_oo%d" % i, [128, 1024], F32) for i in range(2)]
            T_ob = trks(2)
            mc = 0
            osrc = Od.rearrange("(j two) p t -> two p j t", two=2)
            for ti in range(L // 512):
                tb_ = ti % 2
                for two in range(2):
                    K.dma("sp", ot[tb_][two * 64:(two + 1) * 64, :, :], osrc[two][:, :, ti * 512:(ti + 1) * 512], reads=[T_Od],
                          writes=[T_ot[tb_]])
                for sub in range(4):
                    m = ti * 4 + sub
                    xb = mc % 2
                    mc += 1
                    K.dma("sp", xt[xb][:], src[m * 128:(m + 1) * 128, :], reads=[T_src], writes=[T_xt[xb]])
                    for nh in range(2):
                        pb_ = (2 * m + nh) % 4
                        for j in range(NH // 2):
                            K.op("pe", lambda e: e.matmul(g.ps[pb_][:, :], lhsT=ot[tb_][:, j, sub * 128:(sub + 1) * 128],
                                                          rhs=wo[:, j, nh * 512:(nh + 1) * 512], start=(j == 0), stop=(j == NH // 2 - 1)),
                                 reads=[T_ot[tb_], T_wo], writes=[g.pst[pb_]], inc=(j == NH // 2 - 1))
                        K.op("dve", lambda e: e.tensor_tensor(out=ob[xb][:, nh * 512:(nh + 1) * 512], in0=g.ps[pb_][:, :],
                                                              in1=g1bc[:, nh * 512:(nh + 1) * 512], op=ALU.mult),
                             reads=[g.pst[pb_], T_g1], writes=[T_ob[xb]])
                    K.op("pool", lambda e: e.tensor_tensor(out=ob[xb][:], in0=ob[xb][:], in1=xt[xb][:], op=ALU.add),
                         reads=[T_ob[xb], T_xt[xb]], writes=[T_ob[xb]])
                    K.dma("sp", dst[m * 128:(m + 1) * 128, :], ob[xb][:], reads=[T_ob[xb]], writes=[T_dst])
            K.barrier()


def declare_inputs(g, nc, shapes):
    g.d = {}
    for name, (shape, dt) in shapes.items():
        g.d[name] = nc.dram_tensor(name, list(shape), dt, kind="ExternalInput").ap()


def in_shapes():
    return {
        "x": ((LX, D), F32), "ctx": ((LCX, D), F32), "cc": ((128, 8, 2), F32),
        "w_mod": ((2, 1024, 6144), F32), "b_mod": ((2, 1, 6144), F32),
        "normT": ((128, 2, 2, 8), F32), "ident": ((128, 128), F32),
        "wup": ((2, 44, 128, 1024), F32), "wdown": ((2, 22, 128, 1024), F32),
        "fconv": ((128, 2, 22, 4), F32),
        "hydelta": ((1, 512), F32), "alt": ((128, 1), BF16), "cs128": ((128, 256), BF16),
        "hy_w1": ((33, 64), F32), "hy_w2": ((64, 64), F32), "hy_w3": ((64, 64), F32), "hy_w4": ((64, 1024), F32),
        "hyp": ((64, 4), F32), "hyconv": ((128, 12, 4), F32), "hybias": ((128, 4), F32),
        "fhwin": ((16, 128, 1024), F32), "fhwout": ((8, 128, 1024), F32),
        "mwin": ((128, 8 * 416), F32), "mwuq": ((128, 2 * 1536), F32), "mwk": ((128, 1024), F32), "mwv": ((128, 1024), F32),
        "mwo": ((64, 16 * 1024), F32), "mnorm": ((128, 4), F32), "mqk": ((96, 2), F32),
        "esel": ((32, 96), F32), "prot": ((96, 96), BF16), "ones128": ((128, 128), BF16), "ropeC": ((96, LX), F32), "ropeS": ((96, LX), F32),
        **{k + "_%d" % L: v for L in (LX, LCX) for k, v in {
            "hyz": ((33, L), F32), "hynegt": ((128, L // 128), F32), "altrow": ((1, L), BF16),
            "CT": ((L, L), BF16), "ST": ((L, L), BF16), "C4": ((L, L // 2), BF16), "S4n": ((L, L // 2), BF16)}.items()},
    }


def build(mode="full"):
    nc = bass.Bass("TRN2", target_bir_lowering=False)
    g = G()
    g.nc = nc
    declare_inputs(g, nc, in_shapes())
    d = g.d
    d["out"] = nc.dram_tensor("out", [LX, D], F32, kind="ExternalOutput").ap()
    d["modscr"] = nc.dram_tensor("modscr", [2, 2, 6144], F32).ap()
    d["wup_bf"] = nc.dram_tensor("wup_bf", [2, 44, 128, 1024], BF16).ap()
    d["wdown_bf"] = nc.dram_tensor("wdown_bf", [2, 22, 128, 1024], BF16).ap()
    d["xa"] = nc.dram_tensor("xa", [LX, D], F32).ap()
    d["ca"] = nc.dram_tensor("ca", [LCX, D], F32).ap()
    d["cb"] = nc.dram_tensor("cb", [LCX, D], F32).ap()
    d["fhwin_bf"] = nc.dram_tensor("fhwin_bf", [16, 128, 1024], BF16).ap()
    d["fhwout_bf"] = nc.dram_tensor("fhwout_bf", [8, 128, 1024], BF16).ap()
    for L in (LX, LCX):
        sfx = "_%d" % L
        kw = {"kind": "ExternalOutput"} if mode.endswith("dbg") else {}
        d["Kre" + sfx] = nc.dram_tensor("Kre" + sfx, [4, 128, L + 1], F32, **kw).ap()
        d["Ksn" + sfx] = nc.dram_tensor("Ksn" + sfx, [4, 128, L + 1], F32, **kw).ap()
        d["ABd" + sfx] = nc.dram_tensor("ABd" + sfx, [L, 4, 256], BF16, **kw).ap()
        d["X0d" + sfx] = nc.dram_tensor("X0d" + sfx, [4, 128, L], BF16, **kw).ap()
        d["Zd" + sfx] = nc.dram_tensor("Zd" + sfx, [4, 128, L], BF16, **kw).ap()
        d["YTd" + sfx] = nc.dram_tensor("YTd" + sfx, [2, L // 128, 128, 512], BF16, **kw).ap()
        d["YTn" + sfx] = nc.dram_tensor("YTn" + sfx, [1, 512], BF16, **kw).ap()
    d["Qd"] = nc.dram_tensor("Qd", [16, 96, LX], BF16).ap()
    d["Kd"] = nc.dram_tensor("Kd", [16, 96, LX + LCX], BF16).ap()
    d["Vd"] = nc.dram_tensor("Vd", [(LX + LCX) // 128, 128, 16 * 64], BF16).ap()
    d["Od"] = nc.dram_tensor("Od", [16, 64, LX], BF16).ap()
    d["xb"] = nc.dram_tensor("xb", [LX, D], F32).ap()
    d["xc"] = nc.dram_tensor("xc", [LX, D], F32).ap()
    g.castlist = []
    for nm, shp in (("mwin", [128, 8 * 416]), ("mwuq", [128, 2 * 1536]), ("mwk", [128, 1024]), ("mwv", [128, 1024]), ("mwo", [64, 16 * 1024])):
        d[nm + "_bf"] = nc.dram_tensor(nm + "_bf", shp, BF16).ap()
        g.castlist.append((d[nm + "_bf"], d[nm]))
    g.castlist.append((d["fhwin_bf"][0:8], d["fhwin"][0:8]))
    g.castlist.append((d["fhwin_bf"][8:16], d["fhwin"][8:16]))
    g.castlist.append((d["fhwout_bf"], d["fhwout"]))
    for i in range(2):
        for j in range(0, 44, 11):
            g.castlist.append((d["wup_bf"][i, j:j + 11], d["wup"][i, j:j + 11]))
        for j in range(0, 22, 11):
            g.castlist.append((d["wdown_bf"][i, j:j + 11], d["wdown"][i, j:j + 11]))
    with ExitStack() as es:
        g.es = es
        g.K = KB(nc, es)
        g.psd = [es.enter_context(nc.psum_tensor("psd%d" % i, [128, 1024], F32)) for i in range(4)]
        g.ps = [g.psd[i // 2][:, (i % 2) * 512:(i % 2 + 1) * 512] for i in range(8)]
        g.pst = trks(8)
        phase0(g)
        if mode != "full":
            for _ in phase0_mods(g):
                pass
            g.K.barrier()
        T_x, T_ctx, T_out, T_ca = Trk(), Trk(), DTrk(), DTrk()
        T_xa, T_xb, T_xc, T_cb = DTrk(), DTrk(), DTrk(), DTrk()
        if mode == "ffn_ctx":
            ffn_phase(g, 0, 1, LCX, d["ctx"], T_ctx, d["out"], T_out, "fc")
        elif mode == "mix_ctx":
            for _ in mixer0(g, 1, LCX, d["ctx"], T_ctx, d["out"], T_out, "mc"):
                pass
        elif mode.startswith("mla"):
            g.mla_stop = int(mode[3:4]) if len(mode) > 3 else 0
            g.cutpt = int(mode[5:]) if len(mode) > 5 else 0
            try:
                mla(g, d["x"], T_x, d["ctx"], T_ctx, d["out"], T_out)
            except StopBuild:
                pass
        elif mode == "full":
            mxg = mixer0(g, 0, LX, d["x"], T_x, d["xa"], T_xa, "mx")
            next(mxg)
            p0 = phase0_mods(g)
            p0_done = False
            for _ in mxg:
                if not p0_done:
                    try:
                        next(p0)
                    except StopIteration:
                        p0_done = True
            assert p0_done
            for _ in mixer0(g, 1, LCX, d["ctx"], T_ctx, d["ca"], T_ca, "mc"):
                pass
            ffn_phase(g, 0, 0, LX, d["xa"], T_xa, d["xb"], T_xb, "f0x")
            ffn_phase(g, 0, 1, LCX, d["ca"], T_ca, d["cb"], T_cb, "f0c")
            mla(g, d["xb"], T_xb, d["cb"], T_cb, d["xc"], T_xc)
            ffn_phase(g, 1, 0, LX, d["xc"], T_xc, d["out"], T_out, "f1x")
        elif mode == "mix_x_dbg":
            for _ in mixer0(g, 0, LX, d["x"], T_x, d["out"], T_out, "mx"):
                pass
        elif mode == "mix_x":
            for _ in mixer0(g, 0, LX, d["x"], T_x, d["out"], T_out, "mx"):
                pass
        elif mode == "ffn_x":
            ffn_phase(g, 1, 0, LX, d["x"], T_x, d["out"], T_out, "fx")
        g.K.barrier()
        g.K.final_wait("sp")
    return nc


def prep_common(inp):
    d = {}
    f = lambda a: np.ascontiguousarray(np.asarray(a, dtype=np.float32))
    d["w_mod"] = f(inp["w_mod"])
    d["b_mod"] = f(inp["b_mod"]).reshape(2, 1, 6144)
    n1 = f(inp["norm1"]).reshape(2, 8, 128)
    n2 = f(inp["norm2"]).reshape(2, 8, 128)
    d["normT"] = np.ascontiguousarray(np.stack([n1, n2], axis=1).transpose(3, 0, 1, 2))
    d["ident"] = np.eye(128, dtype=np.float32)
    wu = f(inp["ffn_w_up"]).reshape(2, 8, 128, 44, 128).transpose(0, 3, 2, 1, 4)
    order = [j for c in range(22) for j in (c, 22 + c)]
    d["wup"] = np.ascontiguousarray(wu[:, order]).reshape(2, 44, 128, 1024)
    d["wdown"] = f(inp["ffn_w_down"]).reshape(2, 22, 128, 1024)
    cw = f(inp["ffn_conv_w"]).reshape(2, 3, 22, 128)
    cb = f(inp["ffn_conv_b"]).reshape(2, 1, 22, 128)
    d["fconv"] = np.ascontiguousarray(np.concatenate([cw, cb], axis=1).transpose(3, 0, 2, 1))
    wi = f(inp["fh_w_in"][0]).reshape(8, 128, 16, 128).transpose(2, 1, 0, 3)
    d["fhwin"] = np.ascontiguousarray(wi).reshape(16, 128, 1024)
    d["fhwout"] = f(inp["fh_w_out"][0]).reshape(8, 128, 1024)
    hw = f(inp["hy_conv_w"][0]).reshape(3, 12, 128)
    hb = f(inp["hy_conv_b"][0]).reshape(1, 12, 128)
    d["hyconv"] = np.ascontiguousarray(np.concatenate([hw, hb], axis=0).transpose(2, 1, 0))
    d["hy_w1"] = f(inp["hy_filt_w1"][0]); d["hy_w2"] = f(inp["hy_filt_w2"][0])
    d["hy_w3"] = f(inp["hy_filt_w3"][0]); d["hy_w4"] = f(inp["hy_filt_w4"][0])
    d["hyp"] = np.ascontiguousarray(np.stack([f(inp["hy_freq"][0]), f(inp["hy_filt_b1"][0]), f(inp["hy_filt_b2"][0]),
                                              f(inp["hy_filt_b3"][0])], axis=1))
    d["hybias"] = np.ascontiguousarray(f(inp["hy_bias"][0]).reshape(4, 128).T)
    d["mwin"] = np.ascontiguousarray(f(inp["mla_w_in"][0]).reshape(8, 128, 416).transpose(1, 0, 2)).reshape(128, 8 * 416)
    d["mwuq"] = np.ascontiguousarray(f(inp["mla_w_uq"][0]).reshape(2, 128, 1536).transpose(1, 0, 2)).reshape(128, 2 * 1536)
    wkv = f(inp["mla_w_ukv"][0]).reshape(128, 16, 128)
    d["mwk"] = np.ascontiguousarray(wkv[:, :, :64]).reshape(128, 1024)
    d["mwv"] = np.ascontiguousarray(wkv[:, :, 64:]).reshape(128, 1024)
    d["mwo"] = np.ascontiguousarray(f(inp["mla_w_o"][0]).reshape(16, 64, 1024).transpose(1, 0, 2)).reshape(64, 16 * 1024)
    qa = f(inp["mla_q_a_norm"][0]).reshape(2, 128)
    d["mnorm"] = np.ascontiguousarray(np.stack([qa[0], qa[1], f(inp["mla_kv_a_norm"][0]), np.zeros(128, np.float32)], axis=1))
    d["mqk"] = np.ascontiguousarray(np.stack([f(inp["mla_q_norm"][0]), f(inp["mla_k_norm"][0])], axis=1))
    d.update(const_tables())
    return d


_TABLES = {}


def const_tables():
    if _TABLES:
        return _TABLES
    bf = ml_dtypes.bfloat16
    t = {}
    import math
    min_decay = math.log(1e-2) / 0.3
    max_decay = math.log(1e-2) / 1.5
    deltas = np.linspace(min_decay, max_decay, 512, dtype=np.float32)
    t["hydelta"] = np.abs(deltas).reshape(1, 512).astype(np.float32)
    t["alt"] = ((-1.0) ** np.arange(128)).reshape(128, 1).astype(bf)
    dk = (np.arange(128)[:, None] * np.arange(128)[None, :]) % 128
    ang = 2 * np.pi * dk / 128.0
    t["cs128"] = np.concatenate([np.cos(ang), np.sin(ang)], axis=1).astype(bf)
    prot = np.zeros((96, 96), np.float32)
    for base in (64, 80):
        for i in range(8):
            prot[base + 8 + i, base + i] = -1.0
            prot[base + i, base + 8 + i] = 1.0
    t["prot"] = prot.astype(bf)
    esel = np.zeros((32, 96), np.float32)
    esel[np.arange(32), 64 + np.arange(32)] = 1.0
    t["esel"] = esel
    t["ones128"] = np.ones((128, 128), np.float32).astype(bf)
    posi = np.arange(LX)
    rows = (posi // 64).astype(np.float32)
    colsv = (posi % 64).astype(np.float32)
    inv_freq = (np.float32(10000.0) ** (-np.arange(8, dtype=np.float32) / np.float32(8))).astype(np.float32)
    ang_r = rows[:, None] * inv_freq[None, :]
    ang_c = colsv[:, None] * inv_freq[None, :]
    rc = np.ones((96, LX), np.float32)
    rs = np.zeros((96, LX), np.float32)
    rc[64:72] = np.cos(ang_r).T; rc[72:80] = np.cos(ang_r).T; rc[80:88] = np.cos(ang_c).T; rc[88:96] = np.cos(ang_c).T
    rs[64:72] = np.sin(ang_r).T; rs[72:80] = np.sin(ang_r).T; rs[80:88] = np.sin(ang_c).T; rs[88:96] = np.sin(ang_c).T
    t["ropeC"] = rc
    t["ropeS"] = rs
    for L in (LX, LCX):
        sfx = "_%d" % L
        pos = np.arange(L, dtype=np.float32)
        tt = pos / max(L - 1, 1)
        bands = np.linspace(1e-4, 15, 16, dtype=np.float32)
        a = (np.float32(2.0 * math.pi / L) * pos[:, None] * bands[None, :]).astype(np.float32)
        z = np.concatenate([tt[:, None], np.cos(a), -np.sin(a)], axis=-1).astype(np.float32)
        t["hyz" + sfx] = np.ascontiguousarray(z.T)
        t["hynegt" + sfx] = np.ascontiguousarray((-tt).reshape(L // 128, 128).T).astype(np.float32)
        t["altrow" + sfx] = ((-1.0) ** np.arange(L)).reshape(1, L).astype(bf)
        idx = np.arange(L, dtype=np.int64)
        m8 = (idx[:, None] * idx[None, :]) % (2 * L)
        a8 = (2 * np.pi / (2 * L)) * m8
        t["CT" + sfx] = np.cos(a8).astype(bf)
        t["ST" + sfx] = np.sin(a8).astype(bf)
        m4 = (idx[:, None] * idx[None, :L // 2]) % L
        a4 = (2 * np.pi / L) * m4
        t["C4" + sfx] = np.cos(a4).astype(bf)
        t["S4n" + sfx] = (-np.sin(a4)).astype(bf)
    _TABLES.update(t)
    return _TABLES


def prep_core(inp, b):
    d = {}
    d["x"] = np.ascontiguousarray(np.asarray(inp["x"][b], dtype=np.float32))
    d["ctx"] = np.ascontiguousarray(np.asarray(inp["ctx"][b], dtype=np.float32))
    c = np.asarray(inp["c"][b], dtype=np.float32).reshape(8, 128)
    cx = np.asarray(inp["c_ctx"], dtype=np.float32).reshape(8, 128)
    d["cc"] = np.ascontiguousarray(np.stack([c, cx], axis=-1).transpose(1, 0, 2))
    return d


def kernel(**inputs):
    common = prep_common(inputs)
    nc = build("full")
    in_maps = []
    for b in range(NCORES):
        m = dict(common)
        m.update(prep_core(inputs, b))
        in_maps.append(m)
    res = run_bass_kernel_spmd(nc, in_maps, core_ids=list(range(NCORES)))
    return np.stack([np.asarray(r["out"]) for r in res.results], axis=0).astype(np.float32)
```
